# Optimizing a Trainium2 kernel written in Bass

```python
import jax, jax.numpy as jnp
from jax import lax
import numpy as np

D_MODEL = 1024
BATCH = 32
SEQ = 256
DEPTH = 1
DEC_BATCH = 8
DEC_SEQ = 4096
PAST_LEN = 512

GRID_W = 64
MIX_WIDTH = 2 * D_MODEL
SSD_WIDTH = D_MODEL
SSD_HEAD_DIM = 64
SSD_HEADS = SSD_WIDTH // SSD_HEAD_DIM
N_GROUPS = 2
D_STATE = 128
SSD_CONV = 3
CHUNK = 128
CONV_CH = SSD_WIDTH + 2 * N_GROUPS * D_STATE
CF_WIDTH = MIX_WIDTH - SSD_WIDTH
CF_KERNEL = 31
IN_COLS = SSD_WIDTH + CONV_CH + 2 * SSD_HEADS + 2 * CF_WIDTH
D_FF = 2816
FFN_CONV = 3
EPS = 1e-6

kernel_name = "hybrid_ssd_conformer_diffusion_step"


def rms_norm(x, g):
    xf = x.astype(jnp.float32)
    y = xf * lax.rsqrt(jnp.mean(xf * xf, axis=-1, keepdims=True) + EPS)
    return (y * g.astype(jnp.float32)).astype(x.dtype)


def layer_norm(x, g, b):
    xf = x.astype(jnp.float32)
    mu = jnp.mean(xf, axis=-1, keepdims=True)
    var = jnp.mean(jnp.square(xf - mu), axis=-1, keepdims=True)
    y = (xf - mu) * lax.rsqrt(var + EPS)
    return (y * g.astype(jnp.float32) + b.astype(jnp.float32)).astype(x.dtype)


def dwconv1d(x, w, bias):
    k = w.shape[0]
    out = lax.conv_general_dilated(
        x, w.astype(x.dtype)[:, None, :], window_strides=(1,),
        padding=[((k - 1) // 2, k // 2)],
        dimension_numbers=("NWC", "WIO", "NWC"),
        feature_group_count=x.shape[-1])
    return out + bias.astype(x.dtype)


def dwconv2d(x, w, bias, rows):
    bsz, seqlen, ch = x.shape
    xi = x.reshape(bsz, rows, GRID_W, ch)
    out = lax.conv_general_dilated(
        xi, w.astype(x.dtype)[:, :, None, :], window_strides=(1, 1), padding="SAME",
        dimension_numbers=("NHWC", "HWIO", "NHWC"),
        feature_group_count=ch)
    return out.reshape(bsz, seqlen, ch) + bias.astype(x.dtype)


def ssd_chunked(x, dt, a, b_mat, c_mat, init_state):
    bsz, seqlen, h, p = x.shape
    g, n = b_mat.shape[2], b_mat.shape[3]
    r = h // g
    nc = seqlen // CHUNK
    xc = x.reshape(bsz, nc, CHUNK, g, r, p)
    dtc = dt.reshape(bsz, nc, CHUNK, g, r)
    bc = b_mat.reshape(bsz, nc, CHUNK, g, n)
    cc = c_mat.reshape(bsz, nc, CHUNK, g, n)
    a_cum = jnp.cumsum(dtc * a.reshape(g, r), axis=2)
    xdt = xc * dtc[..., None]
    idx = jnp.arange(CHUNK)
    lower = (idx[:, None] >= idx[None, :])[:, :, None, None]
    seg = a_cum[:, :, :, None] - a_cum[:, :, None, :]
    decay = jnp.exp(jnp.where(lower, seg, -jnp.inf))
    cb = jnp.einsum("bcign,bcjgn->bcijg", cc, bc)
    y_diag = jnp.einsum("bcijg,bcijgr,bcjgrp->bcigrp", cb, decay, xdt)
    decay_to_end = jnp.exp(a_cum[:, :, -1:] - a_cum)
    chunk_states = jnp.einsum("bcjgn,bcjgr,bcjgrp->bcgrpn", bc, decay_to_end, xdt)
    chunk_decay = jnp.exp(a_cum[:, :, -1])

    def step(state, inp):
        cs, cd = inp
        return state * cd[..., None, None] + cs, state

    s0 = init_state.astype(jnp.float32).reshape(bsz, g, r, p, n)
    final, prev = lax.scan(step, s0, (jnp.moveaxis(chunk_states.astype(jnp.float32), 1, 0),
                                      jnp.moveaxis(chunk_decay, 1, 0)))
    prev = jnp.moveaxis(prev, 0, 1)
    y_off = jnp.einsum("bcign,bcgrpn,bcigr->bcigrp", cc, prev, jnp.exp(a_cum))
    y = (y_diag + y_off).reshape(bsz, seqlen, h, p).astype(x.dtype)
    return y, final.reshape(bsz, h, p, n)


def trunk_layer(x, cond, prm, init_f, init_b, rows):
    bsz, seqlen, _ = x.shape
    mod = jax.nn.silu(cond) @ prm["w_ada"] + prm["b_ada"]
    sh1, sc1, g1, sh2, sc2, g2 = [m[:, None, :] for m in jnp.split(mod, 6, axis=-1)]

    h = rms_norm(x, prm["norm_mix_pre"]) * (1 + sc1) + sh1
    proj = h @ prm["w_in"]
    z, xbc, dt_raw, cf_in = jnp.split(
        proj, [SSD_WIDTH, SSD_WIDTH + CONV_CH, SSD_WIDTH + CONV_CH + 2 * SSD_HEADS], axis=-1)

    xbc = jax.nn.silu(dwconv1d(xbc, prm["w_ssd_conv"], prm["b_ssd_conv"]))
    xs, b_mat, c_mat = jnp.split(xbc, [SSD_WIDTH, SSD_WIDTH + N_GROUPS * D_STATE], axis=-1)
    xs = xs.reshape(bsz, seqlen, SSD_HEADS, SSD_HEAD_DIM)
    b_mat = b_mat.reshape(bsz, seqlen, N_GROUPS, D_STATE)
    c_mat = c_mat.reshape(bsz, seqlen, N_GROUPS, D_STATE)
    dt_f, dt_b = jnp.split(dt_raw.astype(jnp.float32), 2, axis=-1)
    dt_f = jax.nn.softplus(dt_f + prm["dt_bias_fwd"].astype(jnp.float32))
    dt_b = jax.nn.softplus(dt_b + prm["dt_bias_bwd"].astype(jnp.float32))
    a_f = -jnp.exp(prm["a_log_fwd"].astype(jnp.float32))
    a_b = -jnp.exp(prm["a_log_bwd"].astype(jnp.float32))
    y_f, s_f = ssd_chunked(xs, dt_f, a_f, b_mat, c_mat, init_f)
    y_b, s_b = ssd_chunked(jnp.flip(xs, 1), jnp.flip(dt_b, 1), a_b,
                           jnp.flip(b_mat, 1), jnp.flip(c_mat, 1), init_b)
    y = y_f + jnp.flip(y_b, 1) + xs * prm["d_skip"][:, None]
    y = y.reshape(bsz, seqlen, SSD_WIDTH)
    y_ssd = rms_norm(y * jax.nn.silu(z), prm["ssd_norm"])

    cf_a, cf_g = jnp.split(cf_in, 2, axis=-1)
    u = cf_a * jax.nn.sigmoid(cf_g)
    u = dwconv1d(u, prm["w_cf_conv"], prm["b_cf_conv"])
    u = jax.nn.silu(layer_norm(u, prm["cf_ln_g"], prm["cf_ln_b"]))

    mix = jnp.concatenate([y_ssd, u], axis=-1) @ prm["w_out"]
    x = x + g1 * rms_norm(mix, prm["norm_mix_post"])

    h = rms_norm(x, prm["norm_ffn_pre"]) * (1 + sc2) + sh2
    up = h @ prm["w_ffn_up"]
    if rows is None:
        up = dwconv1d(up, prm["w_ffn_conv"][1], prm["b_ffn_conv"])
    else:
        up = dwconv2d(up, prm["w_ffn_conv"], prm["b_ffn_conv"], rows)
    f_gate, f_val = jnp.split(up, 2, axis=-1)
    f = (jax.nn.silu(f_gate) * f_val) @ prm["w_ffn_down"]
    x = x + g2 * rms_norm(f, prm["norm_ffn_post"])
    return x, s_f, s_b


def setup_inputs(seed: int = 0) -> dict:
    key = jax.random.key(seed)
    ks = jax.random.split(key, 32)
    f32 = jnp.float32

    def nrm(k, shape, scale):
        return jax.random.normal(k, shape, f32) * scale

    def gain(k, shape):
        return 1.0 + 0.05 * jax.random.normal(k, shape, f32)

    dt0 = jnp.exp(jax.random.uniform(ks[10], (DEPTH, SSD_HEADS), f32, np.log(1e-3), np.log(1e-1)))
    dt1 = jnp.exp(jax.random.uniform(ks[11], (DEPTH, SSD_HEADS), f32, np.log(1e-3), np.log(1e-1)))
    state_shape = (DEC_BATCH, DEPTH, SSD_HEADS, SSD_HEAD_DIM, D_STATE)
    return {
        "x_prompt": nrm(ks[0], (BATCH, SEQ, D_MODEL), 1.0),
        "x_sample": nrm(ks[1], (DEC_BATCH, DEC_SEQ, D_MODEL), 1.0),
        "state_ssd_fwd": nrm(ks[2], state_shape, 0.5),
        "state_ssd_bwd": nrm(ks[3], state_shape, 0.5),
        "c": nrm(ks[4], (DEC_BATCH, D_MODEL), 1.0),
        "c_ctx": nrm(ks[5], (D_MODEL,), 1.0),
        "w_ada": nrm(ks[6], (DEPTH, D_MODEL, 6 * D_MODEL), D_MODEL ** -0.5),
        "b_ada": nrm(ks[7], (DEPTH, 6 * D_MODEL), 0.02),
        "norm_mix_pre": gain(ks[8], (DEPTH, D_MODEL)),
        "norm_mix_post": gain(ks[9], (DEPTH, D_MODEL)),
        "w_in": nrm(ks[12], (DEPTH, D_MODEL, IN_COLS), D_MODEL ** -0.5),
        "w_ssd_conv": nrm(ks[13], (DEPTH, SSD_CONV, CONV_CH), SSD_CONV ** -0.5),
        "b_ssd_conv": nrm(ks[14], (DEPTH, CONV_CH), 0.02),
        "a_log_fwd": jnp.log(jax.random.uniform(ks[15], (DEPTH, SSD_HEADS), f32, 1.0, 16.0)),
        "a_log_bwd": jnp.log(jax.random.uniform(ks[16], (DEPTH, SSD_HEADS), f32, 1.0, 16.0)),
        "dt_bias_fwd": dt0 + jnp.log(-jnp.expm1(-dt0)),
        "dt_bias_bwd": dt1 + jnp.log(-jnp.expm1(-dt1)),
        "d_skip": gain(ks[17], (DEPTH, SSD_HEADS)),
        "ssd_norm": gain(ks[18], (DEPTH, SSD_WIDTH)),
        "w_cf_conv": nrm(ks[19], (DEPTH, CF_KERNEL, CF_WIDTH), CF_KERNEL ** -0.5),
        "b_cf_conv": nrm(ks[20], (DEPTH, CF_WIDTH), 0.02),
        "cf_ln_g": gain(ks[21], (DEPTH, CF_WIDTH)),
        "cf_ln_b": nrm(ks[22], (DEPTH, CF_WIDTH), 0.02),
        "w_out": nrm(ks[23], (DEPTH, MIX_WIDTH, D_MODEL), MIX_WIDTH ** -0.5),
        "norm_ffn_pre": gain(ks[24], (DEPTH, D_MODEL)),
        "norm_ffn_post": gain(ks[25], (DEPTH, D_MODEL)),
        "w_ffn_up": nrm(ks[26], (DEPTH, D_MODEL, 2 * D_FF), D_MODEL ** -0.5),
        "w_ffn_conv": nrm(ks[27], (DEPTH, FFN_CONV, FFN_CONV, 2 * D_FF), 1.0 / FFN_CONV),
        "b_ffn_conv": nrm(ks[28], (DEPTH, 2 * D_FF), 0.02),
        "w_ffn_down": nrm(ks[29], (DEPTH, D_FF, D_MODEL), D_FF ** -0.5),
    }


def reference(x_prompt, x_sample, state_ssd_fwd, state_ssd_bwd, c, c_ctx,
              w_ada, b_ada, norm_mix_pre, norm_mix_post, w_in, w_ssd_conv, b_ssd_conv,
              a_log_fwd, a_log_bwd, dt_bias_fwd, dt_bias_bwd, d_skip, ssd_norm,
              w_cf_conv, b_cf_conv, cf_ln_g, cf_ln_b, w_out, norm_ffn_pre, norm_ffn_post,
              w_ffn_up, w_ffn_conv, b_ffn_conv, w_ffn_down):
    n_ctx_batch = x_prompt.shape[0]
    rows = x_sample.shape[1] // GRID_W
    cond_ctx = jnp.broadcast_to(c_ctx[None, :], (n_ctx_batch, c_ctx.shape[0]))
    zero_state = jnp.zeros((n_ctx_batch, SSD_HEADS, SSD_HEAD_DIM, D_STATE), jnp.float32)

    xp = x_prompt
    xl = x_sample
    new_f, new_b = [], []
    for l in range(DEPTH):
        prm = {
            "w_ada": w_ada[l], "b_ada": b_ada[l],
            "norm_mix_pre": norm_mix_pre[l], "norm_mix_post": norm_mix_post[l],
            "w_in": w_in[l], "w_ssd_conv": w_ssd_conv[l], "b_ssd_conv": b_ssd_conv[l],
            "a_log_fwd": a_log_fwd[l], "a_log_bwd": a_log_bwd[l],
            "dt_bias_fwd": dt_bias_fwd[l], "dt_bias_bwd": dt_bias_bwd[l],
            "d_skip": d_skip[l], "ssd_norm": ssd_norm[l],
            "w_cf_conv": w_cf_conv[l], "b_cf_conv": b_cf_conv[l],
            "cf_ln_g": cf_ln_g[l], "cf_ln_b": cf_ln_b[l], "w_out": w_out[l],
            "norm_ffn_pre": norm_ffn_pre[l], "norm_ffn_post": norm_ffn_post[l],
            "w_ffn_up": w_ffn_up[l], "w_ffn_conv": w_ffn_conv[l],
            "b_ffn_conv": b_ffn_conv[l], "w_ffn_down": w_ffn_down[l],
        }
        xp, s_f, s_b = trunk_layer(xp, cond_ctx, prm, zero_state, zero_state, None)
        new_f.append(s_f)
        new_b.append(s_b)
        xl, _, _ = trunk_layer(xl, c, prm, state_ssd_fwd[:, l], state_ssd_bwd[:, l], rows)

    new_state_ssd_fwd = jnp.stack(new_f, axis=1)
    new_state_ssd_bwd = jnp.stack(new_b, axis=1)
    return (xp, xl, new_state_ssd_fwd, new_state_ssd_bwd)
```

```python
import numpy as np
from contextlib import ExitStack
import concourse.bass as bass
import concourse.mybir as mybir
from concourse.bass_utils import run_bass_kernel_spmd

F32 = mybir.dt.float32
BF16 = mybir.dt.bfloat16
ALU = mybir.AluOpType
AF = mybir.ActivationFunctionType

D = 1024
LAT = 4096
CTX = 256
NCTX = 4
NTOK = LAT + NCTX * CTX
TL = 256
HALO = 16
EPS = 1e-6
USE_POOL_POW = True
RSTD_ON_ACT = [False]
SCHED_WIN = 0.6
DIAG31_ALL_DMA = True
CONV3_ON_DVE = True
N_DVE_TAPS = 1
PREP_VEC_ENG = "dve"
VEC_EXCL = True
PSUM_CANON = {"PDdt": "PD", "PDcum": "PD", "PDa": "PD"}
PSUM_KEYS = {"PD", "PB0", "PB1", "PA0", "PA1", "PC0", "PC1", "PT", "PT0", "PT1", "PT2", "PT3",
             "PM0a", "PM0b", "PM1a", "PM1b", "PF0", "PF1", "PY0", "PY1", "PS0", "PS1", "fPT0", "fPT1"}
NH = 16
IN_COLS = 4640
DFF = 2816
O_WSC, O_BSC, O_WCF, O_BCF, O_LNG, O_LNB, O_WFC, O_BFC, NFMV = 0, 36, 48, 296, 304, 312, 320, 716, 760
R_PRE, R_POST, R_SSD, R_FPRE, R_FPOST, R_DSK, R_DTB, R_ALOG, R_BADA, NROW = (
    0, 1024, 2048, 3072, 4096, 5120, 5136, 5168, 5200, 5200 + 6144)


class Tl:
    def __init__(self, h, shape):
        self.h = h
        self.P = shape[0]
        self.F = int(np.prod(shape[1:]))

    def v(self, off=0, dims=None, p0=0, np_=None):
        if dims is None:
            dims = [(1, self.F - off)]
        if np_ is None:
            np_ = self.P - p0
        return bass.AP(self.h, p0 * self.F + off, [[self.F, np_]] + [[s, n] for (s, n) in dims])


class Prog:
    def __init__(self, nc, es):
        self.nc = nc
        self.es = es
        self.ops = []
        self.key_w = {}
        self.key_r = {}
        self.eng_sem = {}
        for e in ("pe", "act", "dve", "pool"):
            self.eng_sem[e] = es.enter_context(nc.semaphore("sem_" + e))
        self.eng_cnt = {e: 0 for e in self.eng_sem}
        self.streams = {}
        self.known = {e: {} for e in ("pe", "act", "dve", "pool", "sp")}
        self.emitted = 0
        self.out_streams = set()

    def op(self, eng, fn, reads=(), writes=(), dma=None, is_out=False, cost=0.3, lat=2.5):
        reads = [PSUM_CANON.get(k, k) for k in reads]
        writes = [PSUM_CANON.get(k, k) for k in writes]
        writes = writes + [k for k in reads if k in PSUM_KEYS and k not in writes]
        idx = len(self.ops)
        deps = set()
        for k in reads:
            if k in self.key_w:
                deps.add(self.key_w[k])
        for k in writes:
            if k in self.key_w:
                deps.add(self.key_w[k])
            for r in self.key_r.get(k, ()):
                deps.add(r)
        rec = dict(eng=eng, fn=fn, deps=deps, dma=dma, sig=None, users=0, cost=cost, lat=lat)
        if dma is not None:
            if dma not in self.streams:
                self.streams[dma] = [self.es.enter_context(self.nc.semaphore("dq_" + dma)), 0, None]
            st = self.streams[dma]
            if st[2] is not None:
                deps.add(st[2])
            st[1] += 16
            st[2] = idx
            rec["sig"] = (st[0], st[1])
            if is_out:
                self.out_streams.add(dma)
        deps.discard(idx)
        self.ops.append(rec)
        for d in deps:
            self.ops[d]["users"] += 1
        for k in reads:
            self.key_r.setdefault(k, []).append(idx)
        for k in writes:
            self.key_w[k] = idx
            self.key_r[k] = []
        return idx

    def flush(self, final=False, win=None):
        nc = self.nc
        ops = self.ops
        lo = self.emitted
        hi = len(ops)
        n = hi - lo
        succ = [[] for _ in range(n)]
        ndep = [0] * n
        for i in range(lo, hi):
            for d in ops[i]["deps"]:
                if d >= lo:
                    succ[d - lo].append(i - lo)
                    ndep[i - lo] += 1
        dur = [0.0] * n
        for i in range(n):
            r = ops[lo + i]
            dur[i] = r["lat"] if r["dma"] is not None else r["cost"]
        blev = [0.0] * n
        for i in range(n - 1, -1, -1):
            b = 0.0
            for s_ in succ[i]:
                if blev[s_] > b:
                    b = blev[s_]
            blev[i] = b + dur[i]
        engs = ("pe", "act", "dve", "pool", "sp")
        free = {e: 0.0 for e in engs}
        ready = {e: [] for e in engs}
        dready = [0.0] * n
        finish = [0.0] * n
        for i in range(n):
            if ndep[i] == 0:
                ready[ops[lo + i]["eng"]].append(i)
        nsched = 0
        order = {e: [] for e in engs}
        WIN = SCHED_WIN if win is None else win
        while nsched < n:
            best = None
            for e in engs:
                rl = ready[e]
                if not rl:
                    continue
                f = free[e]
                if VEC_EXCL and e in ("dve", "pool"):
                    f = max(free["dve"], free["pool"])
                est = min(max(f, dready[i]) for i in rl)
                cand = None
                for i in rl:
                    st_ = max(f, dready[i])
                    if st_ <= est + WIN:
                        key = (-blev[i], i)
                        if cand is None or key < cand[0]:
                            cand = (key, i, st_)
                if best is None or cand[2] < best[2]:
                    best = (e, cand[1], cand[2])
            e, i, st_ = best
            ready[e].remove(i)
            r = ops[lo + i]
            if r["dma"] is not None:
                free[e] = st_ + 0.12
                finish[i] = st_ + r["lat"]
            else:
                free[e] = st_ + r["cost"]
                finish[i] = free[e]
                if VEC_EXCL and e in ("dve", "pool"):
                    free["dve"] = max(free["dve"], free[e])
                    free["pool"] = max(free["pool"], free[e])
            order[e].append(lo + i)
            nsched += 1
            for s_ in succ[i]:
                same_pe = (e == "pe" and ops[lo + s_]["eng"] == "pe" and r["dma"] is None)
                t_ = finish[i] + (0.0 if same_pe else 0.2)
                if t_ > dready[s_]:
                    dready[s_] = t_
                ndep[s_] -= 1
                if ndep[s_] == 0:
                    ready[ops[lo + s_]["eng"]].append(s_)
        self.sim_time = max(finish) if n else 0.0
        print("[sched] block ops=%d simulated_us=%.0f" % (n, self.sim_time))
        per = order
        for e in ("pe", "act", "dve", "pool"):
            for i in per[e]:
                r = ops[i]
                if r["dma"] is None:
                    self.eng_cnt[e] += 1
                    r["sig"] = (self.eng_sem[e], self.eng_cnt[e])
        self.emitted = hi
        prog = self

        def run(e, name):
            known = prog.known[name]
            for i in per[name]:
                r = ops[i]
                need = {}
                for d in r["deps"]:
                    dr = ops[d]
                    if dr["eng"] == "pe" and name == "pe" and dr["dma"] is None:
                        continue
                    sem, val = dr["sig"]
                    key = id(sem)
                    if known.get(key, 0) >= val:
                        continue
                    if key not in need or need[key][1] < val:
                        need[key] = (sem, val)
                for key in sorted(need, key=lambda k_: need[k_][1]):
                    sem, val = need[key]
                    e.wait_ge(sem, val)
                    known[key] = val
                ins = r["fn"](e)
                if r["sig"] is not None:
                    if r["dma"] is not None:
                        ins.then_inc(r["sig"][0], 16)
                    else:
                        ins.then_inc(r["sig"][0], 1)
            if name == "sp":
                for s in prog.streams.values():
                    if s[1] > 0 and known.get(id(s[0]), 0) < s[1]:
                        e.wait_ge(s[0], s[1])
                        known[id(s[0])] = s[1]

        with nc.Block() as block:
            @block.tensor
            def _(e):
                run(e, "pe")

            @block.scalar
            def _(e):
                run(e, "act")

            @block.vector
            def _(e):
                run(e, "dve")

            @block.gpsimd
            def _(e):
                run(e, "pool")

            @block.sync
            def _(e):
                run(e, "sp")


def build_program():
    nc = bass.Bass("TRN2", target_bir_lowering=False)

    def din(name, shape, dt=F32):
        return nc.dram_tensor(name, list(shape), dt, kind="ExternalInput").ap()

    def dout(name, shape, dt=F32):
        return nc.dram_tensor(name, list(shape), dt, kind="ExternalOutput").ap()

    def dscr(name, shape, dt):
        return nc.dram_tensor(name, list(shape), dt).ap()

    x_all = din("x_all", [NTOK, D])
    st_f = din("st_f", [1024, 128])
    st_b = din("st_b", [1024, 128])
    c_in = din("c_in", [128, 16])
    cst = din("cst", [128, 768])
    fmv_d = din("fmv", [128, NFMV])
    rowv = din("rowv", [1, NROW])
    w_ada = din("w_ada", [D, 6 * D])
    w_in = din("w_in", [D, IN_COLS])
    w_out = din("w_out", [2 * D, D])
    w_up = din("w_up", [D, 2 * DFF])
    w_down = din("w_down", [DFF, D])
    y_all = dout("y_all", [NTOK, D])
    ns_f = dout("ns_f", [NCTX, 1024, 128])
    ns_b = dout("ns_b", [NCTX, 1024, 128])

    mod_d = dscr("mod_d", [2, 6 * D], F32)
    w_in_t = dscr("w_in_t", [28, 128, 8, 128], BF16)
    w_up_t = dscr("w_up_t", [44, 128, 8, 128], BF16)
    dg31_d = dscr("dg31_d", [8, 128, 31 * 128], BF16)
    x1_d = dscr("x1_d", [NTOK, D], F32)
    seqs = []
    for s in range(1 + NCTX):
        L = LAT if s == 0 else CTX
        base = 0 if s == 0 else LAT + (s - 1) * CTX
        sq = dict(s=s, L=L, base=base, cond=1 if s == 0 else 0, nt=L // TL, nch=L // 128)
        sq["hT1"] = dscr(f"hT1_{s}", [8, 128, L + 2 * HALO], BF16)
        sq["oin"] = dscr(f"oin_{s}", [16, 128, L], BF16)
        sq["rec"] = dscr(f"rec_{s}", [L // 128, 128, 3328], BF16)
        sq["e1"] = dscr(f"e1_{s}", [L // 128, 128, 64], F32)
        sq["cd"] = dscr(f"cd_{s}", [L // 128, 128, 256], BF16)
        if s == 0:
            sq["hT2"] = dscr(f"hT2_{s}", [8, 128, 66 * 64], BF16)
            sq["h2off"] = 64
        else:
            sq["hT2"] = dscr(f"hT2_{s}", [8, 128, L], BF16)
            sq["h2off"] = 0
        seqs.append(sq)

    with ExitStack() as es:
        P = Prog(nc, es)

        uid = [0]

        def sbt(st, name, shape, dt):
            uid[0] += 1
            return Tl(st.enter_context(nc.sbuf_tensor("sb%d_%s" % (uid[0], name), list(shape), dt)), shape)

        def pst(st, name, shape, dt):
            return Tl(st.enter_context(nc.psum_tensor("ps_" + name, list(shape), dt)), shape)

        pid = [0]

        def psum_set(st, spec):
            out = []
            for (shape, dt) in spec:
                pid[0] += 1
                out.append(pst(st, "p%d" % pid[0], shape, dt))
            return out

        cst_f = sbt(es, "cst_f", [128, 768], F32)
        cst_b = sbt(es, "cst_b", [128, 768], BF16)
        fmv = sbt(es, "fmv", [128, NFMV], F32)
        diag3 = sbt(es, "diag3", [128, 36 * 128], BF16)
        WZ = sbt(es, "WZ", [128, 8 * 1056], BF16)
        zero_b = sbt(es, "zero_b", [128, 1024], BF16)
        a_b = sbt(es, "a_b", [128, 32], F32)
        dtb_b = sbt(es, "dtb_b", [128, 32], F32)
        sm = sbt(es, "sm", [128, 64], F32)
        nhalf = sbt(es, "nhalf", [128, 8], F32)
        fmvh = sbt(es, "fmvh", [128, NFMV], F32)

        IDF, UINC, LINC, LSTR, USTR, ONES = [i * 128 for i in range(6)]

        def fsz(ap):
            n_ = 1
            for (s_, c_) in list(ap.ap)[1:]:
                n_ *= c_
            return n_

        def dma(fn_out, fn_in, stream, reads=(), writes=(), eng="sp", is_out=False):
            def f(e):
                return e.dma_start(out=fn_out, in_=fn_in)
            try:
                nb = list(fn_out.ap)[0][1] * fsz(fn_out) * (2 if fn_out.dtype == BF16 else 4)
            except Exception:
                nb = 1 << 18
            lat = 2.2 + nb / 120000.0
            return P.op(eng, f, reads=reads, writes=writes, dma=stream, is_out=is_out, lat=lat)

        def act(out, in_, func, reads, writes, bias=None, scale=None, accum=None):
            kw = {}
            if bias is not None:
                kw["bias"] = bias
            if scale is not None:
                kw["scale"] = scale
            if accum is not None:
                kw["accum_out"] = accum

            def f(e):
                return e.activation(out=out, in_=in_, func=func, **kw)
            return P.op("act", f, reads=reads, writes=writes, cost=0.2 + fsz(out) / 1150.0)

        def vcost(eng, out):
            n_ = fsz(out)
            if eng == "pool":
                return 0.4 + n_ / 600.0
            return 0.17 + n_ / 960.0

        def tt(eng, out, in0, in1, op, reads, writes):
            def f(e):
                return e.tensor_tensor(out=out, in0=in0, in1=in1, op=op)
            return P.op(eng, f, reads=reads, writes=writes, cost=vcost(eng, out))

        def stt(eng, out, in0, scalar, in1, op0, op1, reads, writes):
            def f(e):
                return e.scalar_tensor_tensor(out=out, in0=in0, scalar=scalar, in1=in1, op0=op0, op1=op1)
            return P.op(eng, f, reads=reads, writes=writes, cost=vcost(eng, out))

        def ts(eng, out, in0, s1, s2, op0, op1, reads, writes):
            def f(e):
                if s2 is None:
                    return e.tensor_scalar(out=out, in0=in0, scalar1=s1, scalar2=None, op0=op0)
                return e.tensor_scalar(out=out, in0=in0, scalar1=s1, scalar2=s2, op0=op0, op1=op1)
            return P.op(eng, f, reads=reads, writes=writes, cost=vcost(eng, out))

        def cp(eng, out, in_, reads, writes):
            def f(e):
                return e.tensor_copy(out=out, in_=in_)
            return P.op(eng, f, reads=reads, writes=writes, cost=vcost(eng, out))

        def mm(specs, reads, writes):
            def f(e):
                ins = None
                for (o, l, r, st, sp) in specs:
                    ins = e.matmul(o, lhsT=l, rhs=r, start=st, stop=sp)
                return ins
            c_ = 0.0
            for (o, l, r, st, sp) in specs:
                c_ += max(0.1, 0.012 + fsz(r) / 2400.0) * (4.0 if l.dtype == F32 else 1.0)
            return P.op("pe", f, reads=reads, writes=writes, cost=c_)

        def tr(specs, reads, writes):
            def f(e):
                ins = None
                for (o, i, idn) in specs:
                    ins = e.transpose(out=o, in_=i, identity=idn)
                return ins
            return P.op("pe", f, reads=reads, writes=writes, cost=0.11 * len(specs))

        def rstd_from_ms(col, reads_key):
            if USE_POOL_POW and not RSTD_ON_ACT[0]:
                ts("pool", sm.v(col + 2, [(1, 1)]), sm.v(col, [(1, 1)]), EPS, None, ALU.add, None,
                   [reads_key], ["sm%d" % (col + 2)])
                tt("pool", sm.v(col + 1, [(1, 1)]), sm.v(col + 2, [(1, 1)]), nhalf.v(0, [(1, 1)]), ALU.pow,
                   ["sm%d" % (col + 2), "nhalf"], ["sm%d" % (col + 1)])
                return
            act(sm.v(col + 2, [(1, 1)]), sm.v(col, [(1, 1)]), AF.Ln, [reads_key], ["sm%d" % (col + 2)], bias=EPS, scale=1.0)
            act(sm.v(col + 1, [(1, 1)]), sm.v(col + 2, [(1, 1)]), AF.Exp, ["sm%d" % (col + 2)], ["sm%d" % (col + 1)], scale=-0.5)

        def silu_psum(out, ps, n, tT, tV, hb, hs, rkeys, wkey, tkeys):
            kw = {} if hb is None else {"bias": hb}
            act(tT.v(0, [(1, n)]), ps, AF.Tanh, rkeys, [tkeys[0]], scale=hs, **kw)
            act(tV.v(0, [(1, n)]), ps, AF.Identity, rkeys, [tkeys[1]], scale=hs, **kw)
            stt("dve", out, tT.v(0, [(1, n)]), 1.0, tV.v(0, [(1, n)]), ALU.add, ALU.mult, list(tkeys), [wkey])

        ident_b = cst_b.v(IDF, [(1, 128)])
        ident_f = cst_f.v(IDF, [(1, 128)])

        with ExitStack() as s0:
            (PD0,) = psum_set(s0, [([128, 512], F32)])
            PTs = psum_set(s0, [([128, 1024], BF16)] * 4)
            dma(cst_f.v(), cst, "ld0", writes=["cst_f"])
            dma(fmv.v(), fmv_d, "ld1", writes=["fmv"])
            cp("dve", cst_b.v(), cst_f.v(), ["cst_f"], ["cst_b"])
            P.op("dve", lambda e: e.memset(zero_b.v(), 0.0), writes=["zero_b"])
            P.op("dve", lambda e: e.memset(nhalf.v(), -0.5), writes=["nhalf"])
            ts("dve", fmvh.v(), fmv.v(), 0.5, None, ALU.mult, None, ["fmv"], ["fmvh"])
            dma(a_b.v(), rowv[:, R_ALOG:R_ALOG + 32].partition_broadcast(128), "ld0", writes=["a_b"])
            dma(dtb_b.v(), rowv[:, R_DTB:R_DTB + 32].partition_broadcast(128), "ld1", writes=["dtb_b"])
            act(a_b.v(), a_b.v(), AF.Exp, ["a_b"], ["a_b"])
            ts("dve", a_b.v(), a_b.v(), -1.0, None, ALU.mult, None, ["a_b"], ["a_b"])
            tt("dve", diag3.v(0, [(128, 36), (1, 128)]), cst_b.v(IDF, [(0, 36), (1, 128)]),
               fmv.v(O_WSC, [(1, 36), (0, 128)]), ALU.mult, ["cst_b", "fmv"], ["diag3"])

            stgs = [sbt(s0, f"stg{i}", [128, 5632], BF16) for i in range(2)]
            dgs = [sbt(s0, f"dgs{i}", [128, 31 * 128], BF16) for i in range(2)]
            NWA = 3
            wad = [sbt(s0, f"wad{i}", [128, 8 * 512], F32) for i in range(NWA)]
            scv = sbt(s0, "scv", [128, 16], F32)
            badas = [sbt(s0, f"bada{i}", [2, 512], F32) for i in range(2)]
            modss = [sbt(s0, f"modsb{i}", [2, 512], F32) for i in range(2)]
            dma(scv.v(), c_in, "ld0", writes=["scv"])
            act(scv.v(), scv.v(), AF.Silu, ["scv"], ["scv"])
            for j in range(12):
                wb = wad[j % NWA]
                dma(badas[j % 2].v(), rowv[:, R_BADA + j * 512:R_BADA + (j + 1) * 512].partition_broadcast(2),
                    f"bal{j % 2}", writes=[f"bada{j % 2}"])
                dma(wb.v(0, [(512, 8), (1, 512)]),
                    w_ada[:, j * 512:(j + 1) * 512].rearrange("(k p) n -> p k n", p=128),
                    f"wad{j % NWA}", writes=[f"wad{j % NWA}"])
                mm([(PD0.v(0, [(1, 512)], 0, 2), scv.v(k * 2, [(1, 2)]), wb.v(k * 512, [(1, 512)]), k == 0, k == 7)
                    for k in range(8)], ["scv", f"wad{j % NWA}"], ["PD"])
                tt("dve", modss[j % 2].v(), PD0.v(0, [(1, 512)], 0, 2), badas[j % 2].v(),
                   ALU.add, ["PD", f"bada{j % 2}"], [f"modsb{j % 2}"])
                dma(mod_d[:, j * 512:(j + 1) * 512], modss[j % 2].v(), f"mst{j % 2}", reads=[f"modsb{j % 2}"],
                    writes=["mod_d"])
            for m in range(8):
                b = dgs[m % 2]
                tt("pool", b.v(0, [(128, 31), (1, 128)]), cst_b.v(IDF, [(0, 31), (1, 128)]),
                   fmv.v(O_WCF + m * 31, [(1, 31), (0, 128)]), ALU.mult, ["cst_b", "fmv"], [f"dgs{m % 2}"])
                dma(dg31_d[m], b.v(), f"dgst{m % 2}", reads=[f"dgs{m % 2}"], writes=["dg31_d"], eng="pool")
            ws = 0
            f32s = [sbt(s0, f"f32s{i}", [128, 1408], F32) for i in range(2)]
            qc = [0]

            def stage_rows(src_rows, ncols, stg, sk):
                qn = ncols // 4
                for q in range(4):
                    fb = f32s[qc[0] % 2]
                    fk = f"f32s{qc[0] % 2}"
                    dma(fb.v(0, [(1, qn)]), src_rows[:, q * qn:(q + 1) * qn], f"wf{qc[0] % 2}", writes=[fk])
                    act(stg.v(q * qn, [(1, qn)]), fb.v(0, [(1, qn)]), AF.Copy, [fk], [sk])
                    qc[0] += 1

            for k in range(8):
                stg = stgs[ws % 2]
                sk = f"stg{ws % 2}"
                stage_rows(w_in[k * 128:(k + 1) * 128, :], IN_COLS, stg, sk)
                dma(w_in_t[0:12, :, k, :].rearrange("m p c -> p m c"), stg.v(1024, [(128, 12), (1, 128)]),
                    f"wsta{ws % 2}", reads=[sk], writes=["w_in_ta_%d" % k], eng="act")
                dma(w_in_t[12:28, :, k, :].rearrange("m p c -> p m c"), stg.v(2592, [(128, 16), (1, 128)]),
                    f"wstb{ws % 2}", reads=[sk], writes=["w_in_tb_%d" % k], eng="act")
                cp("pool", WZ.v(k * 1056, [(1, 1024)]), stg.v(0, [(1, 1024)]), [sk], ["WZ"])
                cp("pool", WZ.v(k * 1056 + 1024, [(1, 32)]), stg.v(2560, [(1, 32)]), [sk], ["WZ"])
                ws += 1
            for k in range(8):
                stg = stgs[ws % 2]
                sk = f"stg{ws % 2}"
                stage_rows(w_up[k * 128:(k + 1) * 128, :], 2 * DFF, stg, sk)
                dma(w_up_t[:, :, k, :].rearrange("m p c -> p m c"), stg.v(0, [(128, 44), (1, 128)]),
                    f"wsta{ws % 2}", reads=[sk], writes=["w_up_t_%d" % k], eng="act")
                ws += 1

            def load_mod(dst, cond, idx, stream, key):
                dma(dst.v(), mod_d[cond:cond + 1, idx * D:(idx + 1) * D].partition_broadcast(128), stream,
                    reads=["mod_d"], writes=[key])

            def load_row(dst, off, n, stream, key):
                dma(dst.v(0, [(1, n)]), rowv[:, off:off + n].partition_broadcast(128), stream, writes=[key])

            def norm_mod_transpose(xin, xin_key, G, G_key, SH, SH_key, tmpF, tmpF_key, hn, hn_key, junk, junk_key,
                                   hT, hT_key, smcol, PTt, ptk):
                act(junk.v(0, [(1, 1024)]), xin.v(), AF.Square, [xin_key], [junk_key, "sm%d" % smcol],
                    scale=1.0 / 32.0, accum=sm.v(smcol, [(1, 1)]))
                rstd_from_ms(smcol, "sm%d" % smcol)
                stt("dve", tmpF.v(), xin.v(), sm.v(smcol + 1, [(1, 1)]), G.v(), ALU.mult, ALU.mult,
                    [xin_key, "sm%d" % (smcol + 1), G_key], [tmpF_key])
                tt("dve", hn.v(0, [(1, 1024)]), tmpF.v(), SH.v(), ALU.add, [tmpF_key, SH_key], [hn_key])
                if hT is not None:
                    norm_mod_B(hn, hn_key, hT, hT_key, PTt, ptk)

            def norm_mod_B(hn, hn_key, hT, hT_key, PTt, ptk):
                tr([(PTt.v(k * 128, [(1, 128)]), hn.v(k * 128, [(1, 128)]), ident_b) for k in range(8)],
                   [hn_key, "cst_b"], [ptk])
                act(hT.v(0, [(1, 1024)]), PTt.v(), AF.Copy, [ptk], [hT_key])

            Gm = [sbt(s0, f"G{i}", [128, D], F32) for i in range(2)]
            SHm = [sbt(s0, f"SH{i}", [128, D], F32) for i in range(2)]
            wr = sbt(s0, "wr", [128, D], F32)
            NB = 3
            m1work = []
            xs_ = [sbt(s0, f"xs{i}", [128, D], F32) for i in range(NB)]
            tFs = [sbt(s0, f"tF{i}", [128, D], F32) for i in range(NB)]
            hns = [sbt(s0, f"hn{i}", [128, D], BF16) for i in range(NB)]
            jks = [sbt(s0, f"jk{i}", [128, D], BF16) for i in range(NB)]
            hTs = [sbt(s0, f"hTs{i}", [128, D], BF16) for i in range(NB)]
            load_row(wr, R_PRE, D, "ld0", "wr")
            for cnd in (1, 0):
                load_mod(Gm[cnd], cnd, 1, "ld0", "G%d" % cnd)
                load_mod(SHm[cnd], cnd, 0, "ld1", "SH%d" % cnd)
                stt("dve", Gm[cnd].v(), Gm[cnd].v(), 1.0, wr.v(), ALU.add, ALU.mult, ["G%d" % cnd, "wr"],
                    ["G%d" % cnd])
            for sq in seqs:
                L = sq["L"]
                h1 = sq["hT1"]
                for (a, b_) in ((0, HALO), (L + HALO, L + 2 * HALO)):
                    dma(h1[:, :, a:b_].rearrange("k p t -> p k t"), zero_b.v(0, [(HALO, 8), (1, HALO)]),
                        "zst", reads=["zero_b"], writes=["hT1_%d_z" % sq["s"]])
                for c in range(sq["nch"]):
                    m1work.append((sq, c))
            def m1_A(sq, c, it):
                i3 = it % NB
                r0 = sq["base"] + c * 128
                dma(xs_[i3].v(), x_all[r0:r0 + 128, :], f"xl{i3}", writes=[f"xs{i3}"])
                norm_mod_transpose(xs_[i3], f"xs{i3}", Gm[sq["cond"]], "G%d" % sq["cond"], SHm[sq["cond"]],
                                   "SH%d" % sq["cond"], tFs[i3], f"tF{i3}", hns[i3], f"hn{i3}",
                                   jks[i3], f"jk{i3}", None, None, 16 + 4 * i3, None, None)

            def m1_B(sq, c, it):
                i3 = it % NB
                norm_mod_B(hns[i3], f"hn{i3}", hTs[i3], f"hTs{i3}", PTs[it % 4], f"PT{it % 4}")
                dma(sq["hT1"][:, :, HALO + c * 128:HALO + (c + 1) * 128].rearrange("k p t -> p k t"),
                    hTs[i3].v(0, [(128, 8), (1, 128)]), f"hst{i3}", reads=[f"hTs{i3}"],
                    writes=["hT1_%d_%d" % (sq["s"], c)], eng="act")

            for i, (sq, c) in enumerate(m1work):
                m1_A(sq, c, i)
                if i >= 1:
                    m1_B(m1work[i - 1][0], m1work[i - 1][1], i - 1)
            m1_B(m1work[-1][0], m1work[-1][1], len(m1work) - 1)
            P.flush()

        TLH = TL + 2 * HALO

        def interleave(gens):
            gens = [g for g in gens if g is not None]
            while gens:
                for g in list(gens):
                    try:
                        next(g)
                    except StopIteration:
                        gens.remove(g)

        def mixer_phase(pass2):
            with ExitStack() as s2:
                PA, PBX, PC = psum_set(s2, [([128, 1024], F32)] * 3)
                (PD,) = psum_set(s2, [([128, 512], F32)])
                (PT,) = psum_set(s2, [([128, 1024], BF16)])
                hin = [sbt(s2, f"hin{i}", [128, 8 * TLH], BF16) for i in range(2)]
                NWR = 6
                wrg = [sbt(s2, f"wrg{i}", [128, 1024], BF16) for i in range(NWR)]
                xbc_pre = sbt(s2, "xbc_pre", [128, 12 * TLH], BF16)
                xbc_cs = [sbt(s2, f"xbc_c{i}", [128, 12 * TL], BF16) for i in range(3)]
                S_run = sbt(s2, "S_run", [128, D], F32)
                Sbf = [sbt(s2, "Sbf0", [128, D], BF16)]
                xs_tm = sbt(s2, "xs_tm", [128, D], BF16)
                F1 = sbt(s2, "F1", [128, D], F32)
                F2 = sbt(s2, "F2", [128, D], F32)
                stTs = [sbt(s2, f"stT{i}", [128, TL], F32) for i in range(2)]
                c3s = [sbt(s2, f"c3_{i}", [128, TL], F32) for i in range(2)]
                stVs = [sbt(s2, f"stV{i}", [128, TL], F32) for i in range(2)]
                NCL = TL // 128
                H_dts = [[sbt(s2, f"Hdts{i}{c}", [128, 256], F32) for c in range(NCL)] for i in range(2)]
                REC = [[sbt(s2, f"REC{i}{c}", [128, 3328], BF16) for c in range(NCL)] for i in range(2)]
                H_xw = [[sbt(s2, f"Hxw{i}{c}", [128, D], BF16) for c in range(NCL)] for i in range(2)]
                DT, DA, CUM, E1, D2, WW = 0, 32, 64, 128, 192, 224
                XBP_K = ["xbc_pre"]

                def xbc_pre_v(off, dims):
                    return xbc_pre.v(off, dims)

                if pass2:
                    H_yd = [[sbt(s2, f"Hyd{i}{c}", [128, D], BF16) for c in range(NCL)] for i in range(2)]
                    ar1 = sbt(s2, "ar1", [128, 4096], BF16)
                    MT = sbt(s2, "MT", [128, 2048], BF16)
                    usq = sbt(s2, "usq", [128, 8 * TL], BF16)
                    u_pre = sbt(s2, "u_pre", [128, 8 * TLH], BF16)
                    u_cs = [sbt(s2, f"u_c{i}", [128, 8 * TL], BF16) for i in range(1)]
                    sg = sbt(s2, "sg", [128, TLH], F32)
                    lnA = sbt(s2, "lnA", [128, TL], F32)
                    lnB = sbt(s2, "lnB", [128, TL], F32)
                    lnT = sbt(s2, "lnT", [128, TL], F32)
                    dgr = [sbt(s2, f"dgr{i}", [128, 31 * 128], BF16) for i in range(2)]
                    Db = sbt(s2, "Db", [128, D], BF16)
                    dsk = sbt(s2, "dsk", [128, 16], F32)
                    xdt = sbt(s2, "xdt", [128, D], BF16)
                    xsD = sbt(s2, "xsD", [128, D], BF16)
                    cbm = sbt(s2, "cbm", [128, 512], BF16)
                    try:
                        print("SBUF bytes remaining in mixer pass2:", nc.sbuf_bytes_remaining())
                    except Exception as _e:
                        print("sbuf_bytes_remaining unavailable", _e)
                    load_row(dsk, R_DSK, 16, "ld1", "dsk")
                    cp("dve", Db.v(0, [(64, 16), (1, 64)]), dsk.v(0, [(1, 16), (0, 64)]), ["dsk"], ["Db"])

                cnt = dict(w=0, d=0, y=0, sb=0)

                def wload(ci):
                    i = cnt["w"] % NWR
                    cnt["w"] += 1
                    dma(wrg[i].v(), w_in_t[ci].rearrange("p k c -> p (k c)"), f"wr{i}",
                        reads=["w_in_ta_%d" % k_ for k_ in range(8)] + ["w_in_tb_%d" % k_ for k_ in range(8)],
                        writes=[f"wrg{i}"])
                    return wrg[i], f"wrg{i}"

                def prep_chunk(hb, hkey, xbc_c, xck, bi, cl):
                    dts = H_dts[bi][cl]
                    kd = f"Hdts{bi}{cl}"
                    t0 = HALO + cl * 128
                    mm([(PD.v(256, [(1, 32)]), hb.v(k * TLH + t0, [(1, 128)]), WZ.v(k * 1056 + 1024, [(1, 32)]),
                         k == 0, k == 7) for k in range(8)], [hkey, "WZ"], ["PDdt"])
                    tt("dve", dts.v(D2, [(1, 32)]), PD.v(256, [(1, 32)]), dtb_b.v(), ALU.add, ["PDdt", "dtb_b"], [kd])
                    act(dts.v(D2, [(1, 32)]), dts.v(D2, [(1, 32)]), AF.Exp, [kd], [kd])
                    act(dts.v(DT, [(1, 32)]), dts.v(D2, [(1, 32)]), AF.Ln, [kd], [kd], bias=1.0, scale=1.0)
                    tt("dve", dts.v(DA, [(1, 32)]), dts.v(DT, [(1, 32)]), a_b.v(), ALU.mult, [kd, "a_b"], [kd])
                    mm([(PD.v(320, [(1, 16)]), cst_f.v(UINC, [(1, 128)]), dts.v(DA, [(1, 16)]), True, True),
                        (PD.v(336, [(1, 16)]), cst_f.v(LINC, [(1, 128)]), dts.v(DA + 16, [(1, 16)]), True, True),
                        (PD.v(352, [(1, 32)]), cst_f.v(ONES, [(1, 128)]), dts.v(DA, [(1, 32)]), True, True)],
                       [kd, "cst_f"], ["PDcum"])
                    act(dts.v(CUM, [(1, 64)]), PD.v(320, [(1, 64)]), AF.Copy, ["PDcum"], [kd])
                    act(dts.v(E1, [(1, 64)]), dts.v(CUM, [(1, 64)]), AF.Exp, [kd], [kd])
                    tt("dve", dts.v(D2, [(1, 32)]), dts.v(CUM + 32, [(1, 32)]), dts.v(CUM, [(1, 32)]), ALU.subtract,
                       [kd], [kd])
                    act(dts.v(D2, [(1, 32)]), dts.v(D2, [(1, 32)]), AF.Exp, [kd], [kd])
                    tt("dve", dts.v(WW, [(1, 32)]), dts.v(D2, [(1, 32)]), dts.v(DT, [(1, 32)]), ALU.mult, [kd], [kd])
                    yield
                    tr([(PT.v(k * 128, [(1, 128)]), xbc_c.v(k * TL + cl * 128, [(1, 128)]), ident_b) for k in range(8)],
                       [xck, "cst_b"], ["PT"])
                    act(xs_tm.v(0, [(1, 1024)]), PT.v(), AF.Copy, ["PT"], ["xs_tm"])
                    tr([(PT.v(k * 128, [(1, 128)]), xbc_c.v((8 + k) * TL + cl * 128, [(1, 128)]), ident_b)
                        for k in range(2)], [xck, "cst_b"], ["PT"])
                    rec = REC[bi][cl]
                    rk = f"REC{bi}{cl}"
                    act(rec.v(3072, [(1, 256)]), PT.v(0, [(1, 256)]), AF.Copy, ["PT"], [rk + ".B"])
                    tt(PREP_VEC_ENG, H_xw[bi][cl].v(0, [(64, 16), (1, 64)]), xs_tm.v(0, [(64, 16), (1, 64)]),
                       dts.v(WW + 16, [(1, 16), (0, 64)]), ALU.mult, ["xs_tm", kd], [f"Hxw{bi}{cl}"])
                    tt(PREP_VEC_ENG, rec.v(2048, [(64, 16), (1, 64)]), xs_tm.v(0, [(64, 16), (1, 64)]),
                       dts.v(WW, [(1, 16), (0, 64)]), ALU.mult, ["xs_tm", kd], [rk + ".xw"])
                    yield
                    if not pass2:
                        return
                    rhs_v = lambda off, dims: ar1.v(off, dims)
                    lexp_v = lambda off, dims: ar1.v(2048 + off, dims)
                    mm([(PBX.v(hf * 512, [(1, 512)]), hb.v(k * TLH + t0, [(1, 128)]),
                         WZ.v(k * 1056 + hf * 512, [(1, 512)]), k == 0, k == 7)
                        for hf in range(2) for k in range(8)], [hkey, "WZ"], ["PB0", "PB1"])
                    act(lexp_v(0, [(1, 1024)]), PBX.v(), AF.Tanh, ["PB0", "PB1"], ["Lexp"], scale=0.5)
                    act(lexp_v(1024, [(1, 1024)]), PBX.v(), AF.Identity, ["PB0", "PB1"], ["Lexp"], scale=0.5)
                    stt("dve", rec.v(1024, [(1, 1024)]), lexp_v(0, [(1, 1024)]), 1.0, lexp_v(1024, [(1, 1024)]),
                        ALU.add, ALU.mult, ["Lexp"], [rk + ".sz"])
                    yield
                    mm([(PD.v(g * 128, [(1, 128)]), xbc_c.v((8 + g) * TL + cl * 128, [(1, 128)]),
                         xbc_c.v((10 + g) * TL + cl * 128, [(1, 128)]), True, True) for g in range(2)],
                       [xck], ["PDa"])
                    tt("dve", cbm.v(0, [(128, 2), (1, 128)]), PD.v(0, [(128, 2), (1, 128)]),
                       cst_f.v(UINC, [(0, 2), (1, 128)]), ALU.mult, ["PDa", "cst_f"], ["cbm"])
                    tt("dve", cbm.v(256, [(128, 2), (1, 128)]), PD.v(0, [(128, 2), (1, 128)]),
                       cst_f.v(LINC, [(0, 2), (1, 128)]), ALU.mult, ["PDa", "cst_f"], ["cbm"])
                    tt(PREP_VEC_ENG, xsD.v(), xs_tm.v(), Db.v(), ALU.mult, ["xs_tm", "Db"], ["xsD"])
                    mm([(PA.v(hf * 512, [(1, 512)]), ident_b, xsD.v(hf * 512, [(1, 512)]), True, False)
                        for hf in range(2)], ["xsD", "cst_b"], ["PA0", "PA1"])
                    yield
                    for d in range(2):
                        msk = UINC if d == 0 else LINC
                        lmask = LSTR if d == 0 else USTR
                        tt(PREP_VEC_ENG, rhs_v(0, [(128, 16), (1, 128)]), cst_f.v(msk, [(0, 16), (1, 128)]),
                           dts.v(DA + 16 * d, [(1, 16), (0, 128)]), ALU.mult, ["cst_f", kd], ["rhs"])
                        tt(PREP_VEC_ENG, xdt.v(0, [(64, 16), (1, 64)]), xs_tm.v(0, [(64, 16), (1, 64)]),
                           dts.v(DT + 16 * d, [(1, 16), (0, 64)]), ALU.mult, ["xs_tm", kd], ["xdt"])
                        for q in range(4):
                            mm([(PC.v(0, [(1, 512)]), cst_b.v(lmask, [(1, 128)]),
                                 rhs_v(q * 512, [(1, 512)]), True, True)], ["rhs", "cst_b"], ["PC0"])
                            act(lexp_v(q * 512, [(1, 512)]), PC.v(0, [(1, 512)]), AF.Exp, ["PC0"], ["Lexp"])
                            if q % 2 == 1:
                                yield
                        tt("dve", MT.v(0, [(1024, 2), (128, 8), (1, 128)]),
                           lexp_v(0, [(1024, 2), (128, 8), (1, 128)]),
                           cbm.v(d * 256, [(128, 2), (0, 8), (1, 128)]), ALU.mult, ["Lexp", "cbm"], ["MT"])
                        mm([(PA.v(h * 64, [(1, 64)]), MT.v(h * 128, [(1, 128)]), xdt.v(h * 64, [(1, 64)]),
                             False, d == 1) for h in range(NH)], ["MT", "xdt"], ["PA0", "PA1"])
                        yield
                    act(H_yd[bi][cl].v(0, [(1, 1024)]), PA.v(), AF.Copy, ["PA0", "PA1"], [f"Hyd{bi}{cl}"])
                    yield

                def fm_stage(sq, T, n):
                    s = sq["s"]
                    bi = n % 2
                    hb = hin[bi]
                    hkey = f"hin{bi}"
                    xbc_c = xbc_cs[n % 3]
                    xck = f"xbc_c{n % 3}"
                    dma(hb.v(0, [(TLH, 8), (1, TLH)]),
                        sq["hT1"][:, :, T * TL:T * TL + TLH].rearrange("k p t -> p k t"),
                        f"hl{bi}", reads=["hT1_%d_z" % s] + ["hT1_%d_%d" % (s, c_) for c_ in
                                                      range(max(0, 2 * T - 1), min(sq["nch"], 2 * T + 3))],
                        writes=[hkey])
                    for m in range(12):
                        w, wk = wload(m)
                        po = (m % 2) * 512
                        pk = "PB%d" % (m % 2)
                        mm([(PBX.v(po, [(1, TLH)]), w.v(k * 128, [(1, 128)]), hb.v(k * TLH, [(1, TLH)]), k == 0, k == 7)
                            for k in range(8)], [wk, hkey], [pk])
                        act(xbc_pre_v(m * TLH, [(1, TLH)]), PBX.v(po, [(1, TLH)]), AF.Copy, [pk], XBP_K)
                        if m % 2 == 1:
                            yield
                    for m in range(12):
                        po = (m % 2) * 512
                        pk = "PB%d" % (m % 2)
                        if CONV3_ON_DVE:
                            c3 = c3s[m % 2]
                            c3k = f"c3_{m % 2}"
                            for k3 in range(3):
                                rv = xbc_pre_v(m * TLH + HALO - 1 + k3, [(1, TL)])
                                wv = fmv.v(O_WSC + m * 3 + k3, [(1, 1)])
                                if k3 == 0:
                                    ts("dve", c3.v(), rv, wv, None, ALU.mult, None, XBP_K + ["fmv"], [c3k])
                                else:
                                    stt("dve", c3.v(), rv, wv, c3.v(), ALU.mult, ALU.add, XBP_K + ["fmv", c3k], [c3k])
                            silu_psum(xbc_c.v(m * TL, [(1, TL)]), c3.v(), TL, stTs[m % 2], stVs[m % 2],
                                      fmvh.v(O_BSC + m, [(1, 1)]), 0.5, [c3k, "fmvh"], xck,
                                      (f"stT{m % 2}", f"stV{m % 2}"))
                        else:
                            mm([(PBX.v(po, [(1, TL)]), diag3.v((m * 3 + k3) * 128, [(1, 128)]),
                                 xbc_pre_v(m * TLH + HALO - 1 + k3, [(1, TL)]), k3 == 0, k3 == 2) for k3 in range(3)],
                               ["diag3"] + XBP_K, [pk])
                            silu_psum(xbc_c.v(m * TL, [(1, TL)]), PBX.v(po, [(1, TL)]), TL, stTs[m % 2], stVs[m % 2],
                                      fmvh.v(O_BSC + m, [(1, 1)]), 0.5, [pk, "fmvh"], xck,
                                      (f"stT{m % 2}", f"stV{m % 2}"))
                        if m % 2 == 1:
                            yield
                    if pass2:
                        u_c = u_cs[0]
                        uck = "u_c0"
                        for m in range(8):
                            wg, wgk = wload(20 + m)
                            mm([(PBX.v(0, [(1, TLH)]), wg.v(k * 128, [(1, 128)]), hb.v(k * TLH, [(1, TLH)]),
                                 k == 0, k == 7) for k in range(8)], [wgk, hkey], ["PB0"])
                            act(sg.v(), PBX.v(0, [(1, TLH)]), AF.Tanh, ["PB0"], ["sg"], scale=0.5)
                            wa, wak = wload(12 + m)
                            mm([(PBX.v(512, [(1, TLH)]), wa.v(k * 128, [(1, 128)]), hb.v(k * TLH, [(1, TLH)]),
                                 k == 0, k == 7) for k in range(8)], [wak, hkey], ["PB1"])
                            stt("dve", u_pre.v(m * TLH, [(1, TLH)]), sg.v(), 1.0, PBX.v(512, [(1, TLH)]), ALU.add,
                                ALU.mult, ["PB1", "sg"], ["u_pre"])
                            yield
                        for m in range(8):
                            di = m % 2
                            po = (m % 2) * 512
                            pk = "PB%d" % (m % 2)
                            if di == 0 or DIAG31_ALL_DMA:
                                dma(dgr[di].v(), dg31_d[m], f"dgl{di}", reads=["dg31_d"], writes=[f"dgr{di}"])
                            else:
                                tt("dve", dgr[1].v(0, [(128, 31), (1, 128)]), cst_b.v(IDF, [(0, 31), (1, 128)]),
                                   fmv.v(O_WCF + m * 31, [(1, 31), (0, 128)]), ALU.mult, ["cst_b", "fmv"], ["dgr1"])
                            mm([(PBX.v(po, [(1, TL)]), dgr[di].v(k * 128, [(1, 128)]),
                                 u_pre.v(m * TLH + 1 + k, [(1, TL)]), k == 0, k == 30) for k in range(31)],
                               [f"dgr{di}", "u_pre"], [pk])
                            act(u_c.v(m * TL, [(1, TL)]), PBX.v(po, [(1, TL)]), AF.Identity, [pk, "fmv"],
                                [uck], bias=fmv.v(O_BCF + m, [(1, 1)]), scale=0.5)
                            act(usq.v(m * TL, [(1, TL)]), PBX.v(po, [(1, TL)]), AF.Square, [pk, "fmv"], ["usq"],
                                bias=fmv.v(O_BCF + m, [(1, 1)]), scale=0.5)
                            yield
                        mm([(PBX.v(0, [(1, TL)]), cst_b.v(ONES, [(1, 128)]), u_c.v(m * TL, [(1, TL)]), m == 0, m == 7)
                            for m in range(8)], [uck, "cst_b"], ["PB0"])
                        mm([(PBX.v(512, [(1, TL)]), cst_b.v(ONES, [(1, 128)]), usq.v(m * TL, [(1, TL)]), m == 0, m == 7)
                            for m in range(8)], ["usq", "cst_b"], ["PB1"])
                        act(lnA.v(), PBX.v(0, [(1, TL)]), AF.Identity, ["PB0"], ["lnA"], scale=1.0 / 1024.0)
                        act(lnB.v(), PBX.v(512, [(1, TL)]), AF.Identity, ["PB1"], ["lnB"], scale=1.0 / 1024.0)
                        tt("dve", lnT.v(), lnA.v(), lnA.v(), ALU.mult, ["lnA"], ["lnT"])
                        tt("dve", lnB.v(), lnB.v(), lnT.v(), ALU.subtract, ["lnB", "lnT"], ["lnB"])
                        act(lnB.v(), lnB.v(), AF.Ln, ["lnB"], ["lnB"], bias=EPS, scale=1.0)
                        act(lnB.v(), lnB.v(), AF.Exp, ["lnB"], ["lnB"], scale=-0.5)
                        yield
                        for m in range(8):
                            tt("dve", lnT.v(), u_c.v(m * TL, [(1, TL)]), lnA.v(), ALU.subtract,
                               [uck, "lnA"], ["lnT"])
                            tt("dve", lnT.v(), lnT.v(), lnB.v(), ALU.mult, ["lnT", "lnB"], ["lnT"])
                            silu_psum(u_c.v(m * TL, [(1, TL)]), lnT.v(), TL, stTs[m % 2], stVs[m % 2],
                                      fmvh.v(O_LNB + m, [(1, 1)]), fmvh.v(O_LNG + m, [(1, 1)]), ["lnT", "fmvh"], uck,
                                      (f"stT{m % 2}", f"stV{m % 2}"))
                            if m % 2 == 1:
                                yield
                        dma(sq["oin"][8:16, :, T * TL:(T + 1) * TL].rearrange("k p t -> p k t"),
                            u_c.v(0, [(TL, 8), (1, TL)]), f"uost{bi}", reads=[uck], writes=["oin_%d" % s], eng="act")
                        yield

                def prep_stage(sq, T, n):
                    bi = n % 2
                    for cl in range(NCL):
                        yield from prep_chunk(hin[bi], f"hin{bi}", xbc_cs[n % 3], f"xbc_c{n % 3}", bi, cl)

                def state_update(bi, cl, d):
                    dts = H_dts[bi][cl]
                    kd = f"Hdts{bi}{cl}"
                    tt("dve", S_run.v(0, [(64, 16), (1, 64)]), S_run.v(0, [(64, 16), (1, 64)]),
                       dts.v(E1 + 32 + 16 * d, [(1, 16), (0, 64)]), ALU.mult, ["S_run", kd], ["S_run"])
                    for g in range(2):
                        mm([(PC.v(512, [(1, 512)]), REC[bi][cl].v(3072 + g * 128, [(1, 128)]),
                             H_xw[bi][cl].v(g * 512, [(1, 512)]), True, True)],
                           [f"REC{bi}{cl}.B", f"Hxw{bi}{cl}"], ["PC1"])
                        tt("dve", S_run.v(g * 512, [(1, 512)]), S_run.v(g * 512, [(1, 512)]), PC.v(512, [(1, 512)]),
                           ALU.add, ["S_run", "PC1"], ["S_run"])

                def init_state(sq, src):
                    if sq["s"] == 0:
                        dma(F1.v(0, [(128, 8), (1, 128)]), src.rearrange("(k q) n -> q k n", q=128), "ld0",
                            writes=["F1"])
                        for hf in range(2):
                            tr([(PC.v(512 + k * 128, [(1, 128)]), F1.v((hf * 4 + k) * 128, [(1, 128)]), ident_f)
                                for k in range(4)], ["F1", "cst_f"], ["PC1"])
                            act(S_run.v(hf * 512, [(1, 512)]), PC.v(512, [(1, 512)]), AF.Copy, ["PC1"], ["S_run"])
                    else:
                        P.op("dve", lambda e: e.memset(S_run.v(), 0.0), writes=["S_run"])

                def final_state(sq, dst):
                    for hf in range(2):
                        tr([(PC.v(512 + k * 128, [(1, 128)]), S_run.v((hf * 4 + k) * 128, [(1, 128)]), ident_f)
                            for k in range(4)], ["S_run", "cst_f"], ["PC1"])
                        act(F2.v(hf * 512, [(1, 512)]), PC.v(512, [(1, 512)]), AF.Copy, ["PC1"], ["F2"])
                    dma(dst[sq["s"] - 1].rearrange("(k q) n -> q k n", q=128), F2.v(0, [(128, 8), (1, 128)]),
                        "nsst", reads=["F2"], writes=["ns_out"], is_out=True, eng="act")

                def st_stage(sq, T, n, first, last):
                    s = sq["s"]
                    bi = n % 2
                    xbc_c = xbc_cs[n % 3]
                    xck = f"xbc_c{n % 3}"
                    if first:
                        init_state(sq, st_b)
                        yield
                    for cl in range(NCL - 1, -1, -1):
                        cg = T * NCL + cl
                        dts = H_dts[bi][cl]
                        kd = f"Hdts{bi}{cl}"
                        rec = REC[bi][cl]
                        rk = f"REC{bi}{cl}"
                        act(Sbf[0].v(), S_run.v(), AF.Copy, ["S_run"], ["Sbf0"])
                        for g in range(2):
                            mm([(PC.v(512, [(1, 512)]), xbc_c.v((10 + g) * TL + cl * 128, [(1, 128)]),
                                 Sbf[0].v(g * 512, [(1, 512)]), True, True)], [xck, "Sbf0"], ["PC1"])
                            tt("dve", F2.v(g * 512, [(64, 8), (1, 64)]), PC.v(512, [(64, 8), (1, 64)]),
                               dts.v(E1 + 16 + 8 * g, [(1, 8), (0, 64)]), ALU.mult, ["PC1", kd], ["F2"])
                        tt("dve", rec.v(0, [(1, 1024)]), F2.v(), H_yd[bi][cl].v(), ALU.add, ["F2", f"Hyd{bi}{cl}"],
                           [rk + ".y"])
                        yield
                        dma(sq["rec"][cg], rec.v(), f"recst{bi}{cl}",
                            reads=[rk + ".y", rk + ".sz", rk + ".xw", rk + ".B"], writes=["rec_%d" % s], eng="act")
                        dma(sq["e1"][cg], dts.v(E1, [(1, 64)]), f"e1st{bi}{cl}", reads=[kd], writes=["e1_%d" % s],
                            eng="act")
                        dma(sq["cd"][cg].rearrange("p (g t) -> p g t", g=2),
                            xbc_c.v(10 * TL + cl * 128, [(TL, 2), (1, 128)]), f"cdst{bi}{cl}", reads=[xck],
                            writes=["cd_%d" % s], eng="act")
                        state_update(bi, cl, 1)
                        yield
                    if last and s > 0:
                        final_state(sq, ns_b)
                        yield

                work = []
                for sq in seqs:
                    nt = sq["nt"]
                    tiles = list(range(nt - 1, -1, -1))
                    for i, T in enumerate(tiles):
                        work.append((sq, T, i == 0, i == nt - 1))
                for n in range(len(work) + 2):
                    gens = []
                    if n < len(work):
                        gens.append(fm_stage(work[n][0], work[n][1], n))
                    if 0 <= n - 1 < len(work):
                        w1 = work[n - 1]
                        gens.append(prep_stage(w1[0], w1[1], n - 1))
                    if 0 <= n - 2 < len(work):
                        w2 = work[n - 2]
                        gens.append(st_stage(w2[0], w2[1], n - 2, w2[2], w2[3]))
                    interleave(gens)
                P.flush(win=0.6)

        mixer_phase(True)

        s3 = ExitStack()
        RSTD_ON_ACT[0] = True
        if True:
            PY = psum_set(s3, [([128, 512], F32)] * 2)
            PS = PY
            fPTs = psum_set(s3, [([128, 1024], BF16)] * 1)
            NR = 3
            RECs = [sbt(s3, f"fREC{i}", [128, 3328], BF16) for i in range(NR)]
            Cbs = [sbt(s3, f"fC{i}", [128, 256], BF16) for i in range(NR)]
            E1s = [sbt(s3, f"fE{i}", [128, 64], F32) for i in range(NR)]
            S_run = sbt(s3, "fS", [128, D], F32)
            Sbfs = [sbt(s3, f"fSbf{i}", [128, D], BF16) for i in range(2)]
            Fs = [sbt(s3, f"fF{i}", [128, D], F32) for i in range(2)]
            jk = sbt(s3, "fjk", [128, D], BF16)
            yns = [sbt(s3, f"fyn{i}", [128, D], BF16) for i in range(2)]
            ysTs = [sbt(s3, f"fysT{i}", [128, D], BF16) for i in range(2)]
            ssdw = sbt(s3, "fssdw", [128, D], F32)
            load_row(ssdw, R_SSD, D, "ld0", "fssdw")
            it = 0
            for sq in seqs:
                s = sq["s"]
                if s == 0:
                    dma(Fs[0].v(0, [(128, 8), (1, 128)]), st_f.rearrange("(k q) n -> q k n", q=128), "ld0",
                        writes=["fF0"])
                    for hf in range(2):
                        tr([(PY[hf].v(k * 128, [(1, 128)]), Fs[0].v((hf * 4 + k) * 128, [(1, 128)]), ident_f)
                            for k in range(4)], ["fF0", "cst_f"], [f"PY{hf}"])
                        act(S_run.v(hf * 512, [(1, 512)]), PY[hf].v(), AF.Copy, [f"PY{hf}"], ["fS"])
                else:
                    P.op("dve", lambda e: e.memset(S_run.v(), 0.0), writes=["fS"])
                for cg in range(sq["nch"]):
                    i3 = it % NR
                    i2 = it % 2
                    it += 1
                    rec, rk = RECs[i3], f"fREC{i3}"
                    dma(rec.v(), sq["rec"][cg], f"frl{i3}", reads=["rec_%d" % s], writes=[rk])
                    dma(E1s[i3].v(), sq["e1"][cg], f"fel{i3}", reads=["e1_%d" % s], writes=[f"fE{i3}"])
                    dma(Cbs[i3].v(), sq["cd"][cg], f"fcl{i3}", reads=["cd_%d" % s], writes=[f"fC{i3}"])
                    act(Sbfs[i2].v(), S_run.v(), AF.Copy, ["fS"], [f"fSbf{i2}"])
                    F = Fs[i2]
                    fk = f"fF{i2}"
                    for g in range(2):
                        mm([(PY[g].v(), Cbs[i3].v(g * 128, [(1, 128)]), Sbfs[i2].v(g * 512, [(1, 512)]), True, True)],
                           [f"fC{i3}", f"fSbf{i2}"], [f"PY{g}"])
                        tt("dve", F.v(g * 512, [(64, 8), (1, 64)]), PY[g].v(0, [(64, 8), (1, 64)]),
                           E1s[i3].v(8 * g, [(1, 8), (0, 64)]), ALU.mult, [f"PY{g}", f"fE{i3}"], [fk])
                    tt("dve", F.v(), F.v(), rec.v(0, [(1, 1024)]), ALU.add, [fk, rk], [fk])
                    tt("dve", F.v(), F.v(), rec.v(1024, [(1, 1024)]), ALU.mult, [fk, rk], [fk])
                    act(jk.v(0, [(1, 1024)]), F.v(), AF.Square, [fk], ["fjk", "sm%d" % (32 + 4 * i2)],
                        scale=1.0 / 32.0, accum=sm.v(32 + 4 * i2, [(1, 1)]))
                    rstd_from_ms(32 + 4 * i2, "sm%d" % (32 + 4 * i2))
                    stt("dve", yns[i2].v(0, [(1, 1024)]), F.v(), sm.v(33 + 4 * i2, [(1, 1)]), ssdw.v(), ALU.mult,
                        ALU.mult, [fk, "sm%d" % (33 + 4 * i2), "fssdw"], [f"fyn{i2}"])
                    tr([(fPTs[0].v(k * 128, [(1, 128)]), yns[i2].v(k * 128, [(1, 128)]), ident_b) for k in range(8)],
                       [f"fyn{i2}", "cst_b"], ["fPT0"])
                    act(ysTs[i2].v(0, [(1, 1024)]), fPTs[0].v(), AF.Copy, ["fPT0"], [f"fysT{i2}"])
                    dma(sq["oin"][0:8, :, cg * 128:(cg + 1) * 128].rearrange("k p t -> p k t"),
                        ysTs[i2].v(0, [(128, 8), (1, 128)]), f"yst{i2}", reads=[f"fysT{i2}"],
                        writes=["oinY_%d_%d" % (s, cg)], eng="act")
                    tt("dve", S_run.v(0, [(64, 16), (1, 64)]), S_run.v(0, [(64, 16), (1, 64)]),
                       E1s[i3].v(32, [(1, 16), (0, 64)]), ALU.mult, ["fS", f"fE{i3}"], ["fS"])
                    for g in range(2):
                        mm([(PS[g].v(), rec.v(3072 + g * 128, [(1, 128)]), rec.v(2048 + g * 512, [(1, 512)]),
                             True, True)], [rk], [f"PY{g}"])
                        tt("dve", S_run.v(g * 512, [(1, 512)]), S_run.v(g * 512, [(1, 512)]), PS[g].v(), ALU.add,
                           ["fS", f"PY{g}"], ["fS"])
                if s > 0:
                    for hf in range(2):
                        tr([(PY[hf].v(k * 128, [(1, 128)]), S_run.v((hf * 4 + k) * 128, [(1, 128)]), ident_f)
                            for k in range(4)], ["fS", "cst_f"], [f"PY{hf}"])
                        act(Fs[0].v(hf * 512, [(1, 512)]), PY[hf].v(), AF.Copy, [f"PY{hf}"], ["fF0"])
                    dma(ns_f[s - 1].rearrange("(k q) n -> q k n", q=128), Fs[0].v(0, [(128, 8), (1, 128)]),
                        "nsst", reads=["fF0"], writes=["ns_out"], is_out=True, eng="act")

        with ExitStack() as s4:
            PMs = psum_set(s4, [([128, 1024], F32)] * 2)
            PTs = psum_set(s4, [([128, 1024], BF16)] * 1)
            Wo = sbt(s4, "Wo", [128, 16 * D], BF16)
            G1m = [sbt(s4, f"G1_{i}", [128, D], F32) for i in range(2)]
            G2m = [sbt(s4, f"G2_{i}", [128, D], F32) for i in range(2)]
            SH2m = [sbt(s4, f"SH2_{i}", [128, D], F32) for i in range(2)]
            m4work = []
            wr1 = sbt(s4, "wr1", [128, D], F32)
            wr2 = sbt(s4, "wr2", [128, D], F32)
            NB = 2
            oc = [sbt(s4, f"oc{i}", [128, 16 * 128], BF16) for i in range(NB)]
            xs_ = [sbt(s4, f"x4_{i}", [128, D], F32) for i in range(NB)]
            X1 = [sbt(s4, f"X1_{i}", [128, D], F32) for i in range(NB)]
            tFs = [sbt(s4, f"tF4_{i}", [128, D], F32) for i in range(2)]
            hns = [sbt(s4, f"hn4_{i}", [128, D], BF16) for i in range(2)]
            jks = [sbt(s4, f"jk4_{i}", [128, D], BF16) for i in range(2)]
            hTs = [sbt(s4, f"hT4_{i}", [128, D], BF16) for i in range(NB)]
            for k in range(16):
                dma(Wo.v(k * D, [(1, D)]), w_out[k * 128:(k + 1) * 128, :], f"wcast{k % 2}", writes=["Wo"], eng="pool")
            load_row(wr1, R_POST, D, "ld0", "wr1")
            load_row(wr2, R_FPRE, D, "ld1", "wr2")
            for cnd in (1, 0):
                load_mod(G1m[cnd], cnd, 2, "ld0", "G1_%d" % cnd)
                tt("dve", G1m[cnd].v(), G1m[cnd].v(), wr1.v(), ALU.mult, ["G1_%d" % cnd, "wr1"], ["G1_%d" % cnd])
                load_mod(G2m[cnd], cnd, 4, "ld1", "G2_%d" % cnd)
                stt("dve", G2m[cnd].v(), G2m[cnd].v(), 1.0, wr2.v(), ALU.add, ALU.mult, ["G2_%d" % cnd, "wr2"],
                    ["G2_%d" % cnd])
                load_mod(SH2m[cnd], cnd, 3, "ld0", "SH2_%d" % cnd)
            for sq in seqs:
                s = sq["s"]
                if s == 0:
                    for a in (0, 65 * 64):
                        dma(sq["hT2"][:, :, a:a + 64].rearrange("k p t -> p k t"), zero_b.v(0, [(64, 8), (1, 64)]),
                            "zst", reads=["zero_b"], writes=["hT2_0"])
                for c in range(sq["nch"]):
                    m4work.append((sq, c))

            def m4_A(sq, c, it):
                s = sq["s"]
                i3 = it % NB
                i2 = it % 2
                PM = PMs[i2]
                pmk = [f"PM{i2}a", f"PM{i2}b"]
                r0 = sq["base"] + c * 128
                dma(oc[i3].v(0, [(128, 16), (1, 128)]),
                    sq["oin"][:, :, c * 128:(c + 1) * 128].rearrange("k p t -> p k t"), f"ol{i3}",
                    reads=["oin_%d" % s, "oinY_%d_%d" % (s, c)], writes=[f"oc{i3}"])
                dma(xs_[i3].v(), x_all[r0:r0 + 128, :], f"xl{i3}", writes=[f"x4_{i3}"])
                mm([(PM.v(hf * 512, [(1, 512)]), oc[i3].v(k * 128, [(1, 128)]), Wo.v(k * D + hf * 512, [(1, 512)]),
                     k == 0, k == 15) for hf in range(2) for k in range(16)], [f"oc{i3}", "Wo"], pmk)

            def m4_B(sq, c, it):
                s = sq["s"]
                cnd = sq["cond"]
                i3 = it % NB
                i2 = it % 2
                PM = PMs[i2]
                pmk = [f"PM{i2}a", f"PM{i2}b"]
                r0 = sq["base"] + c * 128
                sc = 16 + 8 * i2
                act(jks[i2].v(0, [(1, 1024)]), PM.v(), AF.Square, pmk, [f"jk4_{i2}", "sm%d" % sc], scale=1.0 / 32.0,
                    accum=sm.v(sc, [(1, 1)]))
                rstd_from_ms(sc, "sm%d" % sc)
                x1 = X1[i3]
                stt("dve", x1.v(), PM.v(), sm.v(sc + 1, [(1, 1)]), G1m[cnd].v(), ALU.mult, ALU.mult,
                    pmk + ["sm%d" % (sc + 1), "G1_%d" % cnd], [f"X1_{i3}"])
                tt("dve", x1.v(), x1.v(), xs_[i3].v(), ALU.add, [f"X1_{i3}", f"x4_{i3}"], [f"X1_{i3}"])
                dma(x1_d[r0:r0 + 128, :], x1.v(), f"x1st{i3}", reads=[f"X1_{i3}"], writes=["x1_d_%d" % r0], eng="act")
                norm_mod_transpose(x1, f"X1_{i3}", G2m[cnd], "G2_%d" % cnd, SH2m[cnd], "SH2_%d" % cnd, tFs[i2],
                                   f"tF4_{i2}", hns[i2], f"hn4_{i2}", jks[i2], f"jk4_{i2}", hTs[i3], f"hT4_{i3}",
                                   sc + 4, PTs[0], "PT0")
                o2 = sq["h2off"] + c * 128
                dma(sq["hT2"][:, :, o2:o2 + 128].rearrange("k p t -> p k t"), hTs[i3].v(0, [(128, 8), (1, 128)]),
                    f"hst{i3}", reads=[f"hT4_{i3}"], writes=["hT2_%d" % s], eng="act")

            m4_A(m4work[0][0], m4work[0][1], 0)
            for i, (sq, c) in enumerate(m4work):
                if i + 1 < len(m4work):
                    m4_A(m4work[i + 1][0], m4work[i + 1][1], i + 1)
                m4_B(sq, c, i)
            P.flush(win=1.2)
        s3.close()
        RSTD_ON_ACT[0] = False

        with ExitStack() as s5:
            PA, PB, PC, PF = psum_set(s5, [([128, 1024], F32)] * 4)
            Wd = sbt(s5, "Wd", [128, 22 * D], BF16)
            G3 = sbt(s5, "G3", [128, D], F32)
            wr3 = sbt(s5, "wr3", [128, D], F32)
            hin2 = [sbt(s5, f"h2_{i}", [128, 8 * 576], BF16) for i in range(2)]
            NWU = 4
            wur = [sbt(s5, f"wur{i}", [128, 1024], BF16) for i in range(NWU)]
            dgf = [sbt(s5, f"dgf{i}", [128, 18 * 128], BF16) for i in range(2)]
            PgL = sbt(s5, "PgL", [128, 10 * 66], BF16)
            PvL = sbt(s5, "PvL", [128, 10 * 66], BF16)
            PgC = sbt(s5, "PgC", [128, 258], BF16)
            PvC = sbt(s5, "PvC", [128, 258], BF16)
            SV = sbt(s5, "SV", [128, 44 * 132], BF16)
            P.op("dve", lambda e: e.memset(SV.v(), 0.0), writes=["SV%d_%d" % (g_, j_) for g_ in range(2) for j_ in range(22)])
            sgl = sbt(s5, "sgl", [128, 512], F32)
            sgT = sbt(s5, "sgT", [128, 512], F32)
            sgV = sbt(s5, "sgV", [128, 512], F32)
            actTs = [sbt(s5, f"actT{i}", [128, 22 * 512], BF16) for i in range(2)]
            x1t = [sbt(s5, f"x1t{i}", [128, D], F32) for i in range(2)]
            Y = [sbt(s5, f"Y{i}", [128, D], F32) for i in range(1)]
            parts = [sbt(s5, f"part{i}", [128, 512], F32) for i in range(2)]
            for k in range(22):
                dma(Wd.v(k * D, [(1, D)]), w_down[k * 128:(k + 1) * 128, :], f"wcast{k % 2}", writes=["Wd"], eng="pool")
            for (b_, kk) in ((PgL, "Pg"), (PvL, "Pv"), (PgC, "Pg"), (PvC, "Pv")):
                P.op("dve", (lambda bb: (lambda e: e.memset(bb.v(), 0.0)))(b_), writes=[kk])
            load_row(wr3, R_FPOST, D, "ld0", "wr3")
            st8 = dict(cw=0, cdg=0, ih=0, iy=0, ia=0)

            def ffn_tile(sq, T):
                s = sq["s"]
                lat = (s == 0)
                NP = 576 if lat else 256
                TF = 512 if lat else 256
                taps = [(dr, dc) for dr in range(3) for dc in range(3)] if lat else [(1, dc) for dc in range(3)]
                nt_ = len(taps)
                Pbufs = (PgL, PvL) if lat else (PgC, PvC)
                hb = hin2[st8["ih"] % 2]
                hk = f"h2_{st8['ih'] % 2}"
                if lat:
                    NW = 576 if T == 0 else 512
                    t0_ = 64 if T == 0 else T * 512 + 128
                    dma(hb.v(0, [(NP, 8), (1, NW)]),
                        sq["hT2"][:, :, t0_:t0_ + NW].rearrange("k p t -> p k t"), f"hl{st8['ih'] % 2}",
                        reads=["hT2_%d" % s], writes=[hk])
                else:
                    dma(hb.v(0, [(NP, 8), (1, NP)]),
                        sq["hT2"][:, :, T * 512:T * 512 + NP].rearrange("k p t -> p k t"), f"hl{st8['ih'] % 2}",
                        reads=["hT2_%d" % s], writes=[hk])
                st8["ih"] += 1
                actT = actTs[st8["ia"] % 2]
                ak = f"actT{st8['ia'] % 2}"
                st8["ia"] += 1
                dgl = {}

                def U(j, gv):
                    if gv == 0:
                        dg = dgf[st8["cdg"] % 2]
                        dgk = f"dgf{st8['cdg'] % 2}"
                        st8["cdg"] += 1
                        tap0 = taps[0][0] * 3 + taps[0][1]
                        tt("dve", dg.v(0, [(nt_ * 128, 2), (128, nt_), (1, 128)]),
                           cst_b.v(IDF, [(0, 2), (0, nt_), (1, 128)]),
                           fmv.v(O_WFC + j * 9 + tap0, [(22 * 9, 2), (1, nt_), (0, 128)]), ALU.mult,
                           ["cst_b", "fmv"], [dgk])
                        dgl[j] = (dg, dgk)
                    Pbuf = Pbufs[gv]
                    pk = "Pg" if gv == 0 else "Pv"
                    PS, psk = (PA, "PA") if gv == 0 else (PB, "PB")
                    w = wur[st8["cw"] % NWU]
                    wk = f"wur{st8['cw'] % NWU}"
                    dma(w.v(), w_up_t[j + 22 * gv].rearrange("p k c -> p (k c)"), f"wr{st8['cw'] % NWU}",
                        reads=["w_up_t_%d" % k_ for k_ in range(8)], writes=[wk])
                    st8["cw"] += 1
                    if lat:
                        svk = "SV%d_%d" % (gv, j)
                        svo = (gv * 22 + j) * 132
                        act(Pbuf.v(0, [(1, 132)]), SV.v(svo, [(1, 132)]), AF.Copy, [svk], [pk])
                        if T == 0:
                            mm([(PS.v(0, [(1, 512)]), w.v(k * 128, [(1, 128)]), hb.v(k * NP, [(1, 512)]),
                                 k == 0, k == 7) for k in range(8)] +
                               [(PS.v(512, [(1, 64)]), w.v(k * 128, [(1, 128)]), hb.v(k * NP + 512, [(1, 64)]),
                                 k == 0, k == 7) for k in range(8)], [wk, hk], [psk + "0", psk + "1"])
                            act(Pbuf.v(1 * 66 + 1, [(66, 8), (1, 64)]), PS.v(0, [(64, 8), (1, 64)]),
                                AF.Copy, [psk + "0"], [pk])
                            act(Pbuf.v(9 * 66 + 1, [(1, 64)]), PS.v(512, [(1, 64)]), AF.Copy, [psk + "1"], [pk])
                        else:
                            mm([(PS.v(0, [(1, 512)]), w.v(k * 128, [(1, 128)]), hb.v(k * NP, [(1, 512)]),
                                 k == 0, k == 7) for k in range(8)], [wk, hk], [psk + "0"])
                            act(Pbuf.v(2 * 66 + 1, [(66, 8), (1, 64)]), PS.v(0, [(64, 8), (1, 64)]),
                                AF.Copy, [psk + "0"], [pk])
                        if T < 7:
                            act(SV.v(svo, [(1, 132)]), Pbuf.v(8 * 66, [(1, 132)]), AF.Copy, [pk], [svk])
                    else:
                        mm([(PS.v(0, [(1, 256)]), w.v(k * 128, [(1, 128)]), hb.v(k * NP, [(1, 256)]),
                             k == 0, k == 7) for k in range(8)], [wk, hk], [psk + "0"])
                        act(Pbuf.v(1, [(1, 256)]), PS.v(0, [(1, 256)]), AF.Copy, [psk + "0"], [pk])

                def C(j, gv):
                    dg, dgk = dgl[j]
                    Pbuf = Pbufs[gv]
                    pk = "Pg" if gv == 0 else "Pv"
                    specs = []
                    for ti, (dr, dc) in enumerate(taps):
                        if lat:
                            rv = Pbuf.v(dr * 66 + dc, [(66, 8), (1, 64)])
                            ov = PC.v(gv * 512, [(64, 8), (1, 64)])
                        else:
                            rv = Pbuf.v(dc, [(1, 256)])
                            ov = PC.v(gv * 512, [(1, 256)])
                        specs.append((ov, dg.v((gv * nt_ + ti) * 128, [(1, 128)]), rv, ti == 0, ti == nt_ - 1))
                    npe = nt_ - N_DVE_TAPS if lat else nt_
                    specs = [(o_, l_, r_, ti == 0, ti == npe - 1) for ti, (o_, l_, r_, _a, _b) in enumerate(specs[:npe])]
                    mm(specs, [dgk, pk], ["PC%d" % gv])
                    if lat and N_DVE_TAPS > 0:
                        prt = parts[gv]
                        prk = "part%d" % gv
                        wof = O_WFC + (j + 22 * gv) * 9
                        for q, ti in enumerate(range(npe, nt_)):
                            dr, dc = taps[ti]
                            rv = Pbuf.v(dr * 66 + dc, [(66, 8), (1, 64)])
                            if q == 0:
                                ts("dve", prt.v(0, [(64, 8), (1, 64)]), rv, fmv.v(wof + ti, [(1, 1)]), None,
                                   ALU.mult, None, [pk, "fmv"], [prk])
                            else:
                                stt("dve", prt.v(0, [(64, 8), (1, 64)]), rv, fmv.v(wof + ti, [(1, 1)]),
                                    prt.v(0, [(64, 8), (1, 64)]), ALU.mult, ALU.add, [pk, "fmv", prk], [prk])
                        if gv == 0:
                            tt("dve", prt.v(), PC.v(0, [(1, TF)]), prt.v(), ALU.add, ["PC0", prk], [prk])
                            silu_psum(sgl.v(0, [(1, TF)]), prt.v(), TF, sgT, sgV,
                                      fmvh.v(O_BFC + j, [(1, 1)]), 0.5, [prk, "fmvh"], "sgl", ("sgT", "sgV"))
                        else:
                            stt("dve", prt.v(), PC.v(512, [(1, TF)]), fmv.v(O_BFC + 22 + j, [(1, 1)]),
                                prt.v(), ALU.add, ALU.add, ["PC1", "fmv", prk], [prk])
                            tt("dve", actT.v(j * 512, [(1, TF)]), prt.v(), sgl.v(0, [(1, TF)]), ALU.mult,
                               [prk, "sgl"], [ak])
                    elif gv == 0:
                        silu_psum(sgl.v(0, [(1, TF)]), PC.v(0, [(1, TF)]), TF, sgT, sgV,
                                  fmvh.v(O_BFC + j, [(1, 1)]), 0.5, ["PC0", "fmvh"], "sgl", ("sgT", "sgV"))
                    else:
                        stt("dve", actT.v(j * 512, [(1, TF)]), PC.v(512, [(1, TF)]), fmv.v(O_BFC + 22 + j, [(1, 1)]),
                            sgl.v(0, [(1, TF)]), ALU.add, ALU.mult, ["PC1", "fmv", "sgl"], [ak])

                U(0, 0)
                U(0, 1)
                for j in range(22):
                    C(j, 0)
                    if j + 1 < 22:
                        U(j + 1, 0)
                    C(j, 1)
                    if j + 1 < 22:
                        U(j + 1, 1)
                for c in range(TF // 128):
                    i2 = st8["iy"] % 2
                    st8["iy"] += 1
                    r0 = sq["base"] + T * 512 + c * 128
                    dma(x1t[i2].v(), x1_d[r0:r0 + 128, :], f"xl{i2}", reads=["x1_d_%d" % r0], writes=[f"x1t{i2}"])
                    mm([(PF.v(hf * 512, [(1, 512)]), actT.v(j * 512 + c * 128, [(1, 128)]),
                         Wd.v(j * D + hf * 512, [(1, 512)]), j == 0, j == 21)
                        for hf in range(2) for j in range(22)], [ak, "Wd"], ["PF0", "PF1"])
                    y = Y[0]
                    act(y.v(), PF.v(), AF.Square, ["PF0", "PF1"], ["Y0", "sm12"], scale=1.0 / 32.0,
                        accum=sm.v(12, [(1, 1)]))
                    rstd_from_ms(12, "sm12")
                    stt("dve", y.v(), PF.v(), sm.v(13, [(1, 1)]), G3.v(), ALU.mult, ALU.mult,
                        ["PF0", "PF1", "sm13", "G3"], ["Y0"])
                    tt("dve", y.v(), y.v(), x1t[i2].v(), ALU.add, ["Y0", f"x1t{i2}"], ["Y0"])
                    dma(y_all[r0:r0 + 128, :], y.v(), f"yst{i2}", reads=["Y0"], writes=["y_all_%d" % r0], is_out=True, eng="act")

            last_cond = None
            for sq in seqs:
                if sq["cond"] != last_cond:
                    last_cond = sq["cond"]
                    load_mod(G3, sq["cond"], 5, "ld1", "G3")
                    tt("dve", G3.v(), G3.v(), wr3.v(), ALU.mult, ["G3", "wr3"], ["G3"])
                for T in range(8 if sq["s"] == 0 else 1):
                    ffn_tile(sq, T)
            P.flush(final=True, win=0.0)
    return nc


_CACHE = {}


def _consts():
    t = np.arange(128)
    ident = np.eye(128, dtype=np.float32)
    uinc = (t[:, None] <= t[None, :]).astype(np.float32)
    linc = (t[:, None] >= t[None, :]).astype(np.float32)
    lstr = (t[:, None] > t[None, :]).astype(np.float32)
    ustr = (t[:, None] < t[None, :]).astype(np.float32)
    ones = np.ones((128, 128), np.float32)
    return np.ascontiguousarray(np.concatenate([ident, uinc, linc, lstr, ustr, ones], axis=1))


def kernel(x_prompt, x_sample, state_ssd_fwd, state_ssd_bwd, c, c_ctx,
           w_ada, b_ada, norm_mix_pre, norm_mix_post, w_in, w_ssd_conv, b_ssd_conv,
           a_log_fwd, a_log_bwd, dt_bias_fwd, dt_bias_bwd, d_skip, ssd_norm,
           w_cf_conv, b_cf_conv, cf_ln_g, cf_ln_b, w_out, norm_ffn_pre, norm_ffn_post,
           w_ffn_up, w_ffn_conv, b_ffn_conv, w_ffn_down):
    f = lambda a: np.ascontiguousarray(np.asarray(a, dtype=np.float32))
    x_prompt, x_sample = f(x_prompt), f(x_sample)
    def fm(v, n):
        return f(v).reshape(n, 128).T
    wsc = f(w_ssd_conv)[0].reshape(3, 12, 128).transpose(2, 1, 0).reshape(128, 36)
    wcf = f(w_cf_conv)[0].reshape(31, 8, 128).transpose(2, 1, 0).reshape(128, 248)
    wfc = f(w_ffn_conv)[0].reshape(9, 44, 128).transpose(2, 1, 0).reshape(128, 396)
    fmv = np.ascontiguousarray(np.concatenate([
        wsc, fm(b_ssd_conv[0], 12), wcf, fm(b_cf_conv[0], 8), fm(cf_ln_g[0], 8), fm(cf_ln_b[0], 8),
        wfc, fm(b_ffn_conv[0], 44)], axis=1).astype(np.float32))
    rowv = np.ascontiguousarray(np.concatenate([
        f(norm_mix_pre)[0], f(norm_mix_post)[0], f(ssd_norm)[0], f(norm_ffn_pre)[0], f(norm_ffn_post)[0],
        f(d_skip)[0], f(dt_bias_fwd)[0], f(dt_bias_bwd)[0], f(a_log_fwd)[0], f(a_log_bwd)[0], f(b_ada)[0]])[None, :])
    cst = _consts()
    if "nc" not in _CACHE:
        _CACHE["nc"] = build_program()
    nc = _CACHE["nc"]
    wa, wi, wo, wu, wd = f(w_ada)[0], f(w_in)[0], f(w_out)[0], f(w_ffn_up)[0], f(w_ffn_down)[0]
    in_maps = []
    for i in range(8):
        xa = np.ascontiguousarray(np.concatenate([x_sample[i], x_prompt[4 * i:4 * i + 4].reshape(NCTX * CTX, D)], axis=0))
        cc = np.stack([f(c_ctx), f(c)[i]], axis=1)
        cin = np.ascontiguousarray(cc.reshape(8, 128, 2).transpose(1, 0, 2).reshape(128, 16))
        in_maps.append(dict(
            x_all=xa, st_f=f(state_ssd_fwd)[i, 0].reshape(1024, 128), st_b=f(state_ssd_bwd)[i, 0].reshape(1024, 128),
            c_in=cin, cst=cst, fmv=fmv, rowv=rowv, w_ada=wa, w_in=wi, w_out=wo, w_up=wu, w_down=wd))
    res = run_bass_kernel_spmd(nc, in_maps, core_ids=list(range(8)))
    y_prompt = np.zeros((32, CTX, D), np.float32)
    y_sample = np.zeros((8, LAT, D), np.float32)
    nsf = np.zeros((32, 1, 16, 64, 128), np.float32)
    nsb = np.zeros((32, 1, 16, 64, 128), np.float32)
    for i in range(8):
        r = res.results[i]
        ya = np.asarray(r["y_all"])
        y_sample[i] = ya[:LAT]
        y_prompt[4 * i:4 * i + 4] = ya[LAT:].reshape(NCTX, CTX, D)
        nsf[4 * i:4 * i + 4, 0] = np.asarray(r["ns_f"]).reshape(NCTX, 16, 64, 128)
        nsb[4 * i:4 * i + 4, 0] = np.asarray(r["ns_b"]).reshape(NCTX, 16, 64, 128)
    return (y_prompt, y_sample, nsf, nsb)
```

```python
import numpy as np
from contextlib import ExitStack
import concourse.bass as bass
import concourse.mybir as mybir
from concourse.bass_utils import run_bass_kernel_spmd

F32 = mybir.dt.float32
BF16 = mybir.dt.bfloat16
ALU = mybir.AluOpType
AF = mybir.ActivationFunctionType

D = 1024
LAT = 4096
CTX = 256
NCTX = 4
NTOK = LAT + NCTX * CTX
TL = 256
HALO = 16
EPS = 1e-6
USE_POOL_POW = True
RSTD_ON_ACT = [False]
SCHED_WIN = 0.6
DIAG31_ALL_DMA = True
CONV3_ON_DVE = True
N_DVE_TAPS = 1
PREP_VEC_ENG = "dve"
VEC_EXCL = True
PSUM_CANON = {"PDdt": "PD", "PDcum": "PD", "PDa": "PD"}
PSUM_KEYS = {"PD", "PB0", "PB1", "PA0", "PA1", "PC0", "PC1", "PT", "PT0", "PT1", "PT2", "PT3",
             "PM0a", "PM0b", "PM1a", "PM1b", "PF0", "PF1", "PY0", "PY1", "PS0", "PS1", "fPT0", "fPT1"}
NH = 16
IN_COLS = 4640
DFF = 2816
O_WSC, O_BSC, O_WCF, O_BCF, O_LNG, O_LNB, O_WFC, O_BFC, NFMV = 0, 36, 48, 296, 304, 312, 320, 716, 760
R_PRE, R_POST, R_SSD, R_FPRE, R_FPOST, R_DSK, R_DTB, R_ALOG, R_BADA, NROW = (
    0, 1024, 2048, 3072, 4096, 5120, 5136, 5168, 5200, 5200 + 6144)


class Tl:
    def __init__(self, h, shape):
        self.h = h
        self.P = shape[0]
        self.F = int(np.prod(shape[1:]))

    def v(self, off=0, dims=None, p0=0, np_=None):
        if dims is None:
            dims = [(1, self.F - off)]
        if np_ is None:
            np_ = self.P - p0
        return bass.AP(self.h, p0 * self.F + off, [[self.F, np_]] + [[s, n] for (s, n) in dims])


class Prog:
    def __init__(self, nc, es):
        self.nc = nc
        self.es = es
        self.ops = []
        self.key_w = {}
        self.key_r = {}
        self.eng_sem = {}
        for e in ("pe", "act", "dve", "pool"):
            self.eng_sem[e] = es.enter_context(nc.semaphore("sem_" + e))
        self.eng_cnt = {e: 0 for e in self.eng_sem}
        self.streams = {}
        self.known = {e: {} for e in ("pe", "act", "dve", "pool", "sp")}
        self.emitted = 0
        self.out_streams = set()

    def op(self, eng, fn, reads=(), writes=(), dma=None, is_out=False, cost=0.3, lat=2.5):
        reads = [PSUM_CANON.get(k, k) for k in reads]
        writes = [PSUM_CANON.get(k, k) for k in writes]
        writes = writes + [k for k in reads if k in PSUM_KEYS and k not in writes]
        idx = len(self.ops)
        deps = set()
        for k in reads:
            if k in self.key_w:
                deps.add(self.key_w[k])
        for k in writes:
            if k in self.key_w:
                deps.add(self.key_w[k])
            for r in self.key_r.get(k, ()):
                deps.add(r)
        rec = dict(eng=eng, fn=fn, deps=deps, dma=dma, sig=None, users=0, cost=cost, lat=lat)
        if dma is not None:
            if dma not in self.streams:
                self.streams[dma] = [self.es.enter_context(self.nc.semaphore("dq_" + dma)), 0, None]
            st = self.streams[dma]
            if st[2] is not None:
                deps.add(st[2])
            st[1] += 16
            st[2] = idx
            rec["sig"] = (st[0], st[1])
            if is_out:
                self.out_streams.add(dma)
        deps.discard(idx)
        self.ops.append(rec)
        for d in deps:
            self.ops[d]["users"] += 1
        for k in reads:
            self.key_r.setdefault(k, []).append(idx)
        for k in writes:
            self.key_w[k] = idx
            self.key_r[k] = []
        return idx

    def flush(self, final=False, win=None):
        nc = self.nc
        ops = self.ops
        lo = self.emitted
        hi = len(ops)
        n = hi - lo
        succ = [[] for _ in range(n)]
        ndep = [0] * n
        for i in range(lo, hi):
            for d in ops[i]["deps"]:
                if d >= lo:
                    succ[d - lo].append(i - lo)
                    ndep[i - lo] += 1
        dur = [0.0] * n
        for i in range(n):
            r = ops[lo + i]
            dur[i] = r["lat"] if r["dma"] is not None else r["cost"]
        blev = [0.0] * n
        for i in range(n - 1, -1, -1):
            b = 0.0
            for s_ in succ[i]:
                if blev[s_] > b:
                    b = blev[s_]
            blev[i] = b + dur[i]
        engs = ("pe", "act", "dve", "pool", "sp")
        free = {e: 0.0 for e in engs}
        ready = {e: [] for e in engs}
        dready = [0.0] * n
        finish = [0.0] * n
        for i in range(n):
            if ndep[i] == 0:
                ready[ops[lo + i]["eng"]].append(i)
        nsched = 0
        order = {e: [] for e in engs}
        WIN = SCHED_WIN if win is None else win
        while nsched < n:
            best = None
            for e in engs:
                rl = ready[e]
                if not rl:
                    continue
                f = free[e]
                if VEC_EXCL and e in ("dve", "pool"):
                    f = max(free["dve"], free["pool"])
                est = min(max(f, dready[i]) for i in rl)
                cand = None
                for i in rl:
                    st_ = max(f, dready[i])
                    if st_ <= est + WIN:
                        key = (-blev[i], i)
                        if cand is None or key < cand[0]:
                            cand = (key, i, st_)
                if best is None or cand[2] < best[2]:
                    best = (e, cand[1], cand[2])
            e, i, st_ = best
            ready[e].remove(i)
            r = ops[lo + i]
            if r["dma"] is not None:
                free[e] = st_ + 0.12
                finish[i] = st_ + r["lat"]
            else:
                free[e] = st_ + r["cost"]
                finish[i] = free[e]
                if VEC_EXCL and e in ("dve", "pool"):
                    free["dve"] = max(free["dve"], free[e])
                    free["pool"] = max(free["pool"], free[e])
            order[e].append(lo + i)
            nsched += 1
            for s_ in succ[i]:
                same_pe = (e == "pe" and ops[lo + s_]["eng"] == "pe" and r["dma"] is None)
                t_ = finish[i] + (0.0 if same_pe else 0.2)
                if t_ > dready[s_]:
                    dready[s_] = t_
                ndep[s_] -= 1
                if ndep[s_] == 0:
                    ready[ops[lo + s_]["eng"]].append(s_)
        self.sim_time = max(finish) if n else 0.0
        print("[sched] block ops=%d simulated_us=%.0f" % (n, self.sim_time))
        per = order
        for e in ("pe", "act", "dve", "pool"):
            for i in per[e]:
                r = ops[i]
                if r["dma"] is None:
                    self.eng_cnt[e] += 1
                    r["sig"] = (self.eng_sem[e], self.eng_cnt[e])
        self.emitted = hi
        prog = self

        def run(e, name):
            known = prog.known[name]
            for i in per[name]:
                r = ops[i]
                need = {}
                for d in r["deps"]:
                    dr = ops[d]
                    if dr["eng"] == "pe" and name == "pe" and dr["dma"] is None:
                        continue
                    sem, val = dr["sig"]
                    key = id(sem)
                    if known.get(key, 0) >= val:
                        continue
                    if key not in need or need[key][1] < val:
                        need[key] = (sem, val)
                for key in sorted(need, key=lambda k_: need[k_][1]):
                    sem, val = need[key]
                    e.wait_ge(sem, val)
                    known[key] = val
                ins = r["fn"](e)
                if r["sig"] is not None:
                    if r["dma"] is not None:
                        ins.then_inc(r["sig"][0], 16)
                    else:
                        ins.then_inc(r["sig"][0], 1)
            if name == "sp":
                for s in prog.streams.values():
                    if s[1] > 0 and known.get(id(s[0]), 0) < s[1]:
                        e.wait_ge(s[0], s[1])
                        known[id(s[0])] = s[1]

        with nc.Block() as block:
            @block.tensor
            def _(e):
                run(e, "pe")

            @block.scalar
            def _(e):
                run(e, "act")

            @block.vector
            def _(e):
                run(e, "dve")

            @block.gpsimd
            def _(e):
                run(e, "pool")

            @block.sync
            def _(e):
                run(e, "sp")


def build_program():
    nc = bass.Bass("TRN2", target_bir_lowering=False)

    def din(name, shape, dt=F32):
        return nc.dram_tensor(name, list(shape), dt, kind="ExternalInput").ap()

    def dout(name, shape, dt=F32):
        return nc.dram_tensor(name, list(shape), dt, kind="ExternalOutput").ap()

    def dscr(name, shape, dt):
        return nc.dram_tensor(name, list(shape), dt).ap()

    x_all = din("x_all", [NTOK, D])
    st_f = din("st_f", [1024, 128])
    st_b = din("st_b", [1024, 128])
    c_in = din("c_in", [128, 16])
    cst = din("cst", [128, 768])
    fmv_d = din("fmv", [128, NFMV])
    rowv = din("rowv", [1, NROW])
    w_ada = din("w_ada", [D, 6 * D])
    w_in = din("w_in", [D, IN_COLS])
    w_out = din("w_out", [2 * D, D])
    w_up = din("w_up", [D, 2 * DFF])
    w_down = din("w_down", [DFF, D])
    y_all = dout("y_all", [NTOK, D])
    ns_f = dout("ns_f", [NCTX, 1024, 128])
    ns_b = dout("ns_b", [NCTX, 1024, 128])

    mod_d = dscr("mod_d", [2, 6 * D], F32)
    w_in_t = dscr("w_in_t", [28, 128, 8, 128], BF16)
    w_up_t = dscr("w_up_t", [44, 128, 8, 128], BF16)
    dg31_d = dscr("dg31_d", [8, 128, 31 * 128], BF16)
    x1_d = dscr("x1_d", [NTOK, D], F32)
    seqs = []
    for s in range(1 + NCTX):
        L = LAT if s == 0 else CTX
        base = 0 if s == 0 else LAT + (s - 1) * CTX
        sq = dict(s=s, L=L, base=base, cond=1 if s == 0 else 0, nt=L // TL, nch=L // 128)
        sq["hT1"] = dscr(f"hT1_{s}", [8, 128, L + 2 * HALO], BF16)
        sq["oin"] = dscr(f"oin_{s}", [16, 128, L], BF16)
        sq["rec"] = dscr(f"rec_{s}", [L // 128, 128, 3328], BF16)
        sq["e1"] = dscr(f"e1_{s}", [L // 128, 128, 64], F32)
        sq["cd"] = dscr(f"cd_{s}", [L // 128, 128, 256], BF16)
        if s == 0:
            sq["hT2"] = dscr(f"hT2_{s}", [8, 128, 66 * 64], BF16)
            sq["h2off"] = 64
        else:
            sq["hT2"] = dscr(f"hT2_{s}", [8, 128, L], BF16)
            sq["h2off"] = 0
        seqs.append(sq)

    with ExitStack() as es:
        P = Prog(nc, es)

        uid = [0]

        def sbt(st, name, shape, dt):
            uid[0] += 1
            return Tl(st.enter_context(nc.sbuf_tensor("sb%d_%s" % (uid[0], name), list(shape), dt)), shape)

        def pst(st, name, shape, dt):
            return Tl(st.enter_context(nc.psum_tensor("ps_" + name, list(shape), dt)), shape)

        pid = [0]

        def psum_set(st, spec):
            out = []
            for (shape, dt) in spec:
                pid[0] += 1
                out.append(pst(st, "p%d" % pid[0], shape, dt))
            return out

        cst_f = sbt(es, "cst_f", [128, 768], F32)
        cst_b = sbt(es, "cst_b", [128, 768], BF16)
        fmv = sbt(es, "fmv", [128, NFMV], F32)
        diag3 = sbt(es, "diag3", [128, 36 * 128], BF16)
        WZ = sbt(es, "WZ", [128, 8 * 1056], BF16)
        zero_b = sbt(es, "zero_b", [128, 1024], BF16)
        a_b = sbt(es, "a_b", [128, 32], F32)
        dtb_b = sbt(es, "dtb_b", [128, 32], F32)
        sm = sbt(es, "sm", [128, 64], F32)
        nhalf = sbt(es, "nhalf", [128, 8], F32)
        fmvh = sbt(es, "fmvh", [128, NFMV], F32)

        IDF, UINC, LINC, LSTR, USTR, ONES = [i * 128 for i in range(6)]

        def fsz(ap):
            n_ = 1
            for (s_, c_) in list(ap.ap)[1:]:
                n_ *= c_
            return n_

        def dma(fn_out, fn_in, stream, reads=(), writes=(), eng="sp", is_out=False):
            def f(e):
                return e.dma_start(out=fn_out, in_=fn_in)
            try:
                nb = list(fn_out.ap)[0][1] * fsz(fn_out) * (2 if fn_out.dtype == BF16 else 4)
            except Exception:
                nb = 1 << 18
            lat = 2.2 + nb / 120000.0
            return P.op(eng, f, reads=reads, writes=writes, dma=stream, is_out=is_out, lat=lat)

        def act(out, in_, func, reads, writes, bias=None, scale=None, accum=None):
            kw = {}
            if bias is not None:
                kw["bias"] = bias
            if scale is not None:
                kw["scale"] = scale
            if accum is not None:
                kw["accum_out"] = accum

            def f(e):
                return e.activation(out=out, in_=in_, func=func, **kw)
            return P.op("act", f, reads=reads, writes=writes, cost=0.2 + fsz(out) / 1150.0)

        def vcost(eng, out):
            n_ = fsz(out)
            if eng == "pool":
                return 0.4 + n_ / 600.0
            return 0.17 + n_ / 960.0

        def tt(eng, out, in0, in1, op, reads, writes):
            def f(e):
                return e.tensor_tensor(out=out, in0=in0, in1=in1, op=op)
            return P.op(eng, f, reads=reads, writes=writes, cost=vcost(eng, out))

        def stt(eng, out, in0, scalar, in1, op0, op1, reads, writes):
            def f(e):
                return e.scalar_tensor_tensor(out=out, in0=in0, scalar=scalar, in1=in1, op0=op0, op1=op1)
            return P.op(eng, f, reads=reads, writes=writes, cost=vcost(eng, out))

        def ts(eng, out, in0, s1, s2, op0, op1, reads, writes):
            def f(e):
                if s2 is None:
                    return e.tensor_scalar(out=out, in0=in0, scalar1=s1, scalar2=None, op0=op0)
                return e.tensor_scalar(out=out, in0=in0, scalar1=s1, scalar2=s2, op0=op0, op1=op1)
            return P.op(eng, f, reads=reads, writes=writes, cost=vcost(eng, out))

        def cp(eng, out, in_, reads, writes):
            def f(e):
                return e.tensor_copy(out=out, in_=in_)
            return P.op(eng, f, reads=reads, writes=writes, cost=vcost(eng, out))

        def mm(specs, reads, writes):
            def f(e):
                ins = None
                for (o, l, r, st, sp) in specs:
                    ins = e.matmul(o, lhsT=l, rhs=r, start=st, stop=sp)
                return ins
            c_ = 0.0
            for (o, l, r, st, sp) in specs:
                c_ += max(0.1, 0.012 + fsz(r) / 2400.0) * (4.0 if l.dtype == F32 else 1.0)
            return P.op("pe", f, reads=reads, writes=writes, cost=c_)

        def tr(specs, reads, writes):
            def f(e):
                ins = None
                for (o, i, idn) in specs:
                    ins = e.transpose(out=o, in_=i, identity=idn)
                return ins
            return P.op("pe", f, reads=reads, writes=writes, cost=0.11 * len(specs))

        def rstd_from_ms(col, reads_key):
            if USE_POOL_POW and not RSTD_ON_ACT[0]:
                ts("pool", sm.v(col + 2, [(1, 1)]), sm.v(col, [(1, 1)]), EPS, None, ALU.add, None,
                   [reads_key], ["sm%d" % (col + 2)])
                tt("pool", sm.v(col + 1, [(1, 1)]), sm.v(col + 2, [(1, 1)]), nhalf.v(0, [(1, 1)]), ALU.pow,
                   ["sm%d" % (col + 2), "nhalf"], ["sm%d" % (col + 1)])
                return
            act(sm.v(col + 2, [(1, 1)]), sm.v(col, [(1, 1)]), AF.Ln, [reads_key], ["sm%d" % (col + 2)], bias=EPS, scale=1.0)
            act(sm.v(col + 1, [(1, 1)]), sm.v(col + 2, [(1, 1)]), AF.Exp, ["sm%d" % (col + 2)], ["sm%d" % (col + 1)], scale=-0.5)

        def silu_psum(out, ps, n, tT, tV, hb, hs, rkeys, wkey, tkeys):
            kw = {} if hb is None else {"bias": hb}
            act(tT.v(0, [(1, n)]), ps, AF.Tanh, rkeys, [tkeys[0]], scale=hs, **kw)
            act(tV.v(0, [(1, n)]), ps, AF.Identity, rkeys, [tkeys[1]], scale=hs, **kw)
            stt("dve", out, tT.v(0, [(1, n)]), 1.0, tV.v(0, [(1, n)]), ALU.add, ALU.mult, list(tkeys), [wkey])

        ident_b = cst_b.v(IDF, [(1, 128)])
        ident_f = cst_f.v(IDF, [(1, 128)])

        with ExitStack() as s0:
            (PD0,) = psum_set(s0, [([128, 512], F32)])
            PTs = psum_set(s0, [([128, 1024], BF16)] * 4)
            dma(cst_f.v(), cst, "ld0", writes=["cst_f"])
            dma(fmv.v(), fmv_d, "ld1", writes=["fmv"])
            cp("dve", cst_b.v(), cst_f.v(), ["cst_f"], ["cst_b"])
            P.op("dve", lambda e: e.memset(zero_b.v(), 0.0), writes=["zero_b"])
            P.op("dve", lambda e: e.memset(nhalf.v(), -0.5), writes=["nhalf"])
            ts("dve", fmvh.v(), fmv.v(), 0.5, None, ALU.mult, None, ["fmv"], ["fmvh"])
            dma(a_b.v(), rowv[:, R_ALOG:R_ALOG + 32].partition_broadcast(128), "ld0", writes=["a_b"])
            dma(dtb_b.v(), rowv[:, R_DTB:R_DTB + 32].partition_broadcast(128), "ld1", writes=["dtb_b"])
            act(a_b.v(), a_b.v(), AF.Exp, ["a_b"], ["a_b"])
            ts("dve", a_b.v(), a_b.v(), -1.0, None, ALU.mult, None, ["a_b"], ["a_b"])
            tt("dve", diag3.v(0, [(128, 36), (1, 128)]), cst_b.v(IDF, [(0, 36), (1, 128)]),
               fmv.v(O_WSC, [(1, 36), (0, 128)]), ALU.mult, ["cst_b", "fmv"], ["diag3"])

            stgs = [sbt(s0, f"stg{i}", [128, 5632], BF16) for i in range(2)]
            dgs = [sbt(s0, f"dgs{i}", [128, 31 * 128], BF16) for i in range(2)]
            NWA = 3
            wad = [sbt(s0, f"wad{i}", [128, 8 * 512], F32) for i in range(NWA)]
            scv = sbt(s0, "scv", [128, 16], F32)
            badas = [sbt(s0, f"bada{i}", [2, 512], F32) for i in range(2)]
            modss = [sbt(s0, f"modsb{i}", [2, 512], F32) for i in range(2)]
            dma(scv.v(), c_in, "ld0", writes=["scv"])
            act(scv.v(), scv.v(), AF.Silu, ["scv"], ["scv"])
            for j in range(12):
                wb = wad[j % NWA]
                dma(badas[j % 2].v(), rowv[:, R_BADA + j * 512:R_BADA + (j + 1) * 512].partition_broadcast(2),
                    f"bal{j % 2}", writes=[f"bada{j % 2}"])
                dma(wb.v(0, [(512, 8), (1, 512)]),
                    w_ada[:, j * 512:(j + 1) * 512].rearrange("(k p) n -> p k n", p=128),
                    f"wad{j % NWA}", writes=[f"wad{j % NWA}"])
                mm([(PD0.v(0, [(1, 512)], 0, 2), scv.v(k * 2, [(1, 2)]), wb.v(k * 512, [(1, 512)]), k == 0, k == 7)
                    for k in range(8)], ["scv", f"wad{j % NWA}"], ["PD"])
                tt("dve", modss[j % 2].v(), PD0.v(0, [(1, 512)], 0, 2), badas[j % 2].v(),
                   ALU.add, ["PD", f"bada{j % 2}"], [f"modsb{j % 2}"])
                dma(mod_d[:, j * 512:(j + 1) * 512], modss[j % 2].v(), f"mst{j % 2}", reads=[f"modsb{j % 2}"],
                    writes=["mod_d"])
            for m in range(8):
                b = dgs[m % 2]
                tt("pool", b.v(0, [(128, 31), (1, 128)]), cst_b.v(IDF, [(0, 31), (1, 128)]),
                   fmv.v(O_WCF + m * 31, [(1, 31), (0, 128)]), ALU.mult, ["cst_b", "fmv"], [f"dgs{m % 2}"])
                dma(dg31_d[m], b.v(), f"dgst{m % 2}", reads=[f"dgs{m % 2}"], writes=["dg31_d"], eng="pool")
            ws = 0
            f32s = [sbt(s0, f"f32s{i}", [128, 1408], F32) for i in range(2)]
            qc = [0]

            def stage_rows(src_rows, ncols, stg, sk):
                qn = ncols // 4
                for q in range(4):
                    fb = f32s[qc[0] % 2]
                    fk = f"f32s{qc[0] % 2}"
                    dma(fb.v(0, [(1, qn)]), src_rows[:, q * qn:(q + 1) * qn], f"wf{qc[0] % 2}", writes=[fk])
                    act(stg.v(q * qn, [(1, qn)]), fb.v(0, [(1, qn)]), AF.Copy, [fk], [sk])
                    qc[0] += 1

            for k in range(8):
                stg = stgs[ws % 2]
                sk = f"stg{ws % 2}"
                stage_rows(w_in[k * 128:(k + 1) * 128, :], IN_COLS, stg, sk)
                dma(w_in_t[0:12, :, k, :].rearrange("m p c -> p m c"), stg.v(1024, [(128, 12), (1, 128)]),
                    f"wsta{ws % 2}", reads=[sk], writes=["w_in_ta_%d" % k], eng="act")
                dma(w_in_t[12:28, :, k, :].rearrange("m p c -> p m c"), stg.v(2592, [(128, 16), (1, 128)]),
                    f"wstb{ws % 2}", reads=[sk], writes=["w_in_tb_%d" % k], eng="act")
                cp("pool", WZ.v(k * 1056, [(1, 1024)]), stg.v(0, [(1, 1024)]), [sk], ["WZ"])
                cp("pool", WZ.v(k * 1056 + 1024, [(1, 32)]), stg.v(2560, [(1, 32)]), [sk], ["WZ"])
                ws += 1
            for k in range(8):
                stg = stgs[ws % 2]
                sk = f"stg{ws % 2}"
                stage_rows(w_up[k * 128:(k + 1) * 128, :], 2 * DFF, stg, sk)
                dma(w_up_t[:, :, k, :].rearrange("m p c -> p m c"), stg.v(0, [(128, 44), (1, 128)]),
                    f"wsta{ws % 2}", reads=[sk], writes=["w_up_t_%d" % k], eng="act")
                ws += 1

            def load_mod(dst, cond, idx, stream, key):
                dma(dst.v(), mod_d[cond:cond + 1, idx * D:(idx + 1) * D].partition_broadcast(128), stream,
                    reads=["mod_d"], writes=[key])

            def load_row(dst, off, n, stream, key):
                dma(dst.v(0, [(1, n)]), rowv[:, off:off + n].partition_broadcast(128), stream, writes=[key])

            def norm_mod_transpose(xin, xin_key, G, G_key, SH, SH_key, tmpF, tmpF_key, hn, hn_key, junk, junk_key,
                                   hT, hT_key, smcol, PTt, ptk):
                act(junk.v(0, [(1, 1024)]), xin.v(), AF.Square, [xin_key], [junk_key, "sm%d" % smcol],
                    scale=1.0 / 32.0, accum=sm.v(smcol, [(1, 1)]))
                rstd_from_ms(smcol, "sm%d" % smcol)
                stt("dve", tmpF.v(), xin.v(), sm.v(smcol + 1, [(1, 1)]), G.v(), ALU.mult, ALU.mult,
                    [xin_key, "sm%d" % (smcol + 1), G_key], [tmpF_key])
                tt("dve", hn.v(0, [(1, 1024)]), tmpF.v(), SH.v(), ALU.add, [tmpF_key, SH_key], [hn_key])
                if hT is not None:
                    norm_mod_B(hn, hn_key, hT, hT_key, PTt, ptk)

            def norm_mod_B(hn, hn_key, hT, hT_key, PTt, ptk):
                tr([(PTt.v(k * 128, [(1, 128)]), hn.v(k * 128, [(1, 128)]), ident_b) for k in range(8)],
                   [hn_key, "cst_b"], [ptk])
                act(hT.v(0, [(1, 1024)]), PTt.v(), AF.Copy, [ptk], [hT_key])

            Gm = [sbt(s0, f"G{i}", [128, D], F32) for i in range(2)]
            SHm = [sbt(s0, f"SH{i}", [128, D], F32) for i in range(2)]
            wr = sbt(s0, "wr", [128, D], F32)
            NB = 3
            m1work = []
            xs_ = [sbt(s0, f"xs{i}", [128, D], F32) for i in range(NB)]
            tFs = [sbt(s0, f"tF{i}", [128, D], F32) for i in range(NB)]
            hns = [sbt(s0, f"hn{i}", [128, D], BF16) for i in range(NB)]
            jks = [sbt(s0, f"jk{i}", [128, D], BF16) for i in range(NB)]
            hTs = [sbt(s0, f"hTs{i}", [128, D], BF16) for i in range(NB)]
            load_row(wr, R_PRE, D, "ld0", "wr")
            for cnd in (1, 0):
                load_mod(Gm[cnd], cnd, 1, "ld0", "G%d" % cnd)
                load_mod(SHm[cnd], cnd, 0, "ld1", "SH%d" % cnd)
                stt("dve", Gm[cnd].v(), Gm[cnd].v(), 1.0, wr.v(), ALU.add, ALU.mult, ["G%d" % cnd, "wr"],
                    ["G%d" % cnd])
            for sq in seqs:
                L = sq["L"]
                h1 = sq["hT1"]
                for (a, b_) in ((0, HALO), (L + HALO, L + 2 * HALO)):
                    dma(h1[:, :, a:b_].rearrange("k p t -> p k t"), zero_b.v(0, [(HALO, 8), (1, HALO)]),
                        "zst", reads=["zero_b"], writes=["hT1_%d_z" % sq["s"]])
                for c in range(sq["nch"]):
                    m1work.append((sq, c))
            def m1_A(sq, c, it):
                i3 = it % NB
                r0 = sq["base"] + c * 128
                dma(xs_[i3].v(), x_all[r0:r0 + 128, :], f"xl{i3}", writes=[f"xs{i3}"])
                norm_mod_transpose(xs_[i3], f"xs{i3}", Gm[sq["cond"]], "G%d" % sq["cond"], SHm[sq["cond"]],
                                   "SH%d" % sq["cond"], tFs[i3], f"tF{i3}", hns[i3], f"hn{i3}",
                                   jks[i3], f"jk{i3}", None, None, 16 + 4 * i3, None, None)

            def m1_B(sq, c, it):
                i3 = it % NB
                norm_mod_B(hns[i3], f"hn{i3}", hTs[i3], f"hTs{i3}", PTs[it % 4], f"PT{it % 4}")
                dma(sq["hT1"][:, :, HALO + c * 128:HALO + (c + 1) * 128].rearrange("k p t -> p k t"),
                    hTs[i3].v(0, [(128, 8), (1, 128)]), f"hst{i3}", reads=[f"hTs{i3}"],
                    writes=["hT1_%d_%d" % (sq["s"], c)], eng="act")

            for i, (sq, c) in enumerate(m1work):
                m1_A(sq, c, i)
                if i >= 1:
                    m1_B(m1work[i - 1][0], m1work[i - 1][1], i - 1)
            m1_B(m1work[-1][0], m1work[-1][1], len(m1work) - 1)
            P.flush()

        TLH = TL + 2 * HALO

        def interleave(gens):
            gens = [g for g in gens if g is not None]
            while gens:
                for g in list(gens):
                    try:
                        next(g)
                    except StopIteration:
                        gens.remove(g)

        def mixer_phase(pass2):
            with ExitStack() as s2:
                PA, PBX, PC = psum_set(s2, [([128, 1024], F32)] * 3)
                (PD,) = psum_set(s2, [([128, 512], F32)])
                (PT,) = psum_set(s2, [([128, 1024], BF16)])
                hin = [sbt(s2, f"hin{i}", [128, 8 * TLH], BF16) for i in range(2)]
                NWR = 6
                wrg = [sbt(s2, f"wrg{i}", [128, 1024], BF16) for i in range(NWR)]
                xbc_pre = sbt(s2, "xbc_pre", [128, 12 * TLH], BF16)
                xbc_cs = [sbt(s2, f"xbc_c{i}", [128, 12 * TL], BF16) for i in range(3)]
                S_run = sbt(s2, "S_run", [128, D], F32)
                Sbf = [sbt(s2, "Sbf0", [128, D], BF16)]
                xs_tm = sbt(s2, "xs_tm", [128, D], BF16)
                F1 = sbt(s2, "F1", [128, D], F32)
                F2 = sbt(s2, "F2", [128, D], F32)
                stTs = [sbt(s2, f"stT{i}", [128, TL], F32) for i in range(2)]
                c3s = [sbt(s2, f"c3_{i}", [128, TL], F32) for i in range(2)]
                stVs = [sbt(s2, f"stV{i}", [128, TL], F32) for i in range(2)]
                NCL = TL // 128
                H_dts = [[sbt(s2, f"Hdts{i}{c}", [128, 256], F32) for c in range(NCL)] for i in range(2)]
                REC = [[sbt(s2, f"REC{i}{c}", [128, 3328], BF16) for c in range(NCL)] for i in range(2)]
                H_xw = [[sbt(s2, f"Hxw{i}{c}", [128, D], BF16) for c in range(NCL)] for i in range(2)]
                DT, DA, CUM, E1, D2, WW = 0, 32, 64, 128, 192, 224
                XBP_K = ["xbc_pre"]

                def xbc_pre_v(off, dims):
                    return xbc_pre.v(off, dims)

                if pass2:
                    H_yd = [[sbt(s2, f"Hyd{i}{c}", [128, D], BF16) for c in range(NCL)] for i in range(2)]
                    ar1 = sbt(s2, "ar1", [128, 4096], BF16)
                    MT = sbt(s2, "MT", [128, 2048], BF16)
                    usq = sbt(s2, "usq", [128, 8 * TL], BF16)
                    u_pre = sbt(s2, "u_pre", [128, 8 * TLH], BF16)
                    u_cs = [sbt(s2, f"u_c{i}", [128, 8 * TL], BF16) for i in range(1)]
                    sg = sbt(s2, "sg", [128, TLH], F32)
                    lnA = sbt(s2, "lnA", [128, TL], F32)
                    lnB = sbt(s2, "lnB", [128, TL], F32)
                    lnT = sbt(s2, "lnT", [128, TL], F32)
                    dgr = [sbt(s2, f"dgr{i}", [128, 31 * 128], BF16) for i in range(2)]
                    Db = sbt(s2, "Db", [128, D], BF16)
                    dsk = sbt(s2, "dsk", [128, 16], F32)
                    xdt = sbt(s2, "xdt", [128, D], BF16)
                    xsD = sbt(s2, "xsD", [128, D], BF16)
                    cbm = sbt(s2, "cbm", [128, 512], BF16)
                    try:
                        print("SBUF bytes remaining in mixer pass2:", nc.sbuf_bytes_remaining())
                    except Exception as _e:
                        print("sbuf_bytes_remaining unavailable", _e)
                    load_row(dsk, R_DSK, 16, "ld1", "dsk")
                    cp("dve", Db.v(0, [(64, 16), (1, 64)]), dsk.v(0, [(1, 16), (0, 64)]), ["dsk"], ["Db"])

                cnt = dict(w=0, d=0, y=0, sb=0)

                def wload(ci):
                    i = cnt["w"] % NWR
                    cnt["w"] += 1
                    dma(wrg[i].v(), w_in_t[ci].rearrange("p k c -> p (k c)"), f"wr{i}",
                        reads=["w_in_ta_%d" % k_ for k_ in range(8)] + ["w_in_tb_%d" % k_ for k_ in range(8)],
                        writes=[f"wrg{i}"])
                    return wrg[i], f"wrg{i}"

                def prep_chunk(hb, hkey, xbc_c, xck, bi, cl):
                    dts = H_dts[bi][cl]
                    kd = f"Hdts{bi}{cl}"
                    t0 = HALO + cl * 128
                    mm([(PD.v(256, [(1, 32)]), hb.v(k * TLH + t0, [(1, 128)]), WZ.v(k * 1056 + 1024, [(1, 32)]),
                         k == 0, k == 7) for k in range(8)], [hkey, "WZ"], ["PDdt"])
                    tt("dve", dts.v(D2, [(1, 32)]), PD.v(256, [(1, 32)]), dtb_b.v(), ALU.add, ["PDdt", "dtb_b"], [kd])
                    act(dts.v(D2, [(1, 32)]), dts.v(D2, [(1, 32)]), AF.Exp, [kd], [kd])
                    act(dts.v(DT, [(1, 32)]), dts.v(D2, [(1, 32)]), AF.Ln, [kd], [kd], bias=1.0, scale=1.0)
                    tt("dve", dts.v(DA, [(1, 32)]), dts.v(DT, [(1, 32)]), a_b.v(), ALU.mult, [kd, "a_b"], [kd])
                    mm([(PD.v(320, [(1, 16)]), cst_f.v(UINC, [(1, 128)]), dts.v(DA, [(1, 16)]), True, True),
                        (PD.v(336, [(1, 16)]), cst_f.v(LINC, [(1, 128)]), dts.v(DA + 16, [(1, 16)]), True, True),
                        (PD.v(352, [(1, 32)]), cst_f.v(ONES, [(1, 128)]), dts.v(DA, [(1, 32)]), True, True)],
                       [kd, "cst_f"], ["PDcum"])
                    act(dts.v(CUM, [(1, 64)]), PD.v(320, [(1, 64)]), AF.Copy, ["PDcum"], [kd])
                    act(dts.v(E1, [(1, 64)]), dts.v(CUM, [(1, 64)]), AF.Exp, [kd], [kd])
                    tt("dve", dts.v(D2, [(1, 32)]), dts.v(CUM + 32, [(1, 32)]), dts.v(CUM, [(1, 32)]), ALU.subtract,
                       [kd], [kd])
                    act(dts.v(D2, [(1, 32)]), dts.v(D2, [(1, 32)]), AF.Exp, [kd], [kd])
                    tt("dve", dts.v(WW, [(1, 32)]), dts.v(D2, [(1, 32)]), dts.v(DT, [(1, 32)]), ALU.mult, [kd], [kd])
                    yield
                    tr([(PT.v(k * 128, [(1, 128)]), xbc_c.v(k * TL + cl * 128, [(1, 128)]), ident_b) for k in range(8)],
                       [xck, "cst_b"], ["PT"])
                    act(xs_tm.v(0, [(1, 1024)]), PT.v(), AF.Copy, ["PT"], ["xs_tm"])
                    tr([(PT.v(k * 128, [(1, 128)]), xbc_c.v((8 + k) * TL + cl * 128, [(1, 128)]), ident_b)
                        for k in range(2)], [xck, "cst_b"], ["PT"])
                    rec = REC[bi][cl]
                    rk = f"REC{bi}{cl}"
                    act(rec.v(3072, [(1, 256)]), PT.v(0, [(1, 256)]), AF.Copy, ["PT"], [rk + ".B"])
                    tt(PREP_VEC_ENG, H_xw[bi][cl].v(0, [(64, 16), (1, 64)]), xs_tm.v(0, [(64, 16), (1, 64)]),
                       dts.v(WW + 16, [(1, 16), (0, 64)]), ALU.mult, ["xs_tm", kd], [f"Hxw{bi}{cl}"])
                    tt(PREP_VEC_ENG, rec.v(2048, [(64, 16), (1, 64)]), xs_tm.v(0, [(64, 16), (1, 64)]),
                       dts.v(WW, [(1, 16), (0, 64)]), ALU.mult, ["xs_tm", kd], [rk + ".xw"])
                    yield
                    if not pass2:
                        return
                    rhs_v = lambda off, dims: ar1.v(off, dims)
                    lexp_v = lambda off, dims: ar1.v(2048 + off, dims)
                    mm([(PBX.v(hf * 512, [(1, 512)]), hb.v(k * TLH + t0, [(1, 128)]),
                         WZ.v(k * 1056 + hf * 512, [(1, 512)]), k == 0, k == 7)
                        for hf in range(2) for k in range(8)], [hkey, "WZ"], ["PB0", "PB1"])
                    act(lexp_v(0, [(1, 1024)]), PBX.v(), AF.Tanh, ["PB0", "PB1"], ["Lexp"], scale=0.5)
                    act(lexp_v(1024, [(1, 1024)]), PBX.v(), AF.Identity, ["PB0", "PB1"], ["Lexp"], scale=0.5)
                    stt("dve", rec.v(1024, [(1, 1024)]), lexp_v(0, [(1, 1024)]), 1.0, lexp_v(1024, [(1, 1024)]),
                        ALU.add, ALU.mult, ["Lexp"], [rk + ".sz"])
                    yield
                    mm([(PD.v(g * 128, [(1, 128)]), xbc_c.v((8 + g) * TL + cl * 128, [(1, 128)]),
                         xbc_c.v((10 + g) * TL + cl * 128, [(1, 128)]), True, True) for g in range(2)],
                       [xck], ["PDa"])
                    tt("dve", cbm.v(0, [(128, 2), (1, 128)]), PD.v(0, [(128, 2), (1, 128)]),
                       cst_f.v(UINC, [(0, 2), (1, 128)]), ALU.mult, ["PDa", "cst_f"], ["cbm"])
                    tt("dve", cbm.v(256, [(128, 2), (1, 128)]), PD.v(0, [(128, 2), (1, 128)]),
                       cst_f.v(LINC, [(0, 2), (1, 128)]), ALU.mult, ["PDa", "cst_f"], ["cbm"])
                    tt(PREP_VEC_ENG, xsD.v(), xs_tm.v(), Db.v(), ALU.mult, ["xs_tm", "Db"], ["xsD"])
                    mm([(PA.v(hf * 512, [(1, 512)]), ident_b, xsD.v(hf * 512, [(1, 512)]), True, False)
                        for hf in range(2)], ["xsD", "cst_b"], ["PA0", "PA1"])
                    yield
                    for d in range(2):
                        msk = UINC if d == 0 else LINC
                        lmask = LSTR if d == 0 else USTR
                        tt(PREP_VEC_ENG, rhs_v(0, [(128, 16), (1, 128)]), cst_f.v(msk, [(0, 16), (1, 128)]),
                           dts.v(DA + 16 * d, [(1, 16), (0, 128)]), ALU.mult, ["cst_f", kd], ["rhs"])
                        tt(PREP_VEC_ENG, xdt.v(0, [(64, 16), (1, 64)]), xs_tm.v(0, [(64, 16), (1, 64)]),
                           dts.v(DT + 16 * d, [(1, 16), (0, 64)]), ALU.mult, ["xs_tm", kd], ["xdt"])
                        for q in range(4):
                            mm([(PC.v(0, [(1, 512)]), cst_b.v(lmask, [(1, 128)]),
                                 rhs_v(q * 512, [(1, 512)]), True, True)], ["rhs", "cst_b"], ["PC0"])
                            act(lexp_v(q * 512, [(1, 512)]), PC.v(0, [(1, 512)]), AF.Exp, ["PC0"], ["Lexp"])
                            if q % 2 == 1:
                                yield
                        tt("dve", MT.v(0, [(1024, 2), (128, 8), (1, 128)]),
                           lexp_v(0, [(1024, 2), (128, 8), (1, 128)]),
                           cbm.v(d * 256, [(128, 2), (0, 8), (1, 128)]), ALU.mult, ["Lexp", "cbm"], ["MT"])
                        mm([(PA.v(h * 64, [(1, 64)]), MT.v(h * 128, [(1, 128)]), xdt.v(h * 64, [(1, 64)]),
                             False, d == 1) for h in range(NH)], ["MT", "xdt"], ["PA0", "PA1"])
                        yield
                    act(H_yd[bi][cl].v(0, [(1, 1024)]), PA.v(), AF.Copy, ["PA0", "PA1"], [f"Hyd{bi}{cl}"])
                    yield

                def fm_stage(sq, T, n):
                    s = sq["s"]
                    bi = n % 2
                    hb = hin[bi]
                    hkey = f"hin{bi}"
                    xbc_c = xbc_cs[n % 3]
                    xck = f"xbc_c{n % 3}"
                    dma(hb.v(0, [(TLH, 8), (1, TLH)]),
                        sq["hT1"][:, :, T * TL:T * TL + TLH].rearrange("k p t -> p k t"),
                        f"hl{bi}", reads=["hT1_%d_z" % s] + ["hT1_%d_%d" % (s, c_) for c_ in
                                                      range(max(0, 2 * T - 1), min(sq["nch"], 2 * T + 3))],
                        writes=[hkey])
                    for m in range(12):
                        w, wk = wload(m)
                        po = (m % 2) * 512
                        pk = "PB%d" % (m % 2)
                        NX = TL + 2
                        mm([(PBX.v(po, [(1, NX)]), w.v(k * 128, [(1, 128)]), hb.v(k * TLH + HALO - 1, [(1, NX)]),
                             k == 0, k == 7) for k in range(8)], [wk, hkey], [pk])
                        act(xbc_pre_v(m * TLH + HALO - 1, [(1, NX)]), PBX.v(po, [(1, NX)]), AF.Copy, [pk], XBP_K)
                        if m % 2 == 1:
                            yield
                    for m in range(12):
                        po = (m % 2) * 512
                        pk = "PB%d" % (m % 2)
                        if CONV3_ON_DVE:
                            c3 = c3s[m % 2]
                            c3k = f"c3_{m % 2}"
                            for k3 in range(3):
                                rv = xbc_pre_v(m * TLH + HALO - 1 + k3, [(1, TL)])
                                wv = fmv.v(O_WSC + m * 3 + k3, [(1, 1)])
                                if k3 == 0:
                                    ts("dve", c3.v(), rv, wv, None, ALU.mult, None, XBP_K + ["fmv"], [c3k])
                                else:
                                    stt("dve", c3.v(), rv, wv, c3.v(), ALU.mult, ALU.add, XBP_K + ["fmv", c3k], [c3k])
                            silu_psum(xbc_c.v(m * TL, [(1, TL)]), c3.v(), TL, stTs[m % 2], stVs[m % 2],
                                      fmvh.v(O_BSC + m, [(1, 1)]), 0.5, [c3k, "fmvh"], xck,
                                      (f"stT{m % 2}", f"stV{m % 2}"))
                        else:
                            mm([(PBX.v(po, [(1, TL)]), diag3.v((m * 3 + k3) * 128, [(1, 128)]),
                                 xbc_pre_v(m * TLH + HALO - 1 + k3, [(1, TL)]), k3 == 0, k3 == 2) for k3 in range(3)],
                               ["diag3"] + XBP_K, [pk])
                            silu_psum(xbc_c.v(m * TL, [(1, TL)]), PBX.v(po, [(1, TL)]), TL, stTs[m % 2], stVs[m % 2],
                                      fmvh.v(O_BSC + m, [(1, 1)]), 0.5, [pk, "fmvh"], xck,
                                      (f"stT{m % 2}", f"stV{m % 2}"))
                        if m % 2 == 1:
                            yield
                    if pass2:
                        u_c = u_cs[0]
                        uck = "u_c0"
                        for m in range(8):
                            wg, wgk = wload(20 + m)
                            mm([(PBX.v(0, [(1, TLH)]), wg.v(k * 128, [(1, 128)]), hb.v(k * TLH, [(1, TLH)]),
                                 k == 0, k == 7) for k in range(8)], [wgk, hkey], ["PB0"])
                            act(sg.v(), PBX.v(0, [(1, TLH)]), AF.Tanh, ["PB0"], ["sg"], scale=0.5)
                            wa, wak = wload(12 + m)
                            mm([(PBX.v(512, [(1, TLH)]), wa.v(k * 128, [(1, 128)]), hb.v(k * TLH, [(1, TLH)]),
                                 k == 0, k == 7) for k in range(8)], [wak, hkey], ["PB1"])
                            stt("dve", u_pre.v(m * TLH, [(1, TLH)]), sg.v(), 1.0, PBX.v(512, [(1, TLH)]), ALU.add,
                                ALU.mult, ["PB1", "sg"], ["u_pre"])
                            yield
                        for m in range(8):
                            di = m % 2
                            po = (m % 2) * 512
                            pk = "PB%d" % (m % 2)
                            if di == 0 or DIAG31_ALL_DMA:
                                dma(dgr[di].v(), dg31_d[m], f"dgl{di}", reads=["dg31_d"], writes=[f"dgr{di}"])
                            else:
                                tt("dve", dgr[1].v(0, [(128, 31), (1, 128)]), cst_b.v(IDF, [(0, 31), (1, 128)]),
                                   fmv.v(O_WCF + m * 31, [(1, 31), (0, 128)]), ALU.mult, ["cst_b", "fmv"], ["dgr1"])
                            mm([(PBX.v(po, [(1, TL)]), dgr[di].v(k * 128, [(1, 128)]),
                                 u_pre.v(m * TLH + 1 + k, [(1, TL)]), k == 0, k == 30) for k in range(31)],
                               [f"dgr{di}", "u_pre"], [pk])
                            act(u_c.v(m * TL, [(1, TL)]), PBX.v(po, [(1, TL)]), AF.Identity, [pk, "fmv"],
                                [uck], bias=fmv.v(O_BCF + m, [(1, 1)]), scale=0.5)
                            act(usq.v(m * TL, [(1, TL)]), PBX.v(po, [(1, TL)]), AF.Square, [pk, "fmv"], ["usq"],
                                bias=fmv.v(O_BCF + m, [(1, 1)]), scale=0.5)
                            yield
                        mm([(PBX.v(0, [(1, TL)]), cst_b.v(ONES, [(1, 128)]), u_c.v(m * TL, [(1, TL)]), m == 0, m == 7)
                            for m in range(8)], [uck, "cst_b"], ["PB0"])
                        mm([(PBX.v(512, [(1, TL)]), cst_b.v(ONES, [(1, 128)]), usq.v(m * TL, [(1, TL)]), m == 0, m == 7)
                            for m in range(8)], ["usq", "cst_b"], ["PB1"])
                        act(lnA.v(), PBX.v(0, [(1, TL)]), AF.Identity, ["PB0"], ["lnA"], scale=1.0 / 1024.0)
                        act(lnB.v(), PBX.v(512, [(1, TL)]), AF.Identity, ["PB1"], ["lnB"], scale=1.0 / 1024.0)
                        tt("dve", lnT.v(), lnA.v(), lnA.v(), ALU.mult, ["lnA"], ["lnT"])
                        tt("dve", lnB.v(), lnB.v(), lnT.v(), ALU.subtract, ["lnB", "lnT"], ["lnB"])
                        act(lnB.v(), lnB.v(), AF.Ln, ["lnB"], ["lnB"], bias=EPS, scale=1.0)
                        act(lnB.v(), lnB.v(), AF.Exp, ["lnB"], ["lnB"], scale=-0.5)
                        yield
                        for m in range(8):
                            tt("dve", lnT.v(), u_c.v(m * TL, [(1, TL)]), lnA.v(), ALU.subtract,
                               [uck, "lnA"], ["lnT"])
                            tt("dve", lnT.v(), lnT.v(), lnB.v(), ALU.mult, ["lnT", "lnB"], ["lnT"])
                            silu_psum(u_c.v(m * TL, [(1, TL)]), lnT.v(), TL, stTs[m % 2], stVs[m % 2],
                                      fmvh.v(O_LNB + m, [(1, 1)]), fmvh.v(O_LNG + m, [(1, 1)]), ["lnT", "fmvh"], uck,
                                      (f"stT{m % 2}", f"stV{m % 2}"))
                            if m % 2 == 1:
                                yield
                        dma(sq["oin"][8:16, :, T * TL:(T + 1) * TL].rearrange("k p t -> p k t"),
                            u_c.v(0, [(TL, 8), (1, TL)]), f"uost{bi}", reads=[uck], writes=["oin_%d" % s], eng="act")
                        yield

                def prep_stage(sq, T, n):
                    bi = n % 2
                    for cl in range(NCL):
                        yield from prep_chunk(hin[bi], f"hin{bi}", xbc_cs[n % 3], f"xbc_c{n % 3}", bi, cl)

                def state_update(bi, cl, d):
                    dts = H_dts[bi][cl]
                    kd = f"Hdts{bi}{cl}"
                    tt("dve", S_run.v(0, [(64, 16), (1, 64)]), S_run.v(0, [(64, 16), (1, 64)]),
                       dts.v(E1 + 32 + 16 * d, [(1, 16), (0, 64)]), ALU.mult, ["S_run", kd], ["S_run"])
                    for g in range(2):
                        mm([(PC.v(512, [(1, 512)]), REC[bi][cl].v(3072 + g * 128, [(1, 128)]),
                             H_xw[bi][cl].v(g * 512, [(1, 512)]), True, True)],
                           [f"REC{bi}{cl}.B", f"Hxw{bi}{cl}"], ["PC1"])
                        tt("dve", S_run.v(g * 512, [(1, 512)]), S_run.v(g * 512, [(1, 512)]), PC.v(512, [(1, 512)]),
                           ALU.add, ["S_run", "PC1"], ["S_run"])

                def init_state(sq, src):
                    if sq["s"] == 0:
                        dma(F1.v(0, [(128, 8), (1, 128)]), src.rearrange("(k q) n -> q k n", q=128), "ld0",
                            writes=["F1"])
                        for hf in range(2):
                            tr([(PC.v(512 + k * 128, [(1, 128)]), F1.v((hf * 4 + k) * 128, [(1, 128)]), ident_f)
                                for k in range(4)], ["F1", "cst_f"], ["PC1"])
                            act(S_run.v(hf * 512, [(1, 512)]), PC.v(512, [(1, 512)]), AF.Copy, ["PC1"], ["S_run"])
                    else:
                        P.op("dve", lambda e: e.memset(S_run.v(), 0.0), writes=["S_run"])

                def final_state(sq, dst):
                    for hf in range(2):
                        tr([(PC.v(512 + k * 128, [(1, 128)]), S_run.v((hf * 4 + k) * 128, [(1, 128)]), ident_f)
                            for k in range(4)], ["S_run", "cst_f"], ["PC1"])
                        act(F2.v(hf * 512, [(1, 512)]), PC.v(512, [(1, 512)]), AF.Copy, ["PC1"], ["F2"])
                    dma(dst[sq["s"] - 1].rearrange("(k q) n -> q k n", q=128), F2.v(0, [(128, 8), (1, 128)]),
                        "nsst", reads=["F2"], writes=["ns_out"], is_out=True, eng="act")

                def st_stage(sq, T, n, first, last):
                    s = sq["s"]
                    bi = n % 2
                    xbc_c = xbc_cs[n % 3]
                    xck = f"xbc_c{n % 3}"
                    if first:
                        init_state(sq, st_b)
                        yield
                    for cl in range(NCL - 1, -1, -1):
                        cg = T * NCL + cl
                        dts = H_dts[bi][cl]
                        kd = f"Hdts{bi}{cl}"
                        rec = REC[bi][cl]
                        rk = f"REC{bi}{cl}"
                        act(Sbf[0].v(), S_run.v(), AF.Copy, ["S_run"], ["Sbf0"])
                        for g in range(2):
                            mm([(PC.v(512, [(1, 512)]), xbc_c.v((10 + g) * TL + cl * 128, [(1, 128)]),
                                 Sbf[0].v(g * 512, [(1, 512)]), True, True)], [xck, "Sbf0"], ["PC1"])
                            tt("dve", F2.v(g * 512, [(64, 8), (1, 64)]), PC.v(512, [(64, 8), (1, 64)]),
                               dts.v(E1 + 16 + 8 * g, [(1, 8), (0, 64)]), ALU.mult, ["PC1", kd], ["F2"])
                        tt("dve", rec.v(0, [(1, 1024)]), F2.v(), H_yd[bi][cl].v(), ALU.add, ["F2", f"Hyd{bi}{cl}"],
                           [rk + ".y"])
                        yield
                        dma(sq["rec"][cg], rec.v(), f"recst{bi}{cl}",
                            reads=[rk + ".y", rk + ".sz", rk + ".xw", rk + ".B"], writes=["rec_%d" % s], eng="act")
                        dma(sq["e1"][cg], dts.v(E1, [(1, 64)]), f"e1st{bi}{cl}", reads=[kd], writes=["e1_%d" % s],
                            eng="act")
                        dma(sq["cd"][cg].rearrange("p (g t) -> p g t", g=2),
                            xbc_c.v(10 * TL + cl * 128, [(TL, 2), (1, 128)]), f"cdst{bi}{cl}", reads=[xck],
                            writes=["cd_%d" % s], eng="act")
                        state_update(bi, cl, 1)
                        yield
                    if last and s > 0:
                        final_state(sq, ns_b)
                        yield

                work = []
                for sq in seqs:
                    nt = sq["nt"]
                    tiles = list(range(nt - 1, -1, -1))
                    for i, T in enumerate(tiles):
                        work.append((sq, T, i == 0, i == nt - 1))
                for n in range(len(work) + 2):
                    gens = []
                    if n < len(work):
                        gens.append(fm_stage(work[n][0], work[n][1], n))
                    if 0 <= n - 1 < len(work):
                        w1 = work[n - 1]
                        gens.append(prep_stage(w1[0], w1[1], n - 1))
                    if 0 <= n - 2 < len(work):
                        w2 = work[n - 2]
                        gens.append(st_stage(w2[0], w2[1], n - 2, w2[2], w2[3]))
                    interleave(gens)
                P.flush(win=0.6)

        mixer_phase(True)

        s3 = ExitStack()
        RSTD_ON_ACT[0] = True
        if True:
            PY = psum_set(s3, [([128, 512], F32)] * 2)
            PS = PY
            fPTs = psum_set(s3, [([128, 1024], BF16)] * 1)
            NR = 3
            RECs = [sbt(s3, f"fREC{i}", [128, 3328], BF16) for i in range(NR)]
            Cbs = [sbt(s3, f"fC{i}", [128, 256], BF16) for i in range(NR)]
            E1s = [sbt(s3, f"fE{i}", [128, 64], F32) for i in range(NR)]
            S_run = sbt(s3, "fS", [128, D], F32)
            Sbfs = [sbt(s3, f"fSbf{i}", [128, D], BF16) for i in range(2)]
            Fs = [sbt(s3, f"fF{i}", [128, D], F32) for i in range(2)]
            jk = sbt(s3, "fjk", [128, D], BF16)
            yns = [sbt(s3, f"fyn{i}", [128, D], BF16) for i in range(2)]
            ysTs = [sbt(s3, f"fysT{i}", [128, D], BF16) for i in range(2)]
            ssdw = sbt(s3, "fssdw", [128, D], F32)
            load_row(ssdw, R_SSD, D, "ld0", "fssdw")
            it = 0
            for sq in seqs:
                s = sq["s"]
                if s == 0:
                    dma(Fs[0].v(0, [(128, 8), (1, 128)]), st_f.rearrange("(k q) n -> q k n", q=128), "ld0",
                        writes=["fF0"])
                    for hf in range(2):
                        tr([(PY[hf].v(k * 128, [(1, 128)]), Fs[0].v((hf * 4 + k) * 128, [(1, 128)]), ident_f)
                            for k in range(4)], ["fF0", "cst_f"], [f"PY{hf}"])
                        act(S_run.v(hf * 512, [(1, 512)]), PY[hf].v(), AF.Copy, [f"PY{hf}"], ["fS"])
                else:
                    P.op("dve", lambda e: e.memset(S_run.v(), 0.0), writes=["fS"])
                for cg in range(sq["nch"]):
                    i3 = it % NR
                    i2 = it % 2
                    it += 1
                    rec, rk = RECs[i3], f"fREC{i3}"
                    dma(rec.v(), sq["rec"][cg], f"frl{i3}", reads=["rec_%d" % s], writes=[rk])
                    dma(E1s[i3].v(), sq["e1"][cg], f"fel{i3}", reads=["e1_%d" % s], writes=[f"fE{i3}"])
                    dma(Cbs[i3].v(), sq["cd"][cg], f"fcl{i3}", reads=["cd_%d" % s], writes=[f"fC{i3}"])
                    act(Sbfs[i2].v(), S_run.v(), AF.Copy, ["fS"], [f"fSbf{i2}"])
                    F = Fs[i2]
                    fk = f"fF{i2}"
                    for g in range(2):
                        mm([(PY[g].v(), Cbs[i3].v(g * 128, [(1, 128)]), Sbfs[i2].v(g * 512, [(1, 512)]), True, True)],
                           [f"fC{i3}", f"fSbf{i2}"], [f"PY{g}"])
                        tt("dve", F.v(g * 512, [(64, 8), (1, 64)]), PY[g].v(0, [(64, 8), (1, 64)]),
                           E1s[i3].v(8 * g, [(1, 8), (0, 64)]), ALU.mult, [f"PY{g}", f"fE{i3}"], [fk])
                    tt("dve", F.v(), F.v(), rec.v(0, [(1, 1024)]), ALU.add, [fk, rk], [fk])
                    tt("dve", F.v(), F.v(), rec.v(1024, [(1, 1024)]), ALU.mult, [fk, rk], [fk])
                    act(jk.v(0, [(1, 1024)]), F.v(), AF.Square, [fk], ["fjk", "sm%d" % (32 + 4 * i2)],
                        scale=1.0 / 32.0, accum=sm.v(32 + 4 * i2, [(1, 1)]))
                    rstd_from_ms(32 + 4 * i2, "sm%d" % (32 + 4 * i2))
                    stt("dve", yns[i2].v(0, [(1, 1024)]), F.v(), sm.v(33 + 4 * i2, [(1, 1)]), ssdw.v(), ALU.mult,
                        ALU.mult, [fk, "sm%d" % (33 + 4 * i2), "fssdw"], [f"fyn{i2}"])
                    tr([(fPTs[0].v(k * 128, [(1, 128)]), yns[i2].v(k * 128, [(1, 128)]), ident_b) for k in range(8)],
                       [f"fyn{i2}", "cst_b"], ["fPT0"])
                    act(ysTs[i2].v(0, [(1, 1024)]), fPTs[0].v(), AF.Copy, ["fPT0"], [f"fysT{i2}"])
                    dma(sq["oin"][0:8, :, cg * 128:(cg + 1) * 128].rearrange("k p t -> p k t"),
                        ysTs[i2].v(0, [(128, 8), (1, 128)]), f"yst{i2}", reads=[f"fysT{i2}"],
                        writes=["oinY_%d_%d" % (s, cg)], eng="act")
                    tt("dve", S_run.v(0, [(64, 16), (1, 64)]), S_run.v(0, [(64, 16), (1, 64)]),
                       E1s[i3].v(32, [(1, 16), (0, 64)]), ALU.mult, ["fS", f"fE{i3}"], ["fS"])
                    for g in range(2):
                        mm([(PS[g].v(), rec.v(3072 + g * 128, [(1, 128)]), rec.v(2048 + g * 512, [(1, 512)]),
                             True, True)], [rk], [f"PY{g}"])
                        tt("dve", S_run.v(g * 512, [(1, 512)]), S_run.v(g * 512, [(1, 512)]), PS[g].v(), ALU.add,
                           ["fS", f"PY{g}"], ["fS"])
                if s > 0:
                    for hf in range(2):
                        tr([(PY[hf].v(k * 128, [(1, 128)]), S_run.v((hf * 4 + k) * 128, [(1, 128)]), ident_f)
                            for k in range(4)], ["fS", "cst_f"], [f"PY{hf}"])
                        act(Fs[0].v(hf * 512, [(1, 512)]), PY[hf].v(), AF.Copy, [f"PY{hf}"], ["fF0"])
                    dma(ns_f[s - 1].rearrange("(k q) n -> q k n", q=128), Fs[0].v(0, [(128, 8), (1, 128)]),
                        "nsst", reads=["fF0"], writes=["ns_out"], is_out=True, eng="act")

        with ExitStack() as s4:
            PMs = psum_set(s4, [([128, 1024], F32)] * 2)
            PTs = psum_set(s4, [([128, 1024], BF16)] * 1)
            Wo = sbt(s4, "Wo", [128, 16 * D], BF16)
            G1m = [sbt(s4, f"G1_{i}", [128, D], F32) for i in range(2)]
            G2m = [sbt(s4, f"G2_{i}", [128, D], F32) for i in range(2)]
            SH2m = [sbt(s4, f"SH2_{i}", [128, D], F32) for i in range(2)]
            m4work = []
            wr1 = sbt(s4, "wr1", [128, D], F32)
            wr2 = sbt(s4, "wr2", [128, D], F32)
            NB = 2
            oc = [sbt(s4, f"oc{i}", [128, 16 * 128], BF16) for i in range(NB)]
            xs_ = [sbt(s4, f"x4_{i}", [128, D], F32) for i in range(NB)]
            X1 = [sbt(s4, f"X1_{i}", [128, D], F32) for i in range(NB)]
            tFs = [sbt(s4, f"tF4_{i}", [128, D], F32) for i in range(2)]
            hns = [sbt(s4, f"hn4_{i}", [128, D], BF16) for i in range(2)]
            jks = [sbt(s4, f"jk4_{i}", [128, D], BF16) for i in range(2)]
            hTs = [sbt(s4, f"hT4_{i}", [128, D], BF16) for i in range(NB)]
            for k in range(16):
                dma(Wo.v(k * D, [(1, D)]), w_out[k * 128:(k + 1) * 128, :], f"wcast{k % 2}", writes=["Wo"], eng="pool")
            load_row(wr1, R_POST, D, "ld0", "wr1")
            load_row(wr2, R_FPRE, D, "ld1", "wr2")
            for cnd in (1, 0):
                load_mod(G1m[cnd], cnd, 2, "ld0", "G1_%d" % cnd)
                tt("dve", G1m[cnd].v(), G1m[cnd].v(), wr1.v(), ALU.mult, ["G1_%d" % cnd, "wr1"], ["G1_%d" % cnd])
                load_mod(G2m[cnd], cnd, 4, "ld1", "G2_%d" % cnd)
                stt("dve", G2m[cnd].v(), G2m[cnd].v(), 1.0, wr2.v(), ALU.add, ALU.mult, ["G2_%d" % cnd, "wr2"],
                    ["G2_%d" % cnd])
                load_mod(SH2m[cnd], cnd, 3, "ld0", "SH2_%d" % cnd)
            for sq in seqs:
                s = sq["s"]
                if s == 0:
                    for a in (0, 65 * 64):
                        dma(sq["hT2"][:, :, a:a + 64].rearrange("k p t -> p k t"), zero_b.v(0, [(64, 8), (1, 64)]),
                            "zst", reads=["zero_b"], writes=["hT2_0"])
                for c in range(sq["nch"]):
                    m4work.append((sq, c))

            def m4_A(sq, c, it):
                s = sq["s"]
                i3 = it % NB
                i2 = it % 2
                PM = PMs[i2]
                pmk = [f"PM{i2}a", f"PM{i2}b"]
                r0 = sq["base"] + c * 128
                dma(oc[i3].v(0, [(128, 16), (1, 128)]),
                    sq["oin"][:, :, c * 128:(c + 1) * 128].rearrange("k p t -> p k t"), f"ol{i3}",
                    reads=["oin_%d" % s, "oinY_%d_%d" % (s, c)], writes=[f"oc{i3}"])
                dma(xs_[i3].v(), x_all[r0:r0 + 128, :], f"xl{i3}", writes=[f"x4_{i3}"])
                mm([(PM.v(hf * 512, [(1, 512)]), oc[i3].v(k * 128, [(1, 128)]), Wo.v(k * D + hf * 512, [(1, 512)]),
                     k == 0, k == 15) for hf in range(2) for k in range(16)], [f"oc{i3}", "Wo"], pmk)

            def m4_B(sq, c, it):
                s = sq["s"]
                cnd = sq["cond"]
                i3 = it % NB
                i2 = it % 2
                PM = PMs[i2]
                pmk = [f"PM{i2}a", f"PM{i2}b"]
                r0 = sq["base"] + c * 128
                sc = 16 + 8 * i2
                act(jks[i2].v(0, [(1, 1024)]), PM.v(), AF.Square, pmk, [f"jk4_{i2}", "sm%d" % sc], scale=1.0 / 32.0,
                    accum=sm.v(sc, [(1, 1)]))
                rstd_from_ms(sc, "sm%d" % sc)
                x1 = X1[i3]
                stt("dve", x1.v(), PM.v(), sm.v(sc + 1, [(1, 1)]), G1m[cnd].v(), ALU.mult, ALU.mult,
                    pmk + ["sm%d" % (sc + 1), "G1_%d" % cnd], [f"X1_{i3}"])
                tt("dve", x1.v(), x1.v(), xs_[i3].v(), ALU.add, [f"X1_{i3}", f"x4_{i3}"], [f"X1_{i3}"])
                dma(x1_d[r0:r0 + 128, :], x1.v(), f"x1st{i3}", reads=[f"X1_{i3}"], writes=["x1_d"], eng="act")
                norm_mod_transpose(x1, f"X1_{i3}", G2m[cnd], "G2_%d" % cnd, SH2m[cnd], "SH2_%d" % cnd, tFs[i2],
                                   f"tF4_{i2}", hns[i2], f"hn4_{i2}", jks[i2], f"jk4_{i2}", hTs[i3], f"hT4_{i3}",
                                   sc + 4, PTs[0], "PT0")
                o2 = sq["h2off"] + c * 128
                dma(sq["hT2"][:, :, o2:o2 + 128].rearrange("k p t -> p k t"), hTs[i3].v(0, [(128, 8), (1, 128)]),
                    f"hst{i3}", reads=[f"hT4_{i3}"], writes=["hT2_%d" % s], eng="act")

            m4_A(m4work[0][0], m4work[0][1], 0)
            for i, (sq, c) in enumerate(m4work):
                if i + 1 < len(m4work):
                    m4_A(m4work[i + 1][0], m4work[i + 1][1], i + 1)
                m4_B(sq, c, i)
            P.flush(win=1.2)
        s3.close()
        RSTD_ON_ACT[0] = False

        with ExitStack() as s5:
            PA, PB, PC, PF = psum_set(s5, [([128, 1024], F32)] * 4)
            Wd = sbt(s5, "Wd", [128, 22 * D], BF16)
            G3 = sbt(s5, "G3", [128, D], F32)
            wr3 = sbt(s5, "wr3", [128, D], F32)
            hin2 = [sbt(s5, f"h2_{i}", [128, 8 * 576], BF16) for i in range(2)]
            NWU = 4
            wur = [sbt(s5, f"wur{i}", [128, 1024], BF16) for i in range(NWU)]
            dgf = [sbt(s5, f"dgf{i}", [128, 18 * 128], BF16) for i in range(2)]
            PgL = sbt(s5, "PgL", [128, 10 * 66], BF16)
            PvL = sbt(s5, "PvL", [128, 10 * 66], BF16)
            PgC = sbt(s5, "PgC", [128, 258], BF16)
            PvC = sbt(s5, "PvC", [128, 258], BF16)
            SV = sbt(s5, "SV", [128, 44 * 132], BF16)
            P.op("dve", lambda e: e.memset(SV.v(), 0.0), writes=["SV%d_%d" % (g_, j_) for g_ in range(2) for j_ in range(22)])
            sgl = sbt(s5, "sgl", [128, 512], F32)
            sgT = sbt(s5, "sgT", [128, 512], F32)
            sgV = sbt(s5, "sgV", [128, 512], F32)
            actTs = [sbt(s5, f"actT{i}", [128, 22 * 512], BF16) for i in range(2)]
            x1t = [sbt(s5, f"x1t{i}", [128, D], F32) for i in range(2)]
            Y = [sbt(s5, f"Y{i}", [128, D], F32) for i in range(1)]
            parts = [sbt(s5, f"part{i}", [128, 512], F32) for i in range(2)]
            for k in range(22):
                dma(Wd.v(k * D, [(1, D)]), w_down[k * 128:(k + 1) * 128, :], f"wcast{k % 2}", writes=["Wd"], eng="pool")
            for (b_, kk) in ((PgL, "Pg"), (PvL, "Pv"), (PgC, "Pg"), (PvC, "Pv")):
                P.op("dve", (lambda bb: (lambda e: e.memset(bb.v(), 0.0)))(b_), writes=[kk])
            load_row(wr3, R_FPOST, D, "ld0", "wr3")
            st8 = dict(cw=0, cdg=0, ih=0, iy=0, ia=0)

            def ffn_tile(sq, T):
                s = sq["s"]
                lat = (s == 0)
                NP = 576 if lat else 256
                TF = 512 if lat else 256
                taps = [(dr, dc) for dr in range(3) for dc in range(3)] if lat else [(1, dc) for dc in range(3)]
                nt_ = len(taps)
                Pbufs = (PgL, PvL) if lat else (PgC, PvC)
                hb = hin2[st8["ih"] % 2]
                hk = f"h2_{st8['ih'] % 2}"
                if lat:
                    NW = 576 if T == 0 else 512
                    t0_ = 64 if T == 0 else T * 512 + 128
                    dma(hb.v(0, [(NP, 8), (1, NW)]),
                        sq["hT2"][:, :, t0_:t0_ + NW].rearrange("k p t -> p k t"), f"hl{st8['ih'] % 2}",
                        reads=["hT2_%d" % s], writes=[hk])
                else:
                    dma(hb.v(0, [(NP, 8), (1, NP)]),
                        sq["hT2"][:, :, T * 512:T * 512 + NP].rearrange("k p t -> p k t"), f"hl{st8['ih'] % 2}",
                        reads=["hT2_%d" % s], writes=[hk])
                st8["ih"] += 1
                actT = actTs[st8["ia"] % 2]
                ak = f"actT{st8['ia'] % 2}"
                st8["ia"] += 1
                dgl = {}

                def U(j, gv):
                    if gv == 0:
                        dg = dgf[st8["cdg"] % 2]
                        dgk = f"dgf{st8['cdg'] % 2}"
                        st8["cdg"] += 1
                        tap0 = taps[0][0] * 3 + taps[0][1]
                        tt("dve", dg.v(0, [(nt_ * 128, 2), (128, nt_), (1, 128)]),
                           cst_b.v(IDF, [(0, 2), (0, nt_), (1, 128)]),
                           fmv.v(O_WFC + j * 9 + tap0, [(22 * 9, 2), (1, nt_), (0, 128)]), ALU.mult,
                           ["cst_b", "fmv"], [dgk])
                        dgl[j] = (dg, dgk)
                    Pbuf = Pbufs[gv]
                    pk = "Pg" if gv == 0 else "Pv"
                    PS, psk = (PA, "PA") if gv == 0 else (PB, "PB")
                    w = wur[st8["cw"] % NWU]
                    wk = f"wur{st8['cw'] % NWU}"
                    dma(w.v(), w_up_t[j + 22 * gv].rearrange("p k c -> p (k c)"), f"wr{st8['cw'] % NWU}",
                        reads=["w_up_t_%d" % k_ for k_ in range(8)], writes=[wk])
                    st8["cw"] += 1
                    if lat:
                        svk = "SV%d_%d" % (gv, j)
                        svo = (gv * 22 + j) * 132
                        act(Pbuf.v(0, [(1, 132)]), SV.v(svo, [(1, 132)]), AF.Copy, [svk], [pk])
                        if T == 0:
                            mm([(PS.v(0, [(1, 512)]), w.v(k * 128, [(1, 128)]), hb.v(k * NP, [(1, 512)]),
                                 k == 0, k == 7) for k in range(8)] +
                               [(PS.v(512, [(1, 64)]), w.v(k * 128, [(1, 128)]), hb.v(k * NP + 512, [(1, 64)]),
                                 k == 0, k == 7) for k in range(8)], [wk, hk], [psk + "0", psk + "1"])
                            act(Pbuf.v(1 * 66 + 1, [(66, 8), (1, 64)]), PS.v(0, [(64, 8), (1, 64)]),
                                AF.Copy, [psk + "0"], [pk])
                            act(Pbuf.v(9 * 66 + 1, [(1, 64)]), PS.v(512, [(1, 64)]), AF.Copy, [psk + "1"], [pk])
                        else:
                            mm([(PS.v(0, [(1, 512)]), w.v(k * 128, [(1, 128)]), hb.v(k * NP, [(1, 512)]),
                                 k == 0, k == 7) for k in range(8)], [wk, hk], [psk + "0"])
                            act(Pbuf.v(2 * 66 + 1, [(66, 8), (1, 64)]), PS.v(0, [(64, 8), (1, 64)]),
                                AF.Copy, [psk + "0"], [pk])
                        if T < 7:
                            act(SV.v(svo, [(1, 132)]), Pbuf.v(8 * 66, [(1, 132)]), AF.Copy, [pk], [svk])
                    else:
                        mm([(PS.v(0, [(1, 256)]), w.v(k * 128, [(1, 128)]), hb.v(k * NP, [(1, 256)]),
                             k == 0, k == 7) for k in range(8)], [wk, hk], [psk + "0"])
                        act(Pbuf.v(1, [(1, 256)]), PS.v(0, [(1, 256)]), AF.Copy, [psk + "0"], [pk])

                def C(j, gv):
                    dg, dgk = dgl[j]
                    Pbuf = Pbufs[gv]
                    pk = "Pg" if gv == 0 else "Pv"
                    specs = []
                    for ti, (dr, dc) in enumerate(taps):
                        if lat:
                            rv = Pbuf.v(dr * 66 + dc, [(66, 8), (1, 64)])
                            ov = PC.v(gv * 512, [(64, 8), (1, 64)])
                        else:
                            rv = Pbuf.v(dc, [(1, 256)])
                            ov = PC.v(gv * 512, [(1, 256)])
                        specs.append((ov, dg.v((gv * nt_ + ti) * 128, [(1, 128)]), rv, ti == 0, ti == nt_ - 1))
                    npe = nt_ - N_DVE_TAPS if lat else nt_
                    specs = [(o_, l_, r_, ti == 0, ti == npe - 1) for ti, (o_, l_, r_, _a, _b) in enumerate(specs[:npe])]
                    mm(specs, [dgk, pk], ["PC%d" % gv])
                    if lat and N_DVE_TAPS > 0:
                        prt = parts[gv]
                        prk = "part%d" % gv
                        wof = O_WFC + (j + 22 * gv) * 9
                        for q, ti in enumerate(range(npe, nt_)):
                            dr, dc = taps[ti]
                            rv = Pbuf.v(dr * 66 + dc, [(66, 8), (1, 64)])
                            if q == 0:
                                ts("dve", prt.v(0, [(64, 8), (1, 64)]), rv, fmv.v(wof + ti, [(1, 1)]), None,
                                   ALU.mult, None, [pk, "fmv"], [prk])
                            else:
                                stt("dve", prt.v(0, [(64, 8), (1, 64)]), rv, fmv.v(wof + ti, [(1, 1)]),
                                    prt.v(0, [(64, 8), (1, 64)]), ALU.mult, ALU.add, [pk, "fmv", prk], [prk])
                        if gv == 0:
                            tt("dve", prt.v(), PC.v(0, [(1, TF)]), prt.v(), ALU.add, ["PC0", prk], [prk])
                            silu_psum(sgl.v(0, [(1, TF)]), prt.v(), TF, sgT, sgV,
                                      fmvh.v(O_BFC + j, [(1, 1)]), 0.5, [prk, "fmvh"], "sgl", ("sgT", "sgV"))
                        else:
                            stt("dve", prt.v(), PC.v(512, [(1, TF)]), fmv.v(O_BFC + 22 + j, [(1, 1)]),
                                prt.v(), ALU.add, ALU.add, ["PC1", "fmv", prk], [prk])
                            tt("dve", actT.v(j * 512, [(1, TF)]), prt.v(), sgl.v(0, [(1, TF)]), ALU.mult,
                               [prk, "sgl"], [ak])
                    elif gv == 0:
                        silu_psum(sgl.v(0, [(1, TF)]), PC.v(0, [(1, TF)]), TF, sgT, sgV,
                                  fmvh.v(O_BFC + j, [(1, 1)]), 0.5, ["PC0", "fmvh"], "sgl", ("sgT", "sgV"))
                    else:
                        stt("dve", actT.v(j * 512, [(1, TF)]), PC.v(512, [(1, TF)]), fmv.v(O_BFC + 22 + j, [(1, 1)]),
                            sgl.v(0, [(1, TF)]), ALU.add, ALU.mult, ["PC1", "fmv", "sgl"], [ak])

                U(0, 0)
                U(0, 1)
                for j in range(22):
                    C(j, 0)
                    if j + 1 < 22:
                        U(j + 1, 0)
                    C(j, 1)
                    if j + 1 < 22:
                        U(j + 1, 1)
                for c in range(TF // 128):
                    i2 = st8["iy"] % 2
                    st8["iy"] += 1
                    r0 = sq["base"] + T * 512 + c * 128
                    dma(x1t[i2].v(), x1_d[r0:r0 + 128, :], f"xl{i2}", reads=["x1_d"], writes=[f"x1t{i2}"])
                    mm([(PF.v(hf * 512, [(1, 512)]), actT.v(j * 512 + c * 128, [(1, 128)]),
                         Wd.v(j * D + hf * 512, [(1, 512)]), j == 0, j == 21)
                        for hf in range(2) for j in range(22)], [ak, "Wd"], ["PF0", "PF1"])
                    y = Y[0]
                    act(y.v(), PF.v(), AF.Square, ["PF0", "PF1"], ["Y0", "sm12"], scale=1.0 / 32.0,
                        accum=sm.v(12, [(1, 1)]))
                    rstd_from_ms(12, "sm12")
                    stt("dve", y.v(), PF.v(), sm.v(13, [(1, 1)]), G3.v(), ALU.mult, ALU.mult,
                        ["PF0", "PF1", "sm13", "G3"], ["Y0"])
                    tt("dve", y.v(), y.v(), x1t[i2].v(), ALU.add, ["Y0", f"x1t{i2}"], ["Y0"])
                    dma(y_all[r0:r0 + 128, :], y.v(), f"yst{i2}", reads=["Y0"], writes=["y_all"], is_out=True, eng="act")

            last_cond = None
            for sq in seqs:
                if sq["cond"] != last_cond:
                    last_cond = sq["cond"]
                    load_mod(G3, sq["cond"], 5, "ld1", "G3")
                    tt("dve", G3.v(), G3.v(), wr3.v(), ALU.mult, ["G3", "wr3"], ["G3"])
                for T in range(8 if sq["s"] == 0 else 1):
                    ffn_tile(sq, T)
            P.flush(final=True, win=0.0)
    return nc


_CACHE = {}


def _consts():
    t = np.arange(128)
    ident = np.eye(128, dtype=np.float32)
    uinc = (t[:, None] <= t[None, :]).astype(np.float32)
    linc = (t[:, None] >= t[None, :]).astype(np.float32)
    lstr = (t[:, None] > t[None, :]).astype(np.float32)
    ustr = (t[:, None] < t[None, :]).astype(np.float32)
    ones = np.ones((128, 128), np.float32)
    return np.ascontiguousarray(np.concatenate([ident, uinc, linc, lstr, ustr, ones], axis=1))


def kernel(x_prompt, x_sample, state_ssd_fwd, state_ssd_bwd, c, c_ctx,
           w_ada, b_ada, norm_mix_pre, norm_mix_post, w_in, w_ssd_conv, b_ssd_conv,
           a_log_fwd, a_log_bwd, dt_bias_fwd, dt_bias_bwd, d_skip, ssd_norm,
           w_cf_conv, b_cf_conv, cf_ln_g, cf_ln_b, w_out, norm_ffn_pre, norm_ffn_post,
           w_ffn_up, w_ffn_conv, b_ffn_conv, w_ffn_down):
    f = lambda a: np.ascontiguousarray(np.asarray(a, dtype=np.float32))
    x_prompt, x_sample = f(x_prompt), f(x_sample)
    def fm(v, n):
        return f(v).reshape(n, 128).T
    wsc = f(w_ssd_conv)[0].reshape(3, 12, 128).transpose(2, 1, 0).reshape(128, 36)
    wcf = f(w_cf_conv)[0].reshape(31, 8, 128).transpose(2, 1, 0).reshape(128, 248)
    wfc = f(w_ffn_conv)[0].reshape(9, 44, 128).transpose(2, 1, 0).reshape(128, 396)
    fmv = np.ascontiguousarray(np.concatenate([
        wsc, fm(b_ssd_conv[0], 12), wcf, fm(b_cf_conv[0], 8), fm(cf_ln_g[0], 8), fm(cf_ln_b[0], 8),
        wfc, fm(b_ffn_conv[0], 44)], axis=1).astype(np.float32))
    rowv = np.ascontiguousarray(np.concatenate([
        f(norm_mix_pre)[0], f(norm_mix_post)[0], f(ssd_norm)[0], f(norm_ffn_pre)[0], f(norm_ffn_post)[0],
        f(d_skip)[0], f(dt_bias_fwd)[0], f(dt_bias_bwd)[0], f(a_log_fwd)[0], f(a_log_bwd)[0], f(b_ada)[0]])[None, :])
    cst = _consts()
    if "nc" not in _CACHE:
        _CACHE["nc"] = build_program()
    nc = _CACHE["nc"]
    wa, wi, wo, wu, wd = f(w_ada)[0], f(w_in)[0], f(w_out)[0], f(w_ffn_up)[0], f(w_ffn_down)[0]
    in_maps = []
    for i in range(8):
        xa = np.ascontiguousarray(np.concatenate([x_sample[i], x_prompt[4 * i:4 * i + 4].reshape(NCTX * CTX, D)], axis=0))
        cc = np.stack([f(c_ctx), f(c)[i]], axis=1)
        cin = np.ascontiguousarray(cc.reshape(8, 128, 2).transpose(1, 0, 2).reshape(128, 16))
        in_maps.append(dict(
            x_all=xa, st_f=f(state_ssd_fwd)[i, 0].reshape(1024, 128), st_b=f(state_ssd_bwd)[i, 0].reshape(1024, 128),
            c_in=cin, cst=cst, fmv=fmv, rowv=rowv, w_ada=wa, w_in=wi, w_out=wo, w_up=wu, w_down=wd))
    res = run_bass_kernel_spmd(nc, in_maps, core_ids=list(range(8)))
    y_prompt = np.zeros((32, CTX, D), np.float32)
    y_sample = np.zeros((8, LAT, D), np.float32)
    nsf = np.zeros((32, 1, 16, 64, 128), np.float32)
    nsb = np.zeros((32, 1, 16, 64, 128), np.float32)
    for i in range(8):
        r = res.results[i]
        ya = np.asarray(r["y_all"])
        y_sample[i] = ya[:LAT]
        y_prompt[4 * i:4 * i + 4] = ya[LAT:].reshape(NCTX, CTX, D)
        nsf[4 * i:4 * i + 4, 0] = np.asarray(r["ns_f"]).reshape(NCTX, 16, 64, 128)
        nsb[4 * i:4 * i + 4, 0] = np.asarray(r["ns_b"]).reshape(NCTX, 16, 64, 128)
    return (y_prompt, y_sample, nsf, nsb)
```

```python
import numpy as np
from contextlib import ExitStack
import concourse.bass as bass
import concourse.mybir as mybir
from concourse.bass_utils import run_bass_kernel_spmd

F32 = mybir.dt.float32
BF16 = mybir.dt.bfloat16
ALU = mybir.AluOpType
AF = mybir.ActivationFunctionType

D = 1024
LAT = 4096
CTX = 256
NCTX = 4
NTOK = LAT + NCTX * CTX
TL = 256
HALO = 16
EPS = 1e-6
USE_POOL_POW = True
RSTD_ON_ACT = [False]
SCHED_WIN = 0.6
DIAG31_ALL_DMA = True
CONV3_ON_DVE = True
N_DVE_TAPS = 1
PREP_VEC_ENG = "dve"
VEC_EXCL = True
PSUM_CANON = {"PDdt": "PD", "PDcum": "PD", "PDa": "PD"}
PSUM_KEYS = {"PD", "PB0", "PB1", "PA0", "PA1", "PC0", "PC1", "PT", "PT0", "PT1", "PT2", "PT3",
             "PM0a", "PM0b", "PM1a", "PM1b", "PF0", "PF1", "PY0", "PY1", "PS0", "PS1", "fPT0", "fPT1"}
NH = 16
IN_COLS = 4640
DFF = 2816
O_WSC, O_BSC, O_WCF, O_BCF, O_LNG, O_LNB, O_WFC, O_BFC, NFMV = 0, 36, 48, 296, 304, 312, 320, 716, 760
R_PRE, R_POST, R_SSD, R_FPRE, R_FPOST, R_DSK, R_DTB, R_ALOG, R_BADA, NROW = (
    0, 1024, 2048, 3072, 4096, 5120, 5136, 5168, 5200, 5200 + 6144)


class Tl:
    def __init__(self, h, shape):
        self.h = h
        self.P = shape[0]
        self.F = int(np.prod(shape[1:]))

    def v(self, off=0, dims=None, p0=0, np_=None):
        if dims is None:
            dims = [(1, self.F - off)]
        if np_ is None:
            np_ = self.P - p0
        return bass.AP(self.h, p0 * self.F + off, [[self.F, np_]] + [[s, n] for (s, n) in dims])


class Prog:
    def __init__(self, nc, es):
        self.nc = nc
        self.es = es
        self.ops = []
        self.key_w = {}
        self.key_r = {}
        self.eng_sem = {}
        for e in ("pe", "act", "dve", "pool"):
            self.eng_sem[e] = es.enter_context(nc.semaphore("sem_" + e))
        self.eng_cnt = {e: 0 for e in self.eng_sem}
        self.streams = {}
        self.known = {e: {} for e in ("pe", "act", "dve", "pool", "sp")}
        self.emitted = 0
        self.out_streams = set()

    def op(self, eng, fn, reads=(), writes=(), dma=None, is_out=False, cost=0.3, lat=2.5):
        reads = [PSUM_CANON.get(k, k) for k in reads]
        writes = [PSUM_CANON.get(k, k) for k in writes]
        writes = writes + [k for k in reads if k in PSUM_KEYS and k not in writes]
        idx = len(self.ops)
        deps = set()
        for k in reads:
            if k in self.key_w:
                deps.add(self.key_w[k])
        for k in writes:
            if k in self.key_w:
                deps.add(self.key_w[k])
            for r in self.key_r.get(k, ()):
                deps.add(r)
        rec = dict(eng=eng, fn=fn, deps=deps, dma=dma, sig=None, users=0, cost=cost, lat=lat)
        if dma is not None:
            if dma not in self.streams:
                self.streams[dma] = [self.es.enter_context(self.nc.semaphore("dq_" + dma)), 0, None]
            st = self.streams[dma]
            if st[2] is not None:
                deps.add(st[2])
            st[1] += 16
            st[2] = idx
            rec["sig"] = (st[0], st[1])
            if is_out:
                self.out_streams.add(dma)
        deps.discard(idx)
        self.ops.append(rec)
        for d in deps:
            self.ops[d]["users"] += 1
        for k in reads:
            self.key_r.setdefault(k, []).append(idx)
        for k in writes:
            self.key_w[k] = idx
            self.key_r[k] = []
        return idx

    def flush(self, final=False, win=None):
        nc = self.nc
        ops = self.ops
        lo = self.emitted
        hi = len(ops)
        n = hi - lo
        succ = [[] for _ in range(n)]
        ndep = [0] * n
        for i in range(lo, hi):
            for d in ops[i]["deps"]:
                if d >= lo:
                    succ[d - lo].append(i - lo)
                    ndep[i - lo] += 1
        dur = [0.0] * n
        for i in range(n):
            r = ops[lo + i]
            dur[i] = r["lat"] if r["dma"] is not None else r["cost"]
        blev = [0.0] * n
        for i in range(n - 1, -1, -1):
            b = 0.0
            for s_ in succ[i]:
                if blev[s_] > b:
                    b = blev[s_]
            blev[i] = b + dur[i]
        engs = ("pe", "act", "dve", "pool", "sp")
        free = {e: 0.0 for e in engs}
        ready = {e: [] for e in engs}
        dready = [0.0] * n
        finish = [0.0] * n
        for i in range(n):
            if ndep[i] == 0:
                ready[ops[lo + i]["eng"]].append(i)
        nsched = 0
        order = {e: [] for e in engs}
        WIN = SCHED_WIN if win is None else win
        while nsched < n:
            best = None
            for e in engs:
                rl = ready[e]
                if not rl:
                    continue
                f = free[e]
                if VEC_EXCL and e in ("dve", "pool"):
                    f = max(free["dve"], free["pool"])
                est = min(max(f, dready[i]) for i in rl)
                cand = None
                for i in rl:
                    st_ = max(f, dready[i])
                    if st_ <= est + WIN:
                        key = (-blev[i], i)
                        if cand is None or key < cand[0]:
                            cand = (key, i, st_)
                if best is None or cand[2] < best[2]:
                    best = (e, cand[1], cand[2])
            e, i, st_ = best
            ready[e].remove(i)
            r = ops[lo + i]
            if r["dma"] is not None:
                free[e] = st_ + 0.12
                finish[i] = st_ + r["lat"]
            else:
                free[e] = st_ + r["cost"]
                finish[i] = free[e]
                if VEC_EXCL and e in ("dve", "pool"):
                    free["dve"] = max(free["dve"], free[e])
                    free["pool"] = max(free["pool"], free[e])
            order[e].append(lo + i)
            nsched += 1
            for s_ in succ[i]:
                same_pe = (e == "pe" and ops[lo + s_]["eng"] == "pe" and r["dma"] is None)
                t_ = finish[i] + (0.0 if same_pe else 0.2)
                if t_ > dready[s_]:
                    dready[s_] = t_
                ndep[s_] -= 1
                if ndep[s_] == 0:
                    ready[ops[lo + s_]["eng"]].append(s_)
        self.sim_time = max(finish) if n else 0.0
        print("[sched] block ops=%d simulated_us=%.0f" % (n, self.sim_time))
        per = order
        for e in ("pe", "act", "dve", "pool"):
            for i in per[e]:
                r = ops[i]
                if r["dma"] is None:
                    self.eng_cnt[e] += 1
                    r["sig"] = (self.eng_sem[e], self.eng_cnt[e])
        self.emitted = hi
        prog = self

        def run(e, name):
            known = prog.known[name]
            for i in per[name]:
                r = ops[i]
                need = {}
                for d in r["deps"]:
                    dr = ops[d]
                    if dr["eng"] == "pe" and name == "pe" and dr["dma"] is None:
                        continue
                    sem, val = dr["sig"]
                    key = id(sem)
                    if known.get(key, 0) >= val:
                        continue
                    if key not in need or need[key][1] < val:
                        need[key] = (sem, val)
                for key in sorted(need, key=lambda k_: need[k_][1]):
                    sem, val = need[key]
                    e.wait_ge(sem, val)
                    known[key] = val
                ins = r["fn"](e)
                if r["sig"] is not None:
                    if r["dma"] is not None:
                        ins.then_inc(r["sig"][0], 16)
                    else:
                        ins.then_inc(r["sig"][0], 1)
            if name == "sp":
                for s in prog.streams.values():
                    if s[1] > 0 and known.get(id(s[0]), 0) < s[1]:
                        e.wait_ge(s[0], s[1])
                        known[id(s[0])] = s[1]

        with nc.Block() as block:
            @block.tensor
            def _(e):
                run(e, "pe")

            @block.scalar
            def _(e):
                run(e, "act")

            @block.vector
            def _(e):
                run(e, "dve")

            @block.gpsimd
            def _(e):
                run(e, "pool")

            @block.sync
            def _(e):
                run(e, "sp")


def build_program():
    nc = bass.Bass("TRN2", target_bir_lowering=False)

    def din(name, shape, dt=F32):
        return nc.dram_tensor(name, list(shape), dt, kind="ExternalInput").ap()

    def dout(name, shape, dt=F32):
        return nc.dram_tensor(name, list(shape), dt, kind="ExternalOutput").ap()

    def dscr(name, shape, dt):
        return nc.dram_tensor(name, list(shape), dt).ap()

    x_all = din("x_all", [NTOK, D])
    st_f = din("st_f", [1024, 128])
    st_b = din("st_b", [1024, 128])
    c_in = din("c_in", [128, 16])
    cst = din("cst", [128, 768])
    fmv_d = din("fmv", [128, NFMV])
    rowv = din("rowv", [1, NROW])
    w_ada = din("w_ada", [D, 6 * D])
    w_in = din("w_in", [D, IN_COLS])
    w_out = din("w_out", [2 * D, D])
    w_up = din("w_up", [D, 2 * DFF])
    w_down = din("w_down", [DFF, D])
    y_all = dout("y_all", [NTOK, D])
    ns_f = dout("ns_f", [NCTX, 1024, 128])
    ns_b = dout("ns_b", [NCTX, 1024, 128])

    mod_d = dscr("mod_d", [2, 6 * D], F32)
    w_in_t = dscr("w_in_t", [28, 128, 8, 128], BF16)
    w_up_t = dscr("w_up_t", [44, 128, 8, 128], BF16)
    dg31_d = dscr("dg31_d", [8, 128, 31 * 128], BF16)
    x1_d = dscr("x1_d", [NTOK, D], F32)
    seqs = []
    for s in range(1 + NCTX):
        L = LAT if s == 0 else CTX
        base = 0 if s == 0 else LAT + (s - 1) * CTX
        sq = dict(s=s, L=L, base=base, cond=1 if s == 0 else 0, nt=L // TL, nch=L // 128)
        sq["hT1"] = dscr(f"hT1_{s}", [8, 128, L + 2 * HALO], BF16)
        sq["oin"] = dscr(f"oin_{s}", [16, 128, L], BF16)
        sq["rec"] = dscr(f"rec_{s}", [L // 128, 128, 3328], BF16)
        sq["e1"] = dscr(f"e1_{s}", [L // 128, 128, 64], F32)
        sq["cd"] = dscr(f"cd_{s}", [L // 128, 128, 256], BF16)
        if s == 0:
            sq["hT2"] = dscr(f"hT2_{s}", [8, 128, 66 * 64], BF16)
            sq["h2off"] = 64
        else:
            sq["hT2"] = dscr(f"hT2_{s}", [8, 128, L], BF16)
            sq["h2off"] = 0
        seqs.append(sq)

    with ExitStack() as es:
        P = Prog(nc, es)

        uid = [0]

        def sbt(st, name, shape, dt):
            uid[0] += 1
            return Tl(st.enter_context(nc.sbuf_tensor("sb%d_%s" % (uid[0], name), list(shape), dt)), shape)

        def pst(st, name, shape, dt):
            return Tl(st.enter_context(nc.psum_tensor("ps_" + name, list(shape), dt)), shape)

        pid = [0]

        def psum_set(st, spec):
            out = []
            for (shape, dt) in spec:
                pid[0] += 1
                out.append(pst(st, "p%d" % pid[0], shape, dt))
            return out

        cst_f = sbt(es, "cst_f", [128, 768], F32)
        cst_b = sbt(es, "cst_b", [128, 768], BF16)
        fmv = sbt(es, "fmv", [128, NFMV], F32)
        diag3 = sbt(es, "diag3", [128, 36 * 128], BF16)
        WZ = sbt(es, "WZ", [128, 8 * 1056], BF16)
        zero_b = sbt(es, "zero_b", [128, 1024], BF16)
        a_b = sbt(es, "a_b", [128, 32], F32)
        dtb_b = sbt(es, "dtb_b", [128, 32], F32)
        sm = sbt(es, "sm", [128, 64], F32)
        nhalf = sbt(es, "nhalf", [128, 8], F32)
        fmvh = sbt(es, "fmvh", [128, NFMV], F32)

        IDF, UINC, LINC, LSTR, USTR, ONES = [i * 128 for i in range(6)]

        def fsz(ap):
            n_ = 1
            for (s_, c_) in list(ap.ap)[1:]:
                n_ *= c_
            return n_

        def dma(fn_out, fn_in, stream, reads=(), writes=(), eng="sp", is_out=False):
            def f(e):
                return e.dma_start(out=fn_out, in_=fn_in)
            try:
                nb = list(fn_out.ap)[0][1] * fsz(fn_out) * (2 if fn_out.dtype == BF16 else 4)
            except Exception:
                nb = 1 << 18
            lat = 2.2 + nb / 120000.0
            return P.op(eng, f, reads=reads, writes=writes, dma=stream, is_out=is_out, lat=lat)

        def act(out, in_, func, reads, writes, bias=None, scale=None, accum=None):
            kw = {}
            if bias is not None:
                kw["bias"] = bias
            if scale is not None:
                kw["scale"] = scale
            if accum is not None:
                kw["accum_out"] = accum

            def f(e):
                return e.activation(out=out, in_=in_, func=func, **kw)
            return P.op("act", f, reads=reads, writes=writes, cost=0.2 + fsz(out) / 1150.0)

        def vcost(eng, out):
            n_ = fsz(out)
            if eng == "pool":
                return 0.4 + n_ / 600.0
            return 0.17 + n_ / 960.0

        def tt(eng, out, in0, in1, op, reads, writes):
            def f(e):
                return e.tensor_tensor(out=out, in0=in0, in1=in1, op=op)
            return P.op(eng, f, reads=reads, writes=writes, cost=vcost(eng, out))

        def stt(eng, out, in0, scalar, in1, op0, op1, reads, writes):
            def f(e):
                return e.scalar_tensor_tensor(out=out, in0=in0, scalar=scalar, in1=in1, op0=op0, op1=op1)
            return P.op(eng, f, reads=reads, writes=writes, cost=vcost(eng, out))

        def ts(eng, out, in0, s1, s2, op0, op1, reads, writes):
            def f(e):
                if s2 is None:
                    return e.tensor_scalar(out=out, in0=in0, scalar1=s1, scalar2=None, op0=op0)
                return e.tensor_scalar(out=out, in0=in0, scalar1=s1, scalar2=s2, op0=op0, op1=op1)
            return P.op(eng, f, reads=reads, writes=writes, cost=vcost(eng, out))

        def cp(eng, out, in_, reads, writes):
            def f(e):
                return e.tensor_copy(out=out, in_=in_)
            return P.op(eng, f, reads=reads, writes=writes, cost=vcost(eng, out))

        def mm(specs, reads, writes):
            def f(e):
                ins = None
                for (o, l, r, st, sp) in specs:
                    ins = e.matmul(o, lhsT=l, rhs=r, start=st, stop=sp)
                return ins
            c_ = 0.0
            for (o, l, r, st, sp) in specs:
                c_ += max(0.1, 0.012 + fsz(r) / 2400.0) * (4.0 if l.dtype == F32 else 1.0)
            return P.op("pe", f, reads=reads, writes=writes, cost=c_)

        def tr(specs, reads, writes):
            def f(e):
                ins = None
                for (o, i, idn) in specs:
                    ins = e.transpose(out=o, in_=i, identity=idn)
                return ins
            return P.op("pe", f, reads=reads, writes=writes, cost=0.11 * len(specs))

        def rstd_from_ms(col, reads_key):
            if USE_POOL_POW and not RSTD_ON_ACT[0]:
                ts("pool", sm.v(col + 2, [(1, 1)]), sm.v(col, [(1, 1)]), EPS, None, ALU.add, None,
                   [reads_key], ["sm%d" % (col + 2)])
                tt("pool", sm.v(col + 1, [(1, 1)]), sm.v(col + 2, [(1, 1)]), nhalf.v(0, [(1, 1)]), ALU.pow,
                   ["sm%d" % (col + 2), "nhalf"], ["sm%d" % (col + 1)])
                return
            act(sm.v(col + 2, [(1, 1)]), sm.v(col, [(1, 1)]), AF.Ln, [reads_key], ["sm%d" % (col + 2)], bias=EPS, scale=1.0)
            act(sm.v(col + 1, [(1, 1)]), sm.v(col + 2, [(1, 1)]), AF.Exp, ["sm%d" % (col + 2)], ["sm%d" % (col + 1)], scale=-0.5)

        def silu_psum(out, ps, n, tT, tV, hb, hs, rkeys, wkey, tkeys):
            kw = {} if hb is None else {"bias": hb}
            act(tT.v(0, [(1, n)]), ps, AF.Tanh, rkeys, [tkeys[0]], scale=hs, **kw)
            act(tV.v(0, [(1, n)]), ps, AF.Identity, rkeys, [tkeys[1]], scale=hs, **kw)
            stt("dve", out, tT.v(0, [(1, n)]), 1.0, tV.v(0, [(1, n)]), ALU.add, ALU.mult, list(tkeys), [wkey])

        ident_b = cst_b.v(IDF, [(1, 128)])
        ident_f = cst_f.v(IDF, [(1, 128)])

        with ExitStack() as s0:
            (PD0,) = psum_set(s0, [([128, 512], F32)])
            PTs = psum_set(s0, [([128, 1024], BF16)] * 4)
            dma(cst_f.v(), cst, "ld0", writes=["cst_f"])
            dma(fmv.v(), fmv_d, "ld1", writes=["fmv"])
            cp("dve", cst_b.v(), cst_f.v(), ["cst_f"], ["cst_b"])
            P.op("dve", lambda e: e.memset(zero_b.v(), 0.0), writes=["zero_b"])
            P.op("dve", lambda e: e.memset(nhalf.v(), -0.5), writes=["nhalf"])
            ts("dve", fmvh.v(), fmv.v(), 0.5, None, ALU.mult, None, ["fmv"], ["fmvh"])
            dma(a_b.v(), rowv[:, R_ALOG:R_ALOG + 32].partition_broadcast(128), "ld0", writes=["a_b"])
            dma(dtb_b.v(), rowv[:, R_DTB:R_DTB + 32].partition_broadcast(128), "ld1", writes=["dtb_b"])
            act(a_b.v(), a_b.v(), AF.Exp, ["a_b"], ["a_b"])
            ts("dve", a_b.v(), a_b.v(), -1.0, None, ALU.mult, None, ["a_b"], ["a_b"])
            tt("dve", diag3.v(0, [(128, 36), (1, 128)]), cst_b.v(IDF, [(0, 36), (1, 128)]),
               fmv.v(O_WSC, [(1, 36), (0, 128)]), ALU.mult, ["cst_b", "fmv"], ["diag3"])

            stgs = [sbt(s0, f"stg{i}", [128, 5632], BF16) for i in range(2)]
            dgs = [sbt(s0, f"dgs{i}", [128, 31 * 128], BF16) for i in range(2)]
            NWA = 3
            wad = [sbt(s0, f"wad{i}", [128, 8 * 512], F32) for i in range(NWA)]
            scv = sbt(s0, "scv", [128, 16], F32)
            badas = [sbt(s0, f"bada{i}", [2, 512], F32) for i in range(2)]
            modss = [sbt(s0, f"modsb{i}", [2, 512], F32) for i in range(2)]
            dma(scv.v(), c_in, "ld0", writes=["scv"])
            act(scv.v(), scv.v(), AF.Silu, ["scv"], ["scv"])
            for j in range(12):
                wb = wad[j % NWA]
                dma(badas[j % 2].v(), rowv[:, R_BADA + j * 512:R_BADA + (j + 1) * 512].partition_broadcast(2),
                    f"bal{j % 2}", writes=[f"bada{j % 2}"])
                dma(wb.v(0, [(512, 8), (1, 512)]),
                    w_ada[:, j * 512:(j + 1) * 512].rearrange("(k p) n -> p k n", p=128),
                    f"wad{j % NWA}", writes=[f"wad{j % NWA}"])
                mm([(PD0.v(0, [(1, 512)], 0, 2), scv.v(k * 2, [(1, 2)]), wb.v(k * 512, [(1, 512)]), k == 0, k == 7)
                    for k in range(8)], ["scv", f"wad{j % NWA}"], ["PD"])
                tt("dve", modss[j % 2].v(), PD0.v(0, [(1, 512)], 0, 2), badas[j % 2].v(),
                   ALU.add, ["PD", f"bada{j % 2}"], [f"modsb{j % 2}"])
                dma(mod_d[:, j * 512:(j + 1) * 512], modss[j % 2].v(), f"mst{j % 2}", reads=[f"modsb{j % 2}"],
                    writes=["mod_d"])
            for m in range(8):
                b = dgs[m % 2]
                tt("pool", b.v(0, [(128, 31), (1, 128)]), cst_b.v(IDF, [(0, 31), (1, 128)]),
                   fmv.v(O_WCF + m * 31, [(1, 31), (0, 128)]), ALU.mult, ["cst_b", "fmv"], [f"dgs{m % 2}"])
                dma(dg31_d[m], b.v(), f"dgst{m % 2}", reads=[f"dgs{m % 2}"], writes=["dg31_d"], eng="pool")
            ws = 0
            f32s = [sbt(s0, f"f32s{i}", [128, 1408], F32) for i in range(2)]
            qc = [0]

            def stage_rows(src_rows, ncols, stg, sk):
                qn = ncols // 4
                for q in range(4):
                    fb = f32s[qc[0] % 2]
                    fk = f"f32s{qc[0] % 2}"
                    dma(fb.v(0, [(1, qn)]), src_rows[:, q * qn:(q + 1) * qn], f"wf{qc[0] % 2}", writes=[fk])
                    act(stg.v(q * qn, [(1, qn)]), fb.v(0, [(1, qn)]), AF.Copy, [fk], [sk])
                    qc[0] += 1

            for k in range(8):
                stg = stgs[ws % 2]
                sk = f"stg{ws % 2}"
                stage_rows(w_in[k * 128:(k + 1) * 128, :], IN_COLS, stg, sk)
                dma(w_in_t[0:12, :, k, :].rearrange("m p c -> p m c"), stg.v(1024, [(128, 12), (1, 128)]),
                    f"wsta{ws % 2}", reads=[sk], writes=["w_in_ta_%d" % k], eng="act")
                dma(w_in_t[12:28, :, k, :].rearrange("m p c -> p m c"), stg.v(2592, [(128, 16), (1, 128)]),
                    f"wstb{ws % 2}", reads=[sk], writes=["w_in_tb_%d" % k], eng="act")
                cp("pool", WZ.v(k * 1056, [(1, 1024)]), stg.v(0, [(1, 1024)]), [sk], ["WZ"])
                cp("pool", WZ.v(k * 1056 + 1024, [(1, 32)]), stg.v(2560, [(1, 32)]), [sk], ["WZ"])
                ws += 1
            for k in range(8):
                stg = stgs[ws % 2]
                sk = f"stg{ws % 2}"
                stage_rows(w_up[k * 128:(k + 1) * 128, :], 2 * DFF, stg, sk)
                dma(w_up_t[:, :, k, :].rearrange("m p c -> p m c"), stg.v(0, [(128, 44), (1, 128)]),
                    f"wsta{ws % 2}", reads=[sk], writes=["w_up_t_%d" % k], eng="act")
                ws += 1

            def load_mod(dst, cond, idx, stream, key):
                dma(dst.v(), mod_d[cond:cond + 1, idx * D:(idx + 1) * D].partition_broadcast(128), stream,
                    reads=["mod_d"], writes=[key])

            def load_row(dst, off, n, stream, key):
                dma(dst.v(0, [(1, n)]), rowv[:, off:off + n].partition_broadcast(128), stream, writes=[key])

            def norm_mod_transpose(xin, xin_key, G, G_key, SH, SH_key, tmpF, tmpF_key, hn, hn_key, junk, junk_key,
                                   hT, hT_key, smcol, PTt, ptk):
                act(junk.v(0, [(1, 1024)]), xin.v(), AF.Square, [xin_key], [junk_key, "sm%d" % smcol],
                    scale=1.0 / 32.0, accum=sm.v(smcol, [(1, 1)]))
                rstd_from_ms(smcol, "sm%d" % smcol)
                stt("dve", tmpF.v(), xin.v(), sm.v(smcol + 1, [(1, 1)]), G.v(), ALU.mult, ALU.mult,
                    [xin_key, "sm%d" % (smcol + 1), G_key], [tmpF_key])
                tt("dve", hn.v(0, [(1, 1024)]), tmpF.v(), SH.v(), ALU.add, [tmpF_key, SH_key], [hn_key])
                if hT is not None:
                    norm_mod_B(hn, hn_key, hT, hT_key, PTt, ptk)

            def norm_mod_B(hn, hn_key, hT, hT_key, PTt, ptk):
                tr([(PTt.v(k * 128, [(1, 128)]), hn.v(k * 128, [(1, 128)]), ident_b) for k in range(8)],
                   [hn_key, "cst_b"], [ptk])
                act(hT.v(0, [(1, 1024)]), PTt.v(), AF.Copy, [ptk], [hT_key])

            Gm = [sbt(s0, f"G{i}", [128, D], F32) for i in range(2)]
            SHm = [sbt(s0, f"SH{i}", [128, D], F32) for i in range(2)]
            wr = sbt(s0, "wr", [128, D], F32)
            NB = 3
            m1work = []
            xs_ = [sbt(s0, f"xs{i}", [128, D], F32) for i in range(NB)]
            tFs = [sbt(s0, f"tF{i}", [128, D], F32) for i in range(NB)]
            hns = [sbt(s0, f"hn{i}", [128, D], BF16) for i in range(NB)]
            jks = [sbt(s0, f"jk{i}", [128, D], BF16) for i in range(NB)]
            hTs = [sbt(s0, f"hTs{i}", [128, D], BF16) for i in range(NB)]
            load_row(wr, R_PRE, D, "ld0", "wr")
            for cnd in (1, 0):
                load_mod(Gm[cnd], cnd, 1, "ld0", "G%d" % cnd)
                load_mod(SHm[cnd], cnd, 0, "ld1", "SH%d" % cnd)
                stt("dve", Gm[cnd].v(), Gm[cnd].v(), 1.0, wr.v(), ALU.add, ALU.mult, ["G%d" % cnd, "wr"],
                    ["G%d" % cnd])
            for sq in seqs:
                L = sq["L"]
                h1 = sq["hT1"]
                for (a, b_) in ((0, HALO), (L + HALO, L + 2 * HALO)):
                    dma(h1[:, :, a:b_].rearrange("k p t -> p k t"), zero_b.v(0, [(HALO, 8), (1, HALO)]),
                        "zst", reads=["zero_b"], writes=["hT1_%d_z" % sq["s"]])
                for c in range(sq["nch"]):
                    m1work.append((sq, c))
            def m1_A(sq, c, it):
                i3 = it % NB
                r0 = sq["base"] + c * 128
                dma(xs_[i3].v(), x_all[r0:r0 + 128, :], f"xl{i3}", writes=[f"xs{i3}"])
                norm_mod_transpose(xs_[i3], f"xs{i3}", Gm[sq["cond"]], "G%d" % sq["cond"], SHm[sq["cond"]],
                                   "SH%d" % sq["cond"], tFs[i3], f"tF{i3}", hns[i3], f"hn{i3}",
                                   jks[i3], f"jk{i3}", None, None, 16 + 4 * i3, None, None)

            def m1_B(sq, c, it):
                i3 = it % NB
                norm_mod_B(hns[i3], f"hn{i3}", hTs[i3], f"hTs{i3}", PTs[it % 4], f"PT{it % 4}")
                dma(sq["hT1"][:, :, HALO + c * 128:HALO + (c + 1) * 128].rearrange("k p t -> p k t"),
                    hTs[i3].v(0, [(128, 8), (1, 128)]), f"hst{i3}", reads=[f"hTs{i3}"],
                    writes=["hT1_%d_%d" % (sq["s"], c)], eng="act")

            for i, (sq, c) in enumerate(m1work):
                m1_A(sq, c, i)
                if i >= 1:
                    m1_B(m1work[i - 1][0], m1work[i - 1][1], i - 1)
            m1_B(m1work[-1][0], m1work[-1][1], len(m1work) - 1)
            P.flush(win=0.3)

        TLH = TL + 2 * HALO

        def interleave(gens):
            gens = [g for g in gens if g is not None]
            while gens:
                for g in list(gens):
                    try:
                        next(g)
                    except StopIteration:
                        gens.remove(g)

        def mixer_phase(pass2):
            with ExitStack() as s2:
                PA, PBX, PC = psum_set(s2, [([128, 1024], F32)] * 3)
                (PD,) = psum_set(s2, [([128, 512], F32)])
                (PT,) = psum_set(s2, [([128, 1024], BF16)])
                hin = [sbt(s2, f"hin{i}", [128, 8 * TLH], BF16) for i in range(2)]
                NWR = 6
                wrg = [sbt(s2, f"wrg{i}", [128, 1024], BF16) for i in range(NWR)]
                xbc_pre = sbt(s2, "xbc_pre", [128, 12 * TLH], BF16)
                xbc_cs = [sbt(s2, f"xbc_c{i}", [128, 12 * TL], BF16) for i in range(3)]
                S_run = sbt(s2, "S_run", [128, D], F32)
                Sbf = [sbt(s2, "Sbf0", [128, D], BF16)]
                xs_tm = sbt(s2, "xs_tm", [128, D], BF16)
                F1 = sbt(s2, "F1", [128, D], F32)
                F2 = sbt(s2, "F2", [128, D], F32)
                stTs = [sbt(s2, f"stT{i}", [128, TL], F32) for i in range(2)]
                c3s = [sbt(s2, f"c3_{i}", [128, TL], F32) for i in range(2)]
                stVs = [sbt(s2, f"stV{i}", [128, TL], F32) for i in range(2)]
                NCL = TL // 128
                H_dts = [[sbt(s2, f"Hdts{i}{c}", [128, 256], F32) for c in range(NCL)] for i in range(2)]
                REC = [[sbt(s2, f"REC{i}{c}", [128, 3328], BF16) for c in range(NCL)] for i in range(2)]
                H_xw = [[sbt(s2, f"Hxw{i}{c}", [128, D], BF16) for c in range(NCL)] for i in range(2)]
                DT, DA, CUM, E1, D2, WW = 0, 32, 64, 128, 192, 224
                XBP_K = ["xbc_pre"]

                def xbc_pre_v(off, dims):
                    return xbc_pre.v(off, dims)

                if pass2:
                    H_yd = [[sbt(s2, f"Hyd{i}{c}", [128, D], BF16) for c in range(NCL)] for i in range(2)]
                    ar1 = sbt(s2, "ar1", [128, 4096], BF16)
                    MT = sbt(s2, "MT", [128, 2048], BF16)
                    usq = sbt(s2, "usq", [128, 8 * TL], BF16)
                    u_pre = sbt(s2, "u_pre", [128, 8 * TLH], BF16)
                    u_cs = [sbt(s2, f"u_c{i}", [128, 8 * TL], BF16) for i in range(1)]
                    sg = sbt(s2, "sg", [128, TLH], F32)
                    lnA = sbt(s2, "lnA", [128, TL], F32)
                    lnB = sbt(s2, "lnB", [128, TL], F32)
                    lnT = sbt(s2, "lnT", [128, TL], F32)
                    dgr = [sbt(s2, f"dgr{i}", [128, 31 * 128], BF16) for i in range(2)]
                    Db = sbt(s2, "Db", [128, D], BF16)
                    dsk = sbt(s2, "dsk", [128, 16], F32)
                    xdt = sbt(s2, "xdt", [128, D], BF16)
                    xsD = sbt(s2, "xsD", [128, D], BF16)
                    cbm = sbt(s2, "cbm", [128, 512], BF16)
                    try:
                        print("SBUF bytes remaining in mixer pass2:", nc.sbuf_bytes_remaining())
                    except Exception as _e:
                        print("sbuf_bytes_remaining unavailable", _e)
                    load_row(dsk, R_DSK, 16, "ld1", "dsk")
                    cp("dve", Db.v(0, [(64, 16), (1, 64)]), dsk.v(0, [(1, 16), (0, 64)]), ["dsk"], ["Db"])

                cnt = dict(w=0, d=0, y=0, sb=0)

                def wload(ci):
                    i = cnt["w"] % NWR
                    cnt["w"] += 1
                    dma(wrg[i].v(), w_in_t[ci].rearrange("p k c -> p (k c)"), f"wr{i}",
                        reads=["w_in_ta_%d" % k_ for k_ in range(8)] + ["w_in_tb_%d" % k_ for k_ in range(8)],
                        writes=[f"wrg{i}"])
                    return wrg[i], f"wrg{i}"

                def prep_chunk(hb, hkey, xbc_c, xck, bi, cl):
                    dts = H_dts[bi][cl]
                    kd = f"Hdts{bi}{cl}"
                    t0 = HALO + cl * 128
                    mm([(PD.v(256, [(1, 32)]), hb.v(k * TLH + t0, [(1, 128)]), WZ.v(k * 1056 + 1024, [(1, 32)]),
                         k == 0, k == 7) for k in range(8)], [hkey, "WZ"], ["PDdt"])
                    tt("dve", dts.v(D2, [(1, 32)]), PD.v(256, [(1, 32)]), dtb_b.v(), ALU.add, ["PDdt", "dtb_b"], [kd])
                    act(dts.v(D2, [(1, 32)]), dts.v(D2, [(1, 32)]), AF.Exp, [kd], [kd])
                    act(dts.v(DT, [(1, 32)]), dts.v(D2, [(1, 32)]), AF.Ln, [kd], [kd], bias=1.0, scale=1.0)
                    tt("dve", dts.v(DA, [(1, 32)]), dts.v(DT, [(1, 32)]), a_b.v(), ALU.mult, [kd, "a_b"], [kd])
                    mm([(PD.v(320, [(1, 16)]), cst_f.v(UINC, [(1, 128)]), dts.v(DA, [(1, 16)]), True, True),
                        (PD.v(336, [(1, 16)]), cst_f.v(LINC, [(1, 128)]), dts.v(DA + 16, [(1, 16)]), True, True),
                        (PD.v(352, [(1, 32)]), cst_f.v(ONES, [(1, 128)]), dts.v(DA, [(1, 32)]), True, True)],
                       [kd, "cst_f"], ["PDcum"])
                    act(dts.v(CUM, [(1, 64)]), PD.v(320, [(1, 64)]), AF.Copy, ["PDcum"], [kd])
                    act(dts.v(E1, [(1, 64)]), dts.v(CUM, [(1, 64)]), AF.Exp, [kd], [kd])
                    tt("dve", dts.v(D2, [(1, 32)]), dts.v(CUM + 32, [(1, 32)]), dts.v(CUM, [(1, 32)]), ALU.subtract,
                       [kd], [kd])
                    act(dts.v(D2, [(1, 32)]), dts.v(D2, [(1, 32)]), AF.Exp, [kd], [kd])
                    tt("dve", dts.v(WW, [(1, 32)]), dts.v(D2, [(1, 32)]), dts.v(DT, [(1, 32)]), ALU.mult, [kd], [kd])
                    yield
                    tr([(PT.v(k * 128, [(1, 128)]), xbc_c.v(k * TL + cl * 128, [(1, 128)]), ident_b) for k in range(8)],
                       [xck, "cst_b"], ["PT"])
                    act(xs_tm.v(0, [(1, 1024)]), PT.v(), AF.Copy, ["PT"], ["xs_tm"])
                    tr([(PT.v(k * 128, [(1, 128)]), xbc_c.v((8 + k) * TL + cl * 128, [(1, 128)]), ident_b)
                        for k in range(2)], [xck, "cst_b"], ["PT"])
                    rec = REC[bi][cl]
                    rk = f"REC{bi}{cl}"
                    act(rec.v(3072, [(1, 256)]), PT.v(0, [(1, 256)]), AF.Copy, ["PT"], [rk + ".B"])
                    tt(PREP_VEC_ENG, H_xw[bi][cl].v(0, [(64, 16), (1, 64)]), xs_tm.v(0, [(64, 16), (1, 64)]),
                       dts.v(WW + 16, [(1, 16), (0, 64)]), ALU.mult, ["xs_tm", kd], [f"Hxw{bi}{cl}"])
                    tt(PREP_VEC_ENG, rec.v(2048, [(64, 16), (1, 64)]), xs_tm.v(0, [(64, 16), (1, 64)]),
                       dts.v(WW, [(1, 16), (0, 64)]), ALU.mult, ["xs_tm", kd], [rk + ".xw"])
                    yield
                    if not pass2:
                        return
                    rhs_v = lambda off, dims: ar1.v(off, dims)
                    lexp_v = lambda off, dims: ar1.v(2048 + off, dims)
                    mm([(PBX.v(hf * 512, [(1, 512)]), hb.v(k * TLH + t0, [(1, 128)]),
                         WZ.v(k * 1056 + hf * 512, [(1, 512)]), k == 0, k == 7)
                        for hf in range(2) for k in range(8)], [hkey, "WZ"], ["PB0", "PB1"])
                    act(lexp_v(0, [(1, 1024)]), PBX.v(), AF.Tanh, ["PB0", "PB1"], ["Lexp"], scale=0.5)
                    act(lexp_v(1024, [(1, 1024)]), PBX.v(), AF.Identity, ["PB0", "PB1"], ["Lexp"], scale=0.5)
                    stt("dve", rec.v(1024, [(1, 1024)]), lexp_v(0, [(1, 1024)]), 1.0, lexp_v(1024, [(1, 1024)]),
                        ALU.add, ALU.mult, ["Lexp"], [rk + ".sz"])
                    yield
                    mm([(PD.v(g * 128, [(1, 128)]), xbc_c.v((8 + g) * TL + cl * 128, [(1, 128)]),
                         xbc_c.v((10 + g) * TL + cl * 128, [(1, 128)]), True, True) for g in range(2)],
                       [xck], ["PDa"])
                    tt("dve", cbm.v(0, [(128, 2), (1, 128)]), PD.v(0, [(128, 2), (1, 128)]),
                       cst_f.v(UINC, [(0, 2), (1, 128)]), ALU.mult, ["PDa", "cst_f"], ["cbm"])
                    tt("dve", cbm.v(256, [(128, 2), (1, 128)]), PD.v(0, [(128, 2), (1, 128)]),
                       cst_f.v(LINC, [(0, 2), (1, 128)]), ALU.mult, ["PDa", "cst_f"], ["cbm"])
                    tt(PREP_VEC_ENG, xsD.v(), xs_tm.v(), Db.v(), ALU.mult, ["xs_tm", "Db"], ["xsD"])
                    mm([(PA.v(hf * 512, [(1, 512)]), ident_b, xsD.v(hf * 512, [(1, 512)]), True, False)
                        for hf in range(2)], ["xsD", "cst_b"], ["PA0", "PA1"])
                    yield
                    for d in range(2):
                        msk = UINC if d == 0 else LINC
                        lmask = LSTR if d == 0 else USTR
                        tt(PREP_VEC_ENG, rhs_v(0, [(128, 16), (1, 128)]), cst_f.v(msk, [(0, 16), (1, 128)]),
                           dts.v(DA + 16 * d, [(1, 16), (0, 128)]), ALU.mult, ["cst_f", kd], ["rhs"])
                        tt(PREP_VEC_ENG, xdt.v(0, [(64, 16), (1, 64)]), xs_tm.v(0, [(64, 16), (1, 64)]),
                           dts.v(DT + 16 * d, [(1, 16), (0, 64)]), ALU.mult, ["xs_tm", kd], ["xdt"])
                        for q in range(4):
                            mm([(PC.v(0, [(1, 512)]), cst_b.v(lmask, [(1, 128)]),
                                 rhs_v(q * 512, [(1, 512)]), True, True)], ["rhs", "cst_b"], ["PC0"])
                            act(lexp_v(q * 512, [(1, 512)]), PC.v(0, [(1, 512)]), AF.Exp, ["PC0"], ["Lexp"])
                            if q % 2 == 1:
                                yield
                        tt("dve", MT.v(0, [(1024, 2), (128, 8), (1, 128)]),
                           lexp_v(0, [(1024, 2), (128, 8), (1, 128)]),
                           cbm.v(d * 256, [(128, 2), (0, 8), (1, 128)]), ALU.mult, ["Lexp", "cbm"], ["MT"])
                        mm([(PA.v(h * 64, [(1, 64)]), MT.v(h * 128, [(1, 128)]), xdt.v(h * 64, [(1, 64)]),
                             False, d == 1) for h in range(NH)], ["MT", "xdt"], ["PA0", "PA1"])
                        yield
                    act(H_yd[bi][cl].v(0, [(1, 1024)]), PA.v(), AF.Copy, ["PA0", "PA1"], [f"Hyd{bi}{cl}"])
                    yield

                def fm_stage(sq, T, n):
                    s = sq["s"]
                    bi = n % 2
                    hb = hin[bi]
                    hkey = f"hin{bi}"
                    xbc_c = xbc_cs[n % 3]
                    xck = f"xbc_c{n % 3}"
                    dma(hb.v(0, [(TLH, 8), (1, TLH)]),
                        sq["hT1"][:, :, T * TL:T * TL + TLH].rearrange("k p t -> p k t"),
                        f"hl{bi}", reads=["hT1_%d_z" % s] + ["hT1_%d_%d" % (s, c_) for c_ in
                                                      range(max(0, 2 * T - 1), min(sq["nch"], 2 * T + 3))],
                        writes=[hkey])
                    for m in range(12):
                        w, wk = wload(m)
                        po = (m % 2) * 512
                        pk = "PB%d" % (m % 2)
                        NX = TL + 2
                        mm([(PBX.v(po, [(1, NX)]), w.v(k * 128, [(1, 128)]), hb.v(k * TLH + HALO - 1, [(1, NX)]),
                             k == 0, k == 7) for k in range(8)], [wk, hkey], [pk])
                        act(xbc_pre_v(m * TLH + HALO - 1, [(1, NX)]), PBX.v(po, [(1, NX)]), AF.Copy, [pk], XBP_K)
                        if m % 2 == 1:
                            yield
                    for m in range(12):
                        po = (m % 2) * 512
                        pk = "PB%d" % (m % 2)
                        if CONV3_ON_DVE:
                            c3 = c3s[m % 2]
                            c3k = f"c3_{m % 2}"
                            for k3 in range(3):
                                rv = xbc_pre_v(m * TLH + HALO - 1 + k3, [(1, TL)])
                                wv = fmv.v(O_WSC + m * 3 + k3, [(1, 1)])
                                if k3 == 0:
                                    ts("dve", c3.v(), rv, wv, None, ALU.mult, None, XBP_K + ["fmv"], [c3k])
                                else:
                                    stt("dve", c3.v(), rv, wv, c3.v(), ALU.mult, ALU.add, XBP_K + ["fmv", c3k], [c3k])
                            silu_psum(xbc_c.v(m * TL, [(1, TL)]), c3.v(), TL, stTs[m % 2], stVs[m % 2],
                                      fmvh.v(O_BSC + m, [(1, 1)]), 0.5, [c3k, "fmvh"], xck,
                                      (f"stT{m % 2}", f"stV{m % 2}"))
                        else:
                            mm([(PBX.v(po, [(1, TL)]), diag3.v((m * 3 + k3) * 128, [(1, 128)]),
                                 xbc_pre_v(m * TLH + HALO - 1 + k3, [(1, TL)]), k3 == 0, k3 == 2) for k3 in range(3)],
                               ["diag3"] + XBP_K, [pk])
                            silu_psum(xbc_c.v(m * TL, [(1, TL)]), PBX.v(po, [(1, TL)]), TL, stTs[m % 2], stVs[m % 2],
                                      fmvh.v(O_BSC + m, [(1, 1)]), 0.5, [pk, "fmvh"], xck,
                                      (f"stT{m % 2}", f"stV{m % 2}"))
                        if m % 2 == 1:
                            yield
                    if pass2:
                        u_c = u_cs[0]
                        uck = "u_c0"
                        for m in range(8):
                            wg, wgk = wload(20 + m)
                            mm([(PBX.v(0, [(1, TLH)]), wg.v(k * 128, [(1, 128)]), hb.v(k * TLH, [(1, TLH)]),
                                 k == 0, k == 7) for k in range(8)], [wgk, hkey], ["PB0"])
                            act(sg.v(), PBX.v(0, [(1, TLH)]), AF.Tanh, ["PB0"], ["sg"], scale=0.5)
                            wa, wak = wload(12 + m)
                            mm([(PBX.v(512, [(1, TLH)]), wa.v(k * 128, [(1, 128)]), hb.v(k * TLH, [(1, TLH)]),
                                 k == 0, k == 7) for k in range(8)], [wak, hkey], ["PB1"])
                            stt("dve", u_pre.v(m * TLH, [(1, TLH)]), sg.v(), 1.0, PBX.v(512, [(1, TLH)]), ALU.add,
                                ALU.mult, ["PB1", "sg"], ["u_pre"])
                            yield
                        for m in range(8):
                            di = m % 2
                            po = (m % 2) * 512
                            pk = "PB%d" % (m % 2)
                            if di == 0 or DIAG31_ALL_DMA:
                                dma(dgr[di].v(), dg31_d[m], f"dgl{di}", reads=["dg31_d"], writes=[f"dgr{di}"])
                            else:
                                tt("dve", dgr[1].v(0, [(128, 31), (1, 128)]), cst_b.v(IDF, [(0, 31), (1, 128)]),
                                   fmv.v(O_WCF + m * 31, [(1, 31), (0, 128)]), ALU.mult, ["cst_b", "fmv"], ["dgr1"])
                            mm([(PBX.v(po, [(1, TL)]), dgr[di].v(k * 128, [(1, 128)]),
                                 u_pre.v(m * TLH + 1 + k, [(1, TL)]), k == 0, k == 30) for k in range(31)],
                               [f"dgr{di}", "u_pre"], [pk])
                            act(u_c.v(m * TL, [(1, TL)]), PBX.v(po, [(1, TL)]), AF.Identity, [pk, "fmv"],
                                [uck], bias=fmv.v(O_BCF + m, [(1, 1)]), scale=0.5)
                            act(usq.v(m * TL, [(1, TL)]), PBX.v(po, [(1, TL)]), AF.Square, [pk, "fmv"], ["usq"],
                                bias=fmv.v(O_BCF + m, [(1, 1)]), scale=0.5)
                            yield
                        mm([(PBX.v(0, [(1, TL)]), cst_b.v(ONES, [(1, 128)]), u_c.v(m * TL, [(1, TL)]), m == 0, m == 7)
                            for m in range(8)], [uck, "cst_b"], ["PB0"])
                        mm([(PBX.v(512, [(1, TL)]), cst_b.v(ONES, [(1, 128)]), usq.v(m * TL, [(1, TL)]), m == 0, m == 7)
                            for m in range(8)], ["usq", "cst_b"], ["PB1"])
                        act(lnA.v(), PBX.v(0, [(1, TL)]), AF.Identity, ["PB0"], ["lnA"], scale=1.0 / 1024.0)
                        act(lnB.v(), PBX.v(512, [(1, TL)]), AF.Identity, ["PB1"], ["lnB"], scale=1.0 / 1024.0)
                        tt("dve", lnT.v(), lnA.v(), lnA.v(), ALU.mult, ["lnA"], ["lnT"])
                        tt("dve", lnB.v(), lnB.v(), lnT.v(), ALU.subtract, ["lnB", "lnT"], ["lnB"])
                        act(lnB.v(), lnB.v(), AF.Ln, ["lnB"], ["lnB"], bias=EPS, scale=1.0)
                        act(lnB.v(), lnB.v(), AF.Exp, ["lnB"], ["lnB"], scale=-0.5)
                        yield
                        for m in range(8):
                            tt("dve", lnT.v(), u_c.v(m * TL, [(1, TL)]), lnA.v(), ALU.subtract,
                               [uck, "lnA"], ["lnT"])
                            tt("dve", lnT.v(), lnT.v(), lnB.v(), ALU.mult, ["lnT", "lnB"], ["lnT"])
                            silu_psum(u_c.v(m * TL, [(1, TL)]), lnT.v(), TL, stTs[m % 2], stVs[m % 2],
                                      fmvh.v(O_LNB + m, [(1, 1)]), fmvh.v(O_LNG + m, [(1, 1)]), ["lnT", "fmvh"], uck,
                                      (f"stT{m % 2}", f"stV{m % 2}"))
                            if m % 2 == 1:
                                yield
                        dma(sq["oin"][8:16, :, T * TL:(T + 1) * TL].rearrange("k p t -> p k t"),
                            u_c.v(0, [(TL, 8), (1, TL)]), f"uost{bi}", reads=[uck], writes=["oin_%d" % s], eng="act")
                        yield

                def prep_stage(sq, T, n):
                    bi = n % 2
                    for cl in range(NCL):
                        yield from prep_chunk(hin[bi], f"hin{bi}", xbc_cs[n % 3], f"xbc_c{n % 3}", bi, cl)

                def state_update(bi, cl, d):
                    dts = H_dts[bi][cl]
                    kd = f"Hdts{bi}{cl}"
                    tt("dve", S_run.v(0, [(64, 16), (1, 64)]), S_run.v(0, [(64, 16), (1, 64)]),
                       dts.v(E1 + 32 + 16 * d, [(1, 16), (0, 64)]), ALU.mult, ["S_run", kd], ["S_run"])
                    for g in range(2):
                        mm([(PC.v(512, [(1, 512)]), REC[bi][cl].v(3072 + g * 128, [(1, 128)]),
                             H_xw[bi][cl].v(g * 512, [(1, 512)]), True, True)],
                           [f"REC{bi}{cl}.B", f"Hxw{bi}{cl}"], ["PC1"])
                        tt("dve", S_run.v(g * 512, [(1, 512)]), S_run.v(g * 512, [(1, 512)]), PC.v(512, [(1, 512)]),
                           ALU.add, ["S_run", "PC1"], ["S_run"])

                def init_state(sq, src):
                    if sq["s"] == 0:
                        dma(F1.v(0, [(128, 8), (1, 128)]), src.rearrange("(k q) n -> q k n", q=128), "ld0",
                            writes=["F1"])
                        for hf in range(2):
                            tr([(PC.v(512 + k * 128, [(1, 128)]), F1.v((hf * 4 + k) * 128, [(1, 128)]), ident_f)
                                for k in range(4)], ["F1", "cst_f"], ["PC1"])
                            act(S_run.v(hf * 512, [(1, 512)]), PC.v(512, [(1, 512)]), AF.Copy, ["PC1"], ["S_run"])
                    else:
                        P.op("dve", lambda e: e.memset(S_run.v(), 0.0), writes=["S_run"])

                def final_state(sq, dst):
                    for hf in range(2):
                        tr([(PC.v(512 + k * 128, [(1, 128)]), S_run.v((hf * 4 + k) * 128, [(1, 128)]), ident_f)
                            for k in range(4)], ["S_run", "cst_f"], ["PC1"])
                        act(F2.v(hf * 512, [(1, 512)]), PC.v(512, [(1, 512)]), AF.Copy, ["PC1"], ["F2"])
                    dma(dst[sq["s"] - 1].rearrange("(k q) n -> q k n", q=128), F2.v(0, [(128, 8), (1, 128)]),
                        "nsst", reads=["F2"], writes=["ns_out"], is_out=True, eng="act")

                def st_stage(sq, T, n, first, last):
                    s = sq["s"]
                    bi = n % 2
                    xbc_c = xbc_cs[n % 3]
                    xck = f"xbc_c{n % 3}"
                    if first:
                        init_state(sq, st_b)
                        yield
                    for cl in range(NCL - 1, -1, -1):
                        cg = T * NCL + cl
                        dts = H_dts[bi][cl]
                        kd = f"Hdts{bi}{cl}"
                        rec = REC[bi][cl]
                        rk = f"REC{bi}{cl}"
                        act(Sbf[0].v(), S_run.v(), AF.Copy, ["S_run"], ["Sbf0"])
                        for g in range(2):
                            mm([(PC.v(512, [(1, 512)]), xbc_c.v((10 + g) * TL + cl * 128, [(1, 128)]),
                                 Sbf[0].v(g * 512, [(1, 512)]), True, True)], [xck, "Sbf0"], ["PC1"])
                            tt("dve", F2.v(g * 512, [(64, 8), (1, 64)]), PC.v(512, [(64, 8), (1, 64)]),
                               dts.v(E1 + 16 + 8 * g, [(1, 8), (0, 64)]), ALU.mult, ["PC1", kd], ["F2"])
                        tt("dve", rec.v(0, [(1, 1024)]), F2.v(), H_yd[bi][cl].v(), ALU.add, ["F2", f"Hyd{bi}{cl}"],
                           [rk + ".y"])
                        yield
                        dma(sq["rec"][cg], rec.v(), f"recst{bi}{cl}",
                            reads=[rk + ".y", rk + ".sz", rk + ".xw", rk + ".B"], writes=["rec_%d" % s], eng="act")
                        dma(sq["e1"][cg], dts.v(E1, [(1, 64)]), f"e1st{bi}{cl}", reads=[kd], writes=["e1_%d" % s],
                            eng="act")
                        dma(sq["cd"][cg].rearrange("p (g t) -> p g t", g=2),
                            xbc_c.v(10 * TL + cl * 128, [(TL, 2), (1, 128)]), f"cdst{bi}{cl}", reads=[xck],
                            writes=["cd_%d" % s], eng="act")
                        state_update(bi, cl, 1)
                        yield
                    if last and s > 0:
                        final_state(sq, ns_b)
                        yield

                work = []
                for sq in seqs:
                    nt = sq["nt"]
                    tiles = list(range(nt - 1, -1, -1))
                    for i, T in enumerate(tiles):
                        work.append((sq, T, i == 0, i == nt - 1))
                for n in range(len(work) + 2):
                    gens = []
                    if n < len(work):
                        gens.append(fm_stage(work[n][0], work[n][1], n))
                    if 0 <= n - 1 < len(work):
                        w1 = work[n - 1]
                        gens.append(prep_stage(w1[0], w1[1], n - 1))
                    if 0 <= n - 2 < len(work):
                        w2 = work[n - 2]
                        gens.append(st_stage(w2[0], w2[1], n - 2, w2[2], w2[3]))
                    interleave(gens)
                P.flush(win=0.6)

        mixer_phase(True)

        s3 = ExitStack()
        RSTD_ON_ACT[0] = True
        if True:
            PY = psum_set(s3, [([128, 512], F32)] * 2)
            PS = PY
            fPTs = psum_set(s3, [([128, 1024], BF16)] * 1)
            NR = 3
            RECs = [sbt(s3, f"fREC{i}", [128, 3328], BF16) for i in range(NR)]
            Cbs = [sbt(s3, f"fC{i}", [128, 256], BF16) for i in range(NR)]
            E1s = [sbt(s3, f"fE{i}", [128, 64], F32) for i in range(NR)]
            S_run = sbt(s3, "fS", [128, D], F32)
            Sbfs = [sbt(s3, f"fSbf{i}", [128, D], BF16) for i in range(2)]
            Fs = [sbt(s3, f"fF{i}", [128, D], F32) for i in range(2)]
            jk = sbt(s3, "fjk", [128, D], BF16)
            yns = [sbt(s3, f"fyn{i}", [128, D], BF16) for i in range(2)]
            ysTs = [sbt(s3, f"fysT{i}", [128, D], BF16) for i in range(2)]
            ssdw = sbt(s3, "fssdw", [128, D], F32)
            load_row(ssdw, R_SSD, D, "ld0", "fssdw")
            it = 0
            for sq in seqs:
                s = sq["s"]
                if s == 0:
                    dma(Fs[0].v(0, [(128, 8), (1, 128)]), st_f.rearrange("(k q) n -> q k n", q=128), "ld0",
                        writes=["fF0"])
                    for hf in range(2):
                        tr([(PY[hf].v(k * 128, [(1, 128)]), Fs[0].v((hf * 4 + k) * 128, [(1, 128)]), ident_f)
                            for k in range(4)], ["fF0", "cst_f"], [f"PY{hf}"])
                        act(S_run.v(hf * 512, [(1, 512)]), PY[hf].v(), AF.Copy, [f"PY{hf}"], ["fS"])
                else:
                    P.op("dve", lambda e: e.memset(S_run.v(), 0.0), writes=["fS"])
                for cg in range(sq["nch"]):
                    i3 = it % NR
                    i2 = it % 2
                    it += 1
                    rec, rk = RECs[i3], f"fREC{i3}"
                    dma(rec.v(), sq["rec"][cg], f"frl{i3}", reads=["rec_%d" % s], writes=[rk])
                    dma(E1s[i3].v(), sq["e1"][cg], f"fel{i3}", reads=["e1_%d" % s], writes=[f"fE{i3}"])
                    dma(Cbs[i3].v(), sq["cd"][cg], f"fcl{i3}", reads=["cd_%d" % s], writes=[f"fC{i3}"])
                    act(Sbfs[i2].v(), S_run.v(), AF.Copy, ["fS"], [f"fSbf{i2}"])
                    F = Fs[i2]
                    fk = f"fF{i2}"
                    for g in range(2):
                        mm([(PY[g].v(), Cbs[i3].v(g * 128, [(1, 128)]), Sbfs[i2].v(g * 512, [(1, 512)]), True, True)],
                           [f"fC{i3}", f"fSbf{i2}"], [f"PY{g}"])
                        tt("dve", F.v(g * 512, [(64, 8), (1, 64)]), PY[g].v(0, [(64, 8), (1, 64)]),
                           E1s[i3].v(8 * g, [(1, 8), (0, 64)]), ALU.mult, [f"PY{g}", f"fE{i3}"], [fk])
                    tt("dve", F.v(), F.v(), rec.v(0, [(1, 1024)]), ALU.add, [fk, rk], [fk])
                    tt("dve", F.v(), F.v(), rec.v(1024, [(1, 1024)]), ALU.mult, [fk, rk], [fk])
                    act(jk.v(0, [(1, 1024)]), F.v(), AF.Square, [fk], ["fjk", "sm%d" % (32 + 4 * i2)],
                        scale=1.0 / 32.0, accum=sm.v(32 + 4 * i2, [(1, 1)]))
                    rstd_from_ms(32 + 4 * i2, "sm%d" % (32 + 4 * i2))
                    stt("dve", yns[i2].v(0, [(1, 1024)]), F.v(), sm.v(33 + 4 * i2, [(1, 1)]), ssdw.v(), ALU.mult,
                        ALU.mult, [fk, "sm%d" % (33 + 4 * i2), "fssdw"], [f"fyn{i2}"])
                    tr([(fPTs[0].v(k * 128, [(1, 128)]), yns[i2].v(k * 128, [(1, 128)]), ident_b) for k in range(8)],
                       [f"fyn{i2}", "cst_b"], ["fPT0"])
                    act(ysTs[i2].v(0, [(1, 1024)]), fPTs[0].v(), AF.Copy, ["fPT0"], [f"fysT{i2}"])
                    dma(sq["oin"][0:8, :, cg * 128:(cg + 1) * 128].rearrange("k p t -> p k t"),
                        ysTs[i2].v(0, [(128, 8), (1, 128)]), f"yst{i2}", reads=[f"fysT{i2}"],
                        writes=["oinY_%d_%d" % (s, cg)], eng="act")
                    tt("dve", S_run.v(0, [(64, 16), (1, 64)]), S_run.v(0, [(64, 16), (1, 64)]),
                       E1s[i3].v(32, [(1, 16), (0, 64)]), ALU.mult, ["fS", f"fE{i3}"], ["fS"])
                    for g in range(2):
                        mm([(PS[g].v(), rec.v(3072 + g * 128, [(1, 128)]), rec.v(2048 + g * 512, [(1, 512)]),
                             True, True)], [rk], [f"PY{g}"])
                        tt("dve", S_run.v(g * 512, [(1, 512)]), S_run.v(g * 512, [(1, 512)]), PS[g].v(), ALU.add,
                           ["fS", f"PY{g}"], ["fS"])
                if s > 0:
                    for hf in range(2):
                        tr([(PY[hf].v(k * 128, [(1, 128)]), S_run.v((hf * 4 + k) * 128, [(1, 128)]), ident_f)
                            for k in range(4)], ["fS", "cst_f"], [f"PY{hf}"])
                        act(Fs[0].v(hf * 512, [(1, 512)]), PY[hf].v(), AF.Copy, [f"PY{hf}"], ["fF0"])
                    dma(ns_f[s - 1].rearrange("(k q) n -> q k n", q=128), Fs[0].v(0, [(128, 8), (1, 128)]),
                        "nsst", reads=["fF0"], writes=["ns_out"], is_out=True, eng="act")

        with ExitStack() as s4:
            PMs = psum_set(s4, [([128, 1024], F32)] * 2)
            PTs = psum_set(s4, [([128, 1024], BF16)] * 1)
            Wo = sbt(s4, "Wo", [128, 16 * D], BF16)
            G1m = [sbt(s4, f"G1_{i}", [128, D], F32) for i in range(2)]
            G2m = [sbt(s4, f"G2_{i}", [128, D], F32) for i in range(2)]
            SH2m = [sbt(s4, f"SH2_{i}", [128, D], F32) for i in range(2)]
            m4work = []
            wr1 = sbt(s4, "wr1", [128, D], F32)
            wr2 = sbt(s4, "wr2", [128, D], F32)
            NB = 2
            oc = [sbt(s4, f"oc{i}", [128, 16 * 128], BF16) for i in range(NB)]
            xs_ = [sbt(s4, f"x4_{i}", [128, D], F32) for i in range(NB)]
            X1 = [sbt(s4, f"X1_{i}", [128, D], F32) for i in range(NB)]
            tFs = [sbt(s4, f"tF4_{i}", [128, D], F32) for i in range(2)]
            hns = [sbt(s4, f"hn4_{i}", [128, D], BF16) for i in range(2)]
            jks = [sbt(s4, f"jk4_{i}", [128, D], BF16) for i in range(2)]
            hTs = [sbt(s4, f"hT4_{i}", [128, D], BF16) for i in range(NB)]
            for k in range(16):
                dma(Wo.v(k * D, [(1, D)]), w_out[k * 128:(k + 1) * 128, :], f"wcast{k % 2}", writes=["Wo"], eng="pool")
            load_row(wr1, R_POST, D, "ld0", "wr1")
            load_row(wr2, R_FPRE, D, "ld1", "wr2")
            for cnd in (1, 0):
                load_mod(G1m[cnd], cnd, 2, "ld0", "G1_%d" % cnd)
                tt("dve", G1m[cnd].v(), G1m[cnd].v(), wr1.v(), ALU.mult, ["G1_%d" % cnd, "wr1"], ["G1_%d" % cnd])
                load_mod(G2m[cnd], cnd, 4, "ld1", "G2_%d" % cnd)
                stt("dve", G2m[cnd].v(), G2m[cnd].v(), 1.0, wr2.v(), ALU.add, ALU.mult, ["G2_%d" % cnd, "wr2"],
                    ["G2_%d" % cnd])
                load_mod(SH2m[cnd], cnd, 3, "ld0", "SH2_%d" % cnd)
            for sq in seqs:
                s = sq["s"]
                if s == 0:
                    for a in (0, 65 * 64):
                        dma(sq["hT2"][:, :, a:a + 64].rearrange("k p t -> p k t"), zero_b.v(0, [(64, 8), (1, 64)]),
                            "zst", reads=["zero_b"], writes=["hT2_0"])
                for c in range(sq["nch"]):
                    m4work.append((sq, c))

            def m4_A(sq, c, it):
                s = sq["s"]
                i3 = it % NB
                i2 = it % 2
                PM = PMs[i2]
                pmk = [f"PM{i2}a", f"PM{i2}b"]
                r0 = sq["base"] + c * 128
                dma(oc[i3].v(0, [(128, 16), (1, 128)]),
                    sq["oin"][:, :, c * 128:(c + 1) * 128].rearrange("k p t -> p k t"), f"ol{i3}",
                    reads=["oin_%d" % s, "oinY_%d_%d" % (s, c)], writes=[f"oc{i3}"])
                dma(xs_[i3].v(), x_all[r0:r0 + 128, :], f"xl{i3}", writes=[f"x4_{i3}"])
                mm([(PM.v(hf * 512, [(1, 512)]), oc[i3].v(k * 128, [(1, 128)]), Wo.v(k * D + hf * 512, [(1, 512)]),
                     k == 0, k == 15) for hf in range(2) for k in range(16)], [f"oc{i3}", "Wo"], pmk)

            def m4_B(sq, c, it):
                s = sq["s"]
                cnd = sq["cond"]
                i3 = it % NB
                i2 = it % 2
                PM = PMs[i2]
                pmk = [f"PM{i2}a", f"PM{i2}b"]
                r0 = sq["base"] + c * 128
                sc = 16 + 8 * i2
                act(jks[i2].v(0, [(1, 1024)]), PM.v(), AF.Square, pmk, [f"jk4_{i2}", "sm%d" % sc], scale=1.0 / 32.0,
                    accum=sm.v(sc, [(1, 1)]))
                rstd_from_ms(sc, "sm%d" % sc)
                x1 = X1[i3]
                stt("dve", x1.v(), PM.v(), sm.v(sc + 1, [(1, 1)]), G1m[cnd].v(), ALU.mult, ALU.mult,
                    pmk + ["sm%d" % (sc + 1), "G1_%d" % cnd], [f"X1_{i3}"])
                tt("dve", x1.v(), x1.v(), xs_[i3].v(), ALU.add, [f"X1_{i3}", f"x4_{i3}"], [f"X1_{i3}"])
                dma(x1_d[r0:r0 + 128, :], x1.v(), f"x1st{i3}", reads=[f"X1_{i3}"], writes=["x1_d"], eng="act")
                norm_mod_transpose(x1, f"X1_{i3}", G2m[cnd], "G2_%d" % cnd, SH2m[cnd], "SH2_%d" % cnd, tFs[i2],
                                   f"tF4_{i2}", hns[i2], f"hn4_{i2}", jks[i2], f"jk4_{i2}", hTs[i3], f"hT4_{i3}",
                                   sc + 4, PTs[0], "PT0")
                o2 = sq["h2off"] + c * 128
                dma(sq["hT2"][:, :, o2:o2 + 128].rearrange("k p t -> p k t"), hTs[i3].v(0, [(128, 8), (1, 128)]),
                    f"hst{i3}", reads=[f"hT4_{i3}"], writes=["hT2_%d" % s], eng="act")

            m4_A(m4work[0][0], m4work[0][1], 0)
            for i, (sq, c) in enumerate(m4work):
                if i + 1 < len(m4work):
                    m4_A(m4work[i + 1][0], m4work[i + 1][1], i + 1)
                m4_B(sq, c, i)
            P.flush(win=0.6)
        s3.close()
        RSTD_ON_ACT[0] = False

        with ExitStack() as s5:
            PA, PB, PC, PF = psum_set(s5, [([128, 1024], F32)] * 4)
            Wd = sbt(s5, "Wd", [128, 22 * D], BF16)
            G3 = sbt(s5, "G3", [128, D], F32)
            wr3 = sbt(s5, "wr3", [128, D], F32)
            hin2 = [sbt(s5, f"h2_{i}", [128, 8 * 576], BF16) for i in range(2)]
            NWU = 4
            wur = [sbt(s5, f"wur{i}", [128, 1024], BF16) for i in range(NWU)]
            dgf = [sbt(s5, f"dgf{i}", [128, 18 * 128], BF16) for i in range(2)]
            PgL = sbt(s5, "PgL", [128, 10 * 66], BF16)
            PvL = sbt(s5, "PvL", [128, 10 * 66], BF16)
            PgC = sbt(s5, "PgC", [128, 258], BF16)
            PvC = sbt(s5, "PvC", [128, 258], BF16)
            SV = sbt(s5, "SV", [128, 44 * 132], BF16)
            P.op("dve", lambda e: e.memset(SV.v(), 0.0), writes=["SV%d_%d" % (g_, j_) for g_ in range(2) for j_ in range(22)])
            sgl = sbt(s5, "sgl", [128, 512], F32)
            sgT = sbt(s5, "sgT", [128, 512], F32)
            sgV = sbt(s5, "sgV", [128, 512], F32)
            actTs = [sbt(s5, f"actT{i}", [128, 22 * 512], BF16) for i in range(2)]
            x1t = [sbt(s5, f"x1t{i}", [128, D], F32) for i in range(2)]
            Y = [sbt(s5, f"Y{i}", [128, D], F32) for i in range(1)]
            parts = [sbt(s5, f"part{i}", [128, 512], F32) for i in range(2)]
            for k in range(22):
                dma(Wd.v(k * D, [(1, D)]), w_down[k * 128:(k + 1) * 128, :], f"wcast{k % 2}", writes=["Wd"], eng="pool")
            for (b_, kk) in ((PgL, "Pg"), (PvL, "Pv"), (PgC, "Pg"), (PvC, "Pv")):
                P.op("dve", (lambda bb: (lambda e: e.memset(bb.v(), 0.0)))(b_), writes=[kk])
            load_row(wr3, R_FPOST, D, "ld0", "wr3")
            st8 = dict(cw=0, cdg=0, ih=0, iy=0, ia=0)

            def ffn_tile(sq, T):
                s = sq["s"]
                lat = (s == 0)
                NP = 576 if lat else 256
                TF = 512 if lat else 256
                taps = [(dr, dc) for dr in range(3) for dc in range(3)] if lat else [(1, dc) for dc in range(3)]
                nt_ = len(taps)
                Pbufs = (PgL, PvL) if lat else (PgC, PvC)
                hb = hin2[st8["ih"] % 2]
                hk = f"h2_{st8['ih'] % 2}"
                if lat:
                    NW = 576 if T == 0 else 512
                    t0_ = 64 if T == 0 else T * 512 + 128
                    dma(hb.v(0, [(NP, 8), (1, NW)]),
                        sq["hT2"][:, :, t0_:t0_ + NW].rearrange("k p t -> p k t"), f"hl{st8['ih'] % 2}",
                        reads=["hT2_%d" % s], writes=[hk])
                else:
                    dma(hb.v(0, [(NP, 8), (1, NP)]),
                        sq["hT2"][:, :, T * 512:T * 512 + NP].rearrange("k p t -> p k t"), f"hl{st8['ih'] % 2}",
                        reads=["hT2_%d" % s], writes=[hk])
                st8["ih"] += 1
                actT = actTs[st8["ia"] % 2]
                ak = f"actT{st8['ia'] % 2}"
                st8["ia"] += 1
                dgl = {}

                def U(j, gv):
                    if gv == 0:
                        dg = dgf[st8["cdg"] % 2]
                        dgk = f"dgf{st8['cdg'] % 2}"
                        st8["cdg"] += 1
                        tap0 = taps[0][0] * 3 + taps[0][1]
                        tt("dve", dg.v(0, [(nt_ * 128, 2), (128, nt_), (1, 128)]),
                           cst_b.v(IDF, [(0, 2), (0, nt_), (1, 128)]),
                           fmv.v(O_WFC + j * 9 + tap0, [(22 * 9, 2), (1, nt_), (0, 128)]), ALU.mult,
                           ["cst_b", "fmv"], [dgk])
                        dgl[j] = (dg, dgk)
                    Pbuf = Pbufs[gv]
                    pk = "Pg" if gv == 0 else "Pv"
                    PS, psk = (PA, "PA") if gv == 0 else (PB, "PB")
                    w = wur[st8["cw"] % NWU]
                    wk = f"wur{st8['cw'] % NWU}"
                    dma(w.v(), w_up_t[j + 22 * gv].rearrange("p k c -> p (k c)"), f"wr{st8['cw'] % NWU}",
                        reads=["w_up_t_%d" % k_ for k_ in range(8)], writes=[wk])
                    st8["cw"] += 1
                    if lat:
                        svk = "SV%d_%d" % (gv, j)
                        svo = (gv * 22 + j) * 132
                        act(Pbuf.v(0, [(1, 132)]), SV.v(svo, [(1, 132)]), AF.Copy, [svk], [pk])
                        if T == 0:
                            mm([(PS.v(0, [(1, 512)]), w.v(k * 128, [(1, 128)]), hb.v(k * NP, [(1, 512)]),
                                 k == 0, k == 7) for k in range(8)] +
                               [(PS.v(512, [(1, 64)]), w.v(k * 128, [(1, 128)]), hb.v(k * NP + 512, [(1, 64)]),
                                 k == 0, k == 7) for k in range(8)], [wk, hk], [psk + "0", psk + "1"])
                            act(Pbuf.v(1 * 66 + 1, [(66, 8), (1, 64)]), PS.v(0, [(64, 8), (1, 64)]),
                                AF.Copy, [psk + "0"], [pk])
                            act(Pbuf.v(9 * 66 + 1, [(1, 64)]), PS.v(512, [(1, 64)]), AF.Copy, [psk + "1"], [pk])
                        else:
                            mm([(PS.v(0, [(1, 512)]), w.v(k * 128, [(1, 128)]), hb.v(k * NP, [(1, 512)]),
                                 k == 0, k == 7) for k in range(8)], [wk, hk], [psk + "0"])
                            act(Pbuf.v(2 * 66 + 1, [(66, 8), (1, 64)]), PS.v(0, [(64, 8), (1, 64)]),
                                AF.Copy, [psk + "0"], [pk])
                        if T < 7:
                            act(SV.v(svo, [(1, 132)]), Pbuf.v(8 * 66, [(1, 132)]), AF.Copy, [pk], [svk])
                    else:
                        mm([(PS.v(0, [(1, 256)]), w.v(k * 128, [(1, 128)]), hb.v(k * NP, [(1, 256)]),
                             k == 0, k == 7) for k in range(8)], [wk, hk], [psk + "0"])
                        act(Pbuf.v(1, [(1, 256)]), PS.v(0, [(1, 256)]), AF.Copy, [psk + "0"], [pk])

                def C(j, gv):
                    dg, dgk = dgl[j]
                    Pbuf = Pbufs[gv]
                    pk = "Pg" if gv == 0 else "Pv"
                    specs = []
                    for ti, (dr, dc) in enumerate(taps):
                        if lat:
                            rv = Pbuf.v(dr * 66 + dc, [(66, 8), (1, 64)])
                            ov = PC.v(gv * 512, [(64, 8), (1, 64)])
                        else:
                            rv = Pbuf.v(dc, [(1, 256)])
                            ov = PC.v(gv * 512, [(1, 256)])
                        specs.append((ov, dg.v((gv * nt_ + ti) * 128, [(1, 128)]), rv, ti == 0, ti == nt_ - 1))
                    npe = nt_ - N_DVE_TAPS if lat else nt_
                    specs = [(o_, l_, r_, ti == 0, ti == npe - 1) for ti, (o_, l_, r_, _a, _b) in enumerate(specs[:npe])]
                    mm(specs, [dgk, pk], ["PC%d" % gv])
                    if lat and N_DVE_TAPS > 0:
                        prt = parts[gv]
                        prk = "part%d" % gv
                        wof = O_WFC + (j + 22 * gv) * 9
                        for q, ti in enumerate(range(npe, nt_)):
                            dr, dc = taps[ti]
                            rv = Pbuf.v(dr * 66 + dc, [(66, 8), (1, 64)])
                            if q == 0:
                                ts("dve", prt.v(0, [(64, 8), (1, 64)]), rv, fmv.v(wof + ti, [(1, 1)]), None,
                                   ALU.mult, None, [pk, "fmv"], [prk])
                            else:
                                stt("dve", prt.v(0, [(64, 8), (1, 64)]), rv, fmv.v(wof + ti, [(1, 1)]),
                                    prt.v(0, [(64, 8), (1, 64)]), ALU.mult, ALU.add, [pk, "fmv", prk], [prk])
                        if gv == 0:
                            tt("dve", prt.v(), PC.v(0, [(1, TF)]), prt.v(), ALU.add, ["PC0", prk], [prk])
                            silu_psum(sgl.v(0, [(1, TF)]), prt.v(), TF, sgT, sgV,
                                      fmvh.v(O_BFC + j, [(1, 1)]), 0.5, [prk, "fmvh"], "sgl", ("sgT", "sgV"))
                        else:
                            stt("dve", prt.v(), PC.v(512, [(1, TF)]), fmv.v(O_BFC + 22 + j, [(1, 1)]),
                                prt.v(), ALU.add, ALU.add, ["PC1", "fmv", prk], [prk])
                            tt("dve", actT.v(j * 512, [(1, TF)]), prt.v(), sgl.v(0, [(1, TF)]), ALU.mult,
                               [prk, "sgl"], [ak])
                    elif gv == 0:
                        silu_psum(sgl.v(0, [(1, TF)]), PC.v(0, [(1, TF)]), TF, sgT, sgV,
                                  fmvh.v(O_BFC + j, [(1, 1)]), 0.5, ["PC0", "fmvh"], "sgl", ("sgT", "sgV"))
                    else:
                        stt("dve", actT.v(j * 512, [(1, TF)]), PC.v(512, [(1, TF)]), fmv.v(O_BFC + 22 + j, [(1, 1)]),
                            sgl.v(0, [(1, TF)]), ALU.add, ALU.mult, ["PC1", "fmv", "sgl"], [ak])

                U(0, 0)
                U(0, 1)
                for j in range(22):
                    C(j, 0)
                    if j + 1 < 22:
                        U(j + 1, 0)
                    C(j, 1)
                    if j + 1 < 22:
                        U(j + 1, 1)
                for c in range(TF // 128):
                    i2 = st8["iy"] % 2
                    st8["iy"] += 1
                    r0 = sq["base"] + T * 512 + c * 128
                    dma(x1t[i2].v(), x1_d[r0:r0 + 128, :], f"xl{i2}", reads=["x1_d"], writes=[f"x1t{i2}"])
                    mm([(PF.v(hf * 512, [(1, 512)]), actT.v(j * 512 + c * 128, [(1, 128)]),
                         Wd.v(j * D + hf * 512, [(1, 512)]), j == 0, j == 21)
                        for hf in range(2) for j in range(22)], [ak, "Wd"], ["PF0", "PF1"])
                    y = Y[0]
                    act(y.v(), PF.v(), AF.Square, ["PF0", "PF1"], ["Y0", "sm12"], scale=1.0 / 32.0,
                        accum=sm.v(12, [(1, 1)]))
                    rstd_from_ms(12, "sm12")
                    stt("dve", y.v(), PF.v(), sm.v(13, [(1, 1)]), G3.v(), ALU.mult, ALU.mult,
                        ["PF0", "PF1", "sm13", "G3"], ["Y0"])
                    tt("dve", y.v(), y.v(), x1t[i2].v(), ALU.add, ["Y0", f"x1t{i2}"], ["Y0"])
                    dma(y_all[r0:r0 + 128, :], y.v(), f"yst{i2}", reads=["Y0"], writes=["y_all"], is_out=True, eng="act")

            last_cond = None
            for sq in seqs:
                if sq["cond"] != last_cond:
                    last_cond = sq["cond"]
                    load_mod(G3, sq["cond"], 5, "ld1", "G3")
                    tt("dve", G3.v(), G3.v(), wr3.v(), ALU.mult, ["G3", "wr3"], ["G3"])
                for T in range(8 if sq["s"] == 0 else 1):
                    ffn_tile(sq, T)
            P.flush(final=True, win=0.0)
    return nc


_CACHE = {}


def _consts():
    t = np.arange(128)
    ident = np.eye(128, dtype=np.float32)
    uinc = (t[:, None] <= t[None, :]).astype(np.float32)
    linc = (t[:, None] >= t[None, :]).astype(np.float32)
    lstr = (t[:, None] > t[None, :]).astype(np.float32)
    ustr = (t[:, None] < t[None, :]).astype(np.float32)
    ones = np.ones((128, 128), np.float32)
    return np.ascontiguousarray(np.concatenate([ident, uinc, linc, lstr, ustr, ones], axis=1))


def kernel(x_prompt, x_sample, state_ssd_fwd, state_ssd_bwd, c, c_ctx,
           w_ada, b_ada, norm_mix_pre, norm_mix_post, w_in, w_ssd_conv, b_ssd_conv,
           a_log_fwd, a_log_bwd, dt_bias_fwd, dt_bias_bwd, d_skip, ssd_norm,
           w_cf_conv, b_cf_conv, cf_ln_g, cf_ln_b, w_out, norm_ffn_pre, norm_ffn_post,
           w_ffn_up, w_ffn_conv, b_ffn_conv, w_ffn_down):
    f = lambda a: np.ascontiguousarray(np.asarray(a, dtype=np.float32))
    x_prompt, x_sample = f(x_prompt), f(x_sample)
    def fm(v, n):
        return f(v).reshape(n, 128).T
    wsc = f(w_ssd_conv)[0].reshape(3, 12, 128).transpose(2, 1, 0).reshape(128, 36)
    wcf = f(w_cf_conv)[0].reshape(31, 8, 128).transpose(2, 1, 0).reshape(128, 248)
    wfc = f(w_ffn_conv)[0].reshape(9, 44, 128).transpose(2, 1, 0).reshape(128, 396)
    fmv = np.ascontiguousarray(np.concatenate([
        wsc, fm(b_ssd_conv[0], 12), wcf, fm(b_cf_conv[0], 8), fm(cf_ln_g[0], 8), fm(cf_ln_b[0], 8),
        wfc, fm(b_ffn_conv[0], 44)], axis=1).astype(np.float32))
    rowv = np.ascontiguousarray(np.concatenate([
        f(norm_mix_pre)[0], f(norm_mix_post)[0], f(ssd_norm)[0], f(norm_ffn_pre)[0], f(norm_ffn_post)[0],
        f(d_skip)[0], f(dt_bias_fwd)[0], f(dt_bias_bwd)[0], f(a_log_fwd)[0], f(a_log_bwd)[0], f(b_ada)[0]])[None, :])
    cst = _consts()
    if "nc" not in _CACHE:
        _CACHE["nc"] = build_program()
    nc = _CACHE["nc"]
    wa, wi, wo, wu, wd = f(w_ada)[0], f(w_in)[0], f(w_out)[0], f(w_ffn_up)[0], f(w_ffn_down)[0]
    in_maps = []
    for i in range(8):
        xa = np.ascontiguousarray(np.concatenate([x_sample[i], x_prompt[4 * i:4 * i + 4].reshape(NCTX * CTX, D)], axis=0))
        cc = np.stack([f(c_ctx), f(c)[i]], axis=1)
        cin = np.ascontiguousarray(cc.reshape(8, 128, 2).transpose(1, 0, 2).reshape(128, 16))
        in_maps.append(dict(
            x_all=xa, st_f=f(state_ssd_fwd)[i, 0].reshape(1024, 128), st_b=f(state_ssd_bwd)[i, 0].reshape(1024, 128),
            c_in=cin, cst=cst, fmv=fmv, rowv=rowv, w_ada=wa, w_in=wi, w_out=wo, w_up=wu, w_down=wd))
    res = run_bass_kernel_spmd(nc, in_maps, core_ids=list(range(8)))
    y_prompt = np.zeros((32, CTX, D), np.float32)
    y_sample = np.zeros((8, LAT, D), np.float32)
    nsf = np.zeros((32, 1, 16, 64, 128), np.float32)
    nsb = np.zeros((32, 1, 16, 64, 128), np.float32)
    for i in range(8):
        r = res.results[i]
        ya = np.asarray(r["y_all"])
        y_sample[i] = ya[:LAT]
        y_prompt[4 * i:4 * i + 4] = ya[LAT:].reshape(NCTX, CTX, D)
        nsf[4 * i:4 * i + 4, 0] = np.asarray(r["ns_f"]).reshape(NCTX, 16, 64, 128)
        nsb[4 * i:4 * i + 4, 0] = np.asarray(r["ns_b"]).reshape(NCTX, 16, 64, 128)
    return (y_prompt, y_sample, nsf, nsb)
```
